# Optimizing a Trainium2 kernel written in Bass

```python
import numpy as np
import jax, jax.numpy as jnp
from jax import lax

D_MODEL = 2048
BATCH = 1
SEQ = 8192
DEPTH = 1

HG_HEADS = 8
HG_DK = 128
HG_DV = 128
HG_FDIM = HG_HEADS * HG_DK
HG_WIDTH = HG_HEADS * HG_DV
HG_CHUNK = 64
NSA_HEADS = 16
NSA_KV = 4
NSA_HG = NSA_HEADS // NSA_KV
HEAD_DIM = 128
NSA_WIDTH = NSA_HEADS * HEAD_DIM
NSA_KVW = NSA_KV * HEAD_DIM
CMP_LEN = 32
CMP_STRIDE = 16
CMP_HIDDEN = 512
SEL_LEN = 64
SEL_TOPK = 16
WINDOW = 512
Q_BLOCK = 128
ROPE_THETA = 500000.0
ROT_DIM = HEAD_DIM // 4
EPS = 1e-6
IN_SPLITS = (HG_FDIM, HG_FDIM, HG_WIDTH, HG_WIDTH,
             NSA_WIDTH,
             NSA_KVW, NSA_KVW, NSA_KVW, NSA_KVW, NSA_KVW, NSA_KVW,
             3 * NSA_HEADS, NSA_WIDTH,
             2 * D_MODEL)
IN_COLS = sum(IN_SPLITS)

kernel_name = "hgrn2_nsa_gated_hybrid"


def rmsnorm(x, w):
    xf = x.astype(jnp.float32)
    return xf * lax.rsqrt(jnp.mean(xf * xf, axis=-1, keepdims=True) + EPS) * w.astype(jnp.float32)


def partial_rope(x, pos):
    half = ROT_DIM // 2
    inv = ROPE_THETA ** (-jnp.arange(0, ROT_DIM, 2, dtype=jnp.float32) / ROT_DIM)
    ang = pos[:, None] * inv[None, :]
    cos, sin = jnp.cos(ang)[:, None, :], jnp.sin(ang)[:, None, :]
    xf = x.astype(jnp.float32)
    x1, x2, rest = xf[..., :half], xf[..., half:ROT_DIM], xf[..., ROT_DIM:]
    return jnp.concatenate([x1 * cos - x2 * sin, x2 * cos + x1 * sin, rest], axis=-1)


def masked_softmax(s, m):
    s = jnp.where(m, s.astype(jnp.float32), -1e30)
    return jnp.where(m, jax.nn.softmax(s, axis=-1), 0.0)


def hgrn2_chunkwise(q, f_pre, i, lb):
    B, S, H, dk = q.shape
    dv = i.shape[-1]
    C = HG_CHUNK
    N = S // C
    q = q.astype(jnp.float32)
    f = lb + (1.0 - lb) * jax.nn.sigmoid(f_pre.astype(jnp.float32))
    log_f = jnp.log(f)
    k = 1.0 - f

    def to_chunks(a):
        return a.reshape(B, N, C, H, a.shape[-1]).transpose(1, 0, 3, 2, 4)

    causal = jnp.tril(jnp.ones((C, C), dtype=bool))[:, :, None]

    def step(state, xs):
        qc, kc, vc, gc = xs
        b = jnp.cumsum(gc, axis=2)
        inter = jnp.einsum('bhtd,bhdv->bhtv', qc * jnp.exp(b), state)
        diff = b[:, :, :, None, :] - b[:, :, None, :, :]
        decay = jnp.exp(jnp.where(causal, diff, -jnp.inf))
        scores = jnp.einsum('bhtd,bhsd,bhtsd->bhts', qc, kc, decay)
        intra = jnp.einsum('bhts,bhsv->bhtv', scores, vc)
        b_last = b[:, :, -1:, :]
        state = jnp.exp(b_last[:, :, 0, :])[..., None] * state + jnp.einsum(
            'bhsd,bhsv->bhdv', kc * jnp.exp(b_last - b), vc)
        return state, inter + intra

    state0 = jnp.zeros((B, H, dk, dv), jnp.float32)
    _, o = lax.scan(step, state0, (to_chunks(q), to_chunks(k),
                                   to_chunks(i.astype(jnp.float32)), to_chunks(log_f)))
    return o.transpose(1, 0, 3, 2, 4).reshape(B, S, H, dv)


def compress_blocks(k_raw, pos_emb, w1, b1, w2):
    B, S, G, dh = k_raw.shape
    Nc = (S - CMP_LEN) // CMP_STRIDE + 1
    idx = np.arange(Nc)[:, None] * CMP_STRIDE + np.arange(CMP_LEN)[None, :]
    blocks = k_raw[:, idx] + pos_emb[None, None, :, None, :]
    flat = blocks.transpose(0, 1, 3, 2, 4).reshape(B, Nc, G, CMP_LEN * dh)
    return jax.nn.gelu(flat @ w1 + b1) @ w2


def nsa_attention(q, kc, vc, k_slc, v_slc, k_win, v_win, gates):
    B, S, G, Hg, dh = q.shape
    Nc = kc.shape[1]
    Nsel = S // SEL_LEN
    topk = min(SEL_TOPK, Nsel)
    nblk = S // Q_BLOCK
    scale = HEAD_DIM ** -0.5
    cmp_end = jnp.arange(Nc) * CMP_STRIDE + CMP_LEN - 1
    ci = np.arange(Nc)[:, None] * CMP_STRIDE
    sj = np.arange(Nsel)[None, :] * SEL_LEN
    overlap = jnp.asarray(((ci < sj + SEL_LEN) & (ci + CMP_LEN > sj)).astype(np.float32))
    kb = k_slc.reshape(B, Nsel, SEL_LEN, G, dh).transpose(0, 3, 1, 2, 4)
    vb = v_slc.reshape(B, Nsel, SEL_LEN, G, dh).transpose(0, 3, 1, 2, 4)
    kw = jnp.pad(k_win, ((0, 0), (WINDOW, 0), (0, 0), (0, 0)))
    vw = jnp.pad(v_win, ((0, 0), (WINDOW, 0), (0, 0), (0, 0)))
    bi = jnp.arange(B)[:, None, None, None]
    gi = jnp.arange(G)[None, :, None, None]
    jsel = jnp.arange(Nsel)

    def block(n):
        q0 = n * Q_BLOCK
        t = q0 + jnp.arange(Q_BLOCK)
        qb = lax.dynamic_slice_in_dim(q, q0, Q_BLOCK, axis=1).astype(jnp.float32) * scale
        gb = jax.nn.sigmoid(lax.dynamic_slice_in_dim(gates, q0, Q_BLOCK, axis=1).astype(jnp.float32))
        s = jnp.einsum('bqghd,bcgd->bghqc', qb, kc)
        p_cmp = masked_softmax(s, cmp_end[None, :] <= t[:, None])
        o_cmp = jnp.einsum('bghqc,bcgd->bqghd', p_cmp, vc)
        imp = jnp.einsum('bghqc,cj->bgqj', p_cmp, overlap)
        jt = t // SEL_LEN
        valid = jsel[None, :] * SEL_LEN <= t[:, None]
        forced = (jsel[None, :] == 0) | (jsel[None, :] == jt[:, None]) | (jsel[None, :] == jt[:, None] - 1)
        score = jnp.where(valid, jnp.where(forced, jnp.inf, imp), -1.0)
        top_val, top_idx = lax.top_k(score, topk)
        ks = kb[bi, gi, top_idx]
        vs = vb[bi, gi, top_idx]
        s = jnp.einsum('bqghd,bgqnkd->bghqnk', qb, ks).reshape(B, G, Hg, Q_BLOCK, topk * SEL_LEN)
        key_pos = top_idx[..., None] * SEL_LEN + jnp.arange(SEL_LEN)
        m = (key_pos <= t[None, None, :, None, None]) & (top_val >= 0)[..., None]
        p = masked_softmax(s, m.reshape(B, G, 1, Q_BLOCK, topk * SEL_LEN))
        o_slc = jnp.einsum('bghqnk,bgqnkd->bqghd', p.reshape(B, G, Hg, Q_BLOCK, topk, SEL_LEN), vs)
        kwb = lax.dynamic_slice_in_dim(kw, q0, Q_BLOCK + WINDOW, axis=1)
        vwb = lax.dynamic_slice_in_dim(vw, q0, Q_BLOCK + WINDOW, axis=1)
        kpos = q0 - WINDOW + jnp.arange(Q_BLOCK + WINDOW)
        m = (kpos[None, :] <= t[:, None]) & (kpos[None, :] > t[:, None] - WINDOW) & (kpos[None, :] >= 0)
        p = masked_softmax(jnp.einsum('bqghd,bkgd->bghqk', qb, kwb), m)
        o_win = jnp.einsum('bghqk,bkgd->bqghd', p, vwb)
        return gb[..., 0:1] * o_cmp + gb[..., 1:2] * o_slc + gb[..., 2:3] * o_win

    out = lax.map(block, jnp.arange(nblk))
    return out.transpose(1, 0, 2, 3, 4, 5).reshape(B, S, G * Hg * dh)


def setup_inputs(seed: int = 0) -> dict:
    key = jax.random.key(seed)
    ks = jax.random.split(key, 18)
    nrm = lambda k, shape, s: jax.random.normal(k, shape, jnp.float32) * s
    L = DEPTH
    return {
        "x": nrm(ks[0], (BATCH, SEQ, D_MODEL), 1.0),
        "norm_w": 1.0 + nrm(ks[1], (L, D_MODEL), 0.02),
        "w_in": nrm(ks[2], (L, D_MODEL, IN_COLS), D_MODEL ** -0.5),
        "hg_lb_logits": nrm(ks[3], (L + 1, HG_FDIM), 0.5),
        "hg_norm_w": 1.0 + nrm(ks[4], (L, HG_DV), 0.02),
        "cmp_k_pos": nrm(ks[5], (L, CMP_LEN, HEAD_DIM), 0.1),
        "cmp_k_w1": nrm(ks[6], (L, CMP_LEN * HEAD_DIM, CMP_HIDDEN), (CMP_LEN * HEAD_DIM) ** -0.5),
        "cmp_k_b1": nrm(ks[7], (L, CMP_HIDDEN), 0.01),
        "cmp_k_w2": nrm(ks[8], (L, CMP_HIDDEN, HEAD_DIM), CMP_HIDDEN ** -0.5),
        "cmp_v_pos": nrm(ks[9], (L, CMP_LEN, HEAD_DIM), 0.1),
        "cmp_v_w1": nrm(ks[10], (L, CMP_LEN * HEAD_DIM, CMP_HIDDEN), (CMP_LEN * HEAD_DIM) ** -0.5),
        "cmp_v_b1": nrm(ks[11], (L, CMP_HIDDEN), 0.01),
        "cmp_v_w2": nrm(ks[12], (L, CMP_HIDDEN, HEAD_DIM), CMP_HIDDEN ** -0.5),
        "w_branch_hg": nrm(ks[13], (L, HG_WIDTH, D_MODEL), HG_WIDTH ** -0.5),
        "w_branch_nsa": nrm(ks[14], (L, NSA_WIDTH, D_MODEL), NSA_WIDTH ** -0.5),
        "w_out": nrm(ks[15], (L, D_MODEL, D_MODEL), D_MODEL ** -0.5),
        "final_norm_w": 1.0 + nrm(ks[16], (D_MODEL,), 0.02),
    }


def reference(x, norm_w, w_in, hg_lb_logits, hg_norm_w, cmp_k_pos, cmp_k_w1, cmp_k_b1, cmp_k_w2,
              cmp_v_pos, cmp_v_w1, cmp_v_b1, cmp_v_w2, w_branch_hg, w_branch_nsa, w_out, final_norm_w):
    B, S, _ = x.shape
    pos = jnp.arange(S, dtype=jnp.float32)
    lower_bounds = jnp.cumsum(jax.nn.softmax(hg_lb_logits.astype(jnp.float32), axis=0), axis=0)
    offsets = np.cumsum(IN_SPLITS)[:-1].tolist()
    h = x
    for layer in range(DEPTH):
        xn = rmsnorm(h, norm_w[layer]).astype(x.dtype)
        proj = xn @ w_in[layer]
        (hg_q, hg_f, hg_i, hg_z, nsa_q, k_cmp, v_cmp, k_slc, v_slc, k_win, v_win,
         nsa_g, nsa_z, merge_g) = jnp.split(proj, offsets, axis=-1)
        lb = lower_bounds[layer].reshape(HG_HEADS, HG_DK)
        o_hg = hgrn2_chunkwise(hg_q.reshape(B, S, HG_HEADS, HG_DK), hg_f.reshape(B, S, HG_HEADS, HG_DK),
                               hg_i.reshape(B, S, HG_HEADS, HG_DV), lb)
        o_hg = rmsnorm(o_hg, hg_norm_w[layer]) * jax.nn.silu(
            hg_z.astype(jnp.float32).reshape(B, S, HG_HEADS, HG_DV))
        y_hg = o_hg.reshape(B, S, HG_WIDTH).astype(x.dtype) @ w_branch_hg[layer]
        q = partial_rope(nsa_q.reshape(B, S, NSA_HEADS, HEAD_DIM), pos).reshape(B, S, NSA_KV, NSA_HG, HEAD_DIM)
        kv = lambda a: a.reshape(B, S, NSA_KV, HEAD_DIM)
        kc = compress_blocks(partial_rope(kv(k_cmp), pos), cmp_k_pos[layer], cmp_k_w1[layer],
                             cmp_k_b1[layer], cmp_k_w2[layer])
        vc = compress_blocks(kv(v_cmp).astype(jnp.float32), cmp_v_pos[layer], cmp_v_w1[layer],
                             cmp_v_b1[layer], cmp_v_w2[layer])
        o_nsa = nsa_attention(q, kc, vc, partial_rope(kv(k_slc), pos), kv(v_slc),
                              partial_rope(kv(k_win), pos), kv(v_win),
                              nsa_g.reshape(B, S, NSA_KV, NSA_HG, 3))
        o_nsa = o_nsa * jax.nn.silu(nsa_z.astype(jnp.float32))
        y_nsa = o_nsa.astype(x.dtype) @ w_branch_nsa[layer]
        g_hg, g_nsa = jnp.split(jax.nn.sigmoid(merge_g.astype(jnp.float32)), 2, axis=-1)
        merged = (g_hg * y_hg + g_nsa * y_nsa).astype(x.dtype)
        h = h + (merged @ w_out[layer]).astype(h.dtype)
    return rmsnorm(h, final_norm_w).astype(x.dtype)
```

```python
import numpy as np
import ml_dtypes
from contextlib import ExitStack
import concourse.bass as bass
import concourse.mybir as mybir
from concourse.bass_utils import run_bass_kernel_spmd

F32 = mybir.dt.float32
BF16 = mybir.dt.bfloat16
AF = mybir.ActivationFunctionType
ALU = mybir.AluOpType
ENG = ('pe', 'act', 'dve', 'pool', 'sp')
NBF = ml_dtypes.bfloat16

S_LEN = 8192
D = 2048
NCOLS = 1808
FM0 = 0
TM0 = 896


class Reg:
    __slots__ = ('w', 'rs', 'excl')

    def __init__(s, excl=False):
        s.w = None
        s.rs = {}
        s.excl = excl


def PReg():
    return Reg(True)


class Buf:
    def __init__(s, t):
        s.t = t
        s.g = Reg()


class Sched:
    EP = 4000
    NDMA = 24

    def __init__(s, nc, es):
        s.nc = nc
        s.es = es
        s.q = {k: [] for k in ENG}
        s.cnt = {k: 0 for k in ENG}
        s.waited = {k: {} for k in ENG}
        s.csem = {k: [] for k in ENG}
        s.dsem = [es.enter_context(nc.semaphore(f"d{j}")) for j in range(s.NDMA)]
        s.dcnt = [0] * s.NDMA
        s.dn = 0

    def _csem(s, eng, ep):
        while len(s.csem[eng]) <= ep:
            s.csem[eng].append(s.es.enter_context(s.nc.semaphore(f"c_{eng}_{len(s.csem[eng])}")))
        return s.csem[eng][ep]

    def _wait(s, eng, evs):
        need = {}
        for ev in evs:
            if ev is None:
                continue
            if ev[0] == 'c':
                _, e2, idx = ev
                if e2 == eng and idx > s.cnt[eng]:
                    continue
                ep = (idx - 1) // s.EP
                val = (idx - 1) % s.EP + 1
                w = s.waited[eng].get(('c', e2), (-1, 0))
                if w[0] > ep or (w[0] == ep and w[1] >= val):
                    continue
                key = ('c', e2)
                cur = need.get(key)
                if cur is None or (ep, val) > (cur[0], cur[1]):
                    need[key] = (ep, val)
            else:
                _, j, m = ev
                if s.waited[eng].get(('d', j), (0, 0))[1] >= m:
                    continue
                key = ('d', j)
                cur = need.get(key)
                if cur is None or m > cur[1]:
                    need[key] = (0, m)
        for key, (ep, val) in need.items():
            s.waited[eng][key] = (ep, val)
            if key[0] == 'c':
                sem = s._csem(key[1], ep)
                v = val
            else:
                sem = s.dsem[key[1]]
                v = 16 * val
            s.q[eng].append(lambda e, sem=sem, v=v: e.wait_ge(sem, v))

    @staticmethod
    def _deps(r, w):
        evs = []
        for x in r:
            evs.append(x.w)
        for x in w:
            evs.append(x.w)
            evs.extend(x.rs.values())
        return evs

    @staticmethod
    def _upd(ev, key, r, w):
        for x in r:
            x.rs[key] = ev
        for x in w:
            x.w = ev
            x.rs = {}

    def op(s, eng, fn, r=(), w=(), inc=True):
        if any(x.excl for x in r):
            w = list(w) + [x for x in r if x.excl]
            r = [x for x in r if not x.excl]
        s._wait(eng, s._deps(r, w))
        idx = s.cnt[eng] + 1
        ev = ('c', eng, idx)
        if inc:
            s.cnt[eng] = idx
            sem = s._csem(eng, (idx - 1) // s.EP)
            s.q[eng].append(lambda e, fn=fn, sem=sem: fn(e).then_inc(sem, 1))
        else:
            s.q[eng].append(lambda e, fn=fn: fn(e))
        s._upd(ev, ('c', eng), r, w)
        return ev

    def dma(s, eng, out, in_, r=(), w=()):
        j = s.dn % s.NDMA
        prev = [('d', j, s.dcnt[j])] if s.dcnt[j] > 0 else []
        s._wait(eng, s._deps(r, w) + prev)
        s.dn += 1
        s.dcnt[j] += 1
        ev = ('d', j, s.dcnt[j])
        sem = s.dsem[j]
        s.q[eng].append(lambda e, out=out, in_=in_, sem=sem: e.dma_start(out=out, in_=in_).then_inc(sem, 16))
        s._upd(ev, ('d', j), r, w)
        return ev

    def barrier(s):
        evs = [('c', e2, s.cnt[e2]) for e2 in ENG if s.cnt[e2] > 0]
        evs += [('d', j, s.dcnt[j]) for j in range(s.NDMA) if s.dcnt[j] > 0]
        for e_ in ENG:
            s._wait(e_, evs)

    def finish(s):
        evs = [('d', j, s.dcnt[j]) for j in range(s.NDMA) if s.dcnt[j] > 0]
        s._wait('sp', evs)

    def emit(s):
        nc = s.nc
        with nc.Block() as block:
            @block.sync
            def _(e):
                for t in s.q['sp']:
                    t(e)

            @block.tensor
            def _(e):
                for t in s.q['pe']:
                    t(e)

            @block.scalar
            def _(e):
                for t in s.q['act']:
                    t(e)

            @block.vector
            def _(e):
                for t in s.q['dve']:
                    t(e)

            @block.gpsimd
            def _(e):
                for t in s.q['pool']:
                    t(e)


class K:
    def __init__(s, S):
        s.S = S

    def act(s, out, in_, func, r, w, **kw):
        s.S.op('act', lambda e: e.activation(out, in_, func, **kw), r=r, w=w)

    def ts(s, eng, out, in0, s1, s2, op0, op1, r, w, **kw):
        if op1 is None:
            s.S.op(eng, lambda e: e.tensor_scalar(out, in0, s1, s2, op0, **kw), r=r, w=w)
        else:
            s.S.op(eng, lambda e: e.tensor_scalar(out, in0, s1, s2, op0, op1, **kw), r=r, w=w)

    def tt(s, eng, out, in0, in1, op, r, w):
        s.S.op(eng, lambda e: e.tensor_tensor(out, in0, in1, op), r=r, w=w)

    def stt(s, eng, out, in0, sc, in1, op0, op1, r, w):
        s.S.op(eng, lambda e: e.scalar_tensor_tensor(out, in0, sc, in1, op0, op1), r=r, w=w)

    def cp(s, eng, out, in_, r, w):
        if eng == 'act':
            s.S.op('act', lambda e: e.activation(out, in_, AF.Copy), r=r, w=w)
        else:
            s.S.op(eng, lambda e: e.tensor_copy(out, in_), r=r, w=w)

    def ms(s, eng, ap, val, w):
        s.S.op(eng, lambda e: e.memset(ap, val), r=(), w=w)

    def mm(s, out, lhsT, rhs, start, stop, r, w, inc=None):
        if inc is None:
            inc = stop
        s.S.op('pe', lambda e: e.matmul(out, lhsT, rhs, start=start, stop=stop), r=r, w=w, inc=inc)

    def tr(s, out, in_, ident, r, w, inc=True):
        s.S.op('pe', lambda e: e.transpose(out, in_, ident), r=r, w=w, inc=inc)

    def rcp(s, out, in_, r, w):
        s.S.op('dve', lambda e: e.reciprocal(out, in_), r=r, w=w)

    def dma(s, out, in_, r=(), w=(), eng='sp'):
        s.S.dma(eng, out, in_, r=r, w=w)


def _mk(nc, es):
    def sb(name, shape, dt):
        return Buf(es.enter_context(nc.sbuf_tensor(name, list(shape), dt)))

    def ps(name, shape, dt):
        return es.enter_context(nc.psum_tensor(name, list(shape), dt))
    return sb, ps


def _norm_transpose(k, xt, xn, tps, tpg, xnT, xnTg, ident, junk, ss, rs, rs2):
    k.act(junk.t[:], xt.t[:], AF.Square, r=[xt.g], w=[junk.g, ss.g], accum_out=ss.t[:])
    k.ts('dve', rs.t[:], ss.t[:], 1.0 / D, 1e-6, ALU.mult, ALU.add, r=[ss.g], w=[rs.g])
    k.act(rs.t[:], rs.t[:], AF.Sqrt, r=[], w=[rs.g])
    k.rcp(rs2.t[:], rs.t[:], r=[rs.g], w=[rs2.g])
    k.ts('dve', xn.t[:], xt.t[:], rs2.t[:, 0:1], None, ALU.mult, None, r=[xt.g, rs2.g], w=[xn.g])
    for half in range(2):
        for kk in range(8):
            kc = half * 8 + kk
            k.tr(tps[half][:, kk, :], xn.t[:, kc * 128:(kc + 1) * 128], ident.t[:], r=[xn.g, ident.g],
                 w=[tpg[half]], inc=(kk == 7))
        k.cp('act' if half == 0 else 'dve', xnT[:, half * 8:(half + 1) * 8, :], tps[half][:, :, :],
             r=[tpg[half]], w=[xnTg])


class _Stop(Exception):
    pass


def build_p1(NTILES=64, STOP=None, FUSED=False):
    nc = bass.Bass("TRN2", target_bir_lowering=False)
    try:
        _build_p1(nc, NTILES, STOP, FUSED)
    except _Stop:
        pass
    return nc


def _build_p1(nc, NTILES, STOP, FUSED):
    NMT = NTILES // 4

    def din(name, shape, dt=F32):
        return nc.dram_tensor(name, list(shape), dt, kind="ExternalInput").ap()

    x = din("x", [S_LEN, D])
    wc = din("wc", [D, NCOLS])
    wcc = din("wcc", [D, 256])
    normw_d = din("normw", [128, 16])
    lb0_d = din("lb0", [128, 128])
    lb1_d = din("lb1", [128, 128])
    hnw_d = din("hnw", [128, 128])
    cw = {}
    for kv in "kv":
        cw[kv] = dict(pos=din(f"c{kv}_posT", [128, 32]), w1=din(f"c{kv}_w1", [4096, 512]),
                      b1=din(f"c{kv}_b1", [128, 4]), w2=din(f"c{kv}_w2", [512, 128]))
    ident_d = din("ident", [128, 128], BF16)
    ublk_d = din("ublk", [128, 128])
    tri_d = din("tri", [128, 128], BF16)
    trilo_d = din("trilo", [128, 128], BF16)
    rmat_d = din("rmat", [128, 32], BF16)
    bexp_d = din("bexp", [128, 8192], BF16)
    ovx_d = din("ovx", [128, 4, 132], BF16)
    mtw_d = din("mtw", [128, 2304], BF16)
    m1w_d = din("m1w", [128, 256])
    m2w_d = din("m2w", [128, 256])
    post_d = din("postab", [128, 2048])
    invf_d = din("invf", [128, 1])
    if FUSED:
        oxd_t = nc.dram_tensor("oxd", [12 * 8 * 32, 1024], BF16)
        gath_t = nc.dram_tensor("gath", [8 * 12 * 8 * 32, 1024], BF16)
        oxd = oxd_t.ap()
        gath = gath_t.ap()
        P2 = dict(xo=din("xo", [1024, D]), wm=din("wm", [D, 4096]), wbh=din("wbh", [1024, D]),
                  wbn=din("wbn", [D, D]), wo=din("wo", [D, D]), fnw=din("fnw", [128, D]),
                  selm=din("selm", [128, 8, 128], BF16),
                  y=nc.dram_tensor("y", [1024, D], F32, kind="ExternalOutput").ap())
    else:
        ox = nc.dram_tensor("ox", [384, S_LEN], BF16, kind="ExternalOutput").ap()
    xnT_d = nc.dram_tensor("xnT_d", [64, 128, 2048], BF16).ap()

    def store_ox(a_base, T, src, srcg):
        if not FUSED:
            k_[0].dma(ox[32 * a_base:32 * a_base + 128, 512 * T:512 * (T + 1)], src, r=[srcg])
            return
        j = T // 2
        c0 = (T % 2) * 512
        for a0 in range(4):
            r0 = ((a_base + a0) * 8 + j) * 32
            k_[0].dma(oxd[r0:r0 + 32, c0:c0 + 512], src[32 * a0:32 * a0 + 32], r=[srcg])

    k_ = [None]

    with ExitStack() as es:
        S = Sched(nc, es)
        k = K(S)
        k_[0] = k
        sb_outer, _ = _mk(nc, es)
        esP = ExitStack()
        sb, ps = _mk(nc, esP)

        def stop_here(tag):
            if STOP == tag:
                S.barrier()
                S.finish()
                S.emit()
                raise _Stop()

        normw = sb_outer("normw_s", [128, 16], F32)
        ident = sb_outer("ident_s", [128, 128], BF16)
        Wb = sb("Wb", [128, 16, NCOLS], BF16)
        ublk = sb("ublk_s", [128, 128], F32)
        tri = sb("tri_s", [128, 128], BF16)
        trilo = sb("trilo_s", [128, 128], BF16)
        rmat = sb("rmat_s", [128, 32], BF16)
        costab = sb("costab", [128, 2048], F32)
        sintab = sb("sintab", [128, 2048], F32)
        kcT = sb("kcT", [128, 512], BF16)
        vc = sb("vc", [128, 4, 132], BF16)
        junk = sb("junk", [128, 128], BF16)
        ss = sb("ss", [128, 1], F32)
        rs = sb("rs", [128, 1], F32)
        rs2 = sb("rs2", [128, 1], F32)
        rt1 = sb("rt1", [32, 512], F32)
        rt2 = sb("rt2", [32, 512], F32)

        for b_, d_ in ((normw, normw_d), (ident, ident_d), (ublk, ublk_d), (tri, tri_d), (trilo, trilo_d),
                       (rmat, rmat_d)):
            k.dma(b_.t[:], d_, w=[b_.g])

        with ExitStack() as es0:
            sb0, _ = _mk(nc, es0)
            post = sb0("post", [128, 2048], F32)
            invf = sb0("invf_s", [128, 1], F32)
            u = sb0("ang_u", [128, 2048], F32)
            u2 = sb0("ang_u2", [128, 2048], F32)
            k.dma(post.t[:], post_d, w=[post.g])
            k.dma(invf.t[:], invf_d, w=[invf.g])
            k.ts('dve', u.t[:], post.t[:], invf.t[:, 0:1], 1.0 / (2 * np.pi), ALU.mult, ALU.mult,
                 r=[post.g, invf.g], w=[u.g])
            SC = 2 * np.pi * (1 - 1e-6)
            BI = -np.pi * (1 - 1e-6)
            ui = sb0("ang_i", [128, 2048], mybir.dt.int32)
            m1 = sb0("ang_m1", [128, 2048], F32)

            def table(dst, shift):
                if shift:
                    k.ts('dve', u2.t[:], u.t[:], shift, None, ALU.add, None, r=[u.g], w=[u2.g])
                    src = u2
                else:
                    src = u
                k.cp('dve', ui.t[:], src.t[:], r=[src.g], w=[ui.g])
                k.cp('dve', m1.t[:], ui.t[:], r=[ui.g], w=[m1.g])
                k.tt('dve', u2.t[:], src.t[:], m1.t[:], ALU.subtract, r=[src.g, m1.g], w=[u2.g])
                k.ts('dve', m1.t[:], u2.t[:], 0.5, None, ALU.is_gt, None, r=[u2.g], w=[m1.g])
                k.tt('dve', u2.t[:], u2.t[:], m1.t[:], ALU.subtract, r=[m1.g], w=[u2.g])
                k.ts('dve', m1.t[:], u2.t[:], -0.5, None, ALU.is_lt, None, r=[u2.g], w=[m1.g])
                k.tt('dve', u2.t[:], u2.t[:], m1.t[:], ALU.add, r=[m1.g], w=[u2.g])
                k.act(dst.t[:], u2.t[:], AF.Sin, r=[u2.g], w=[dst.g], scale=SC)

            table(sintab, 0.0)
            table(costab, 0.25)
            S.barrier()
            stop_here('tables')

        def rope(X, Xg, N, cs, csg, rp, rpg):
            k.mm(rp[0:32, 0:N], rmat.t[:, :], X, True, True, r=[Xg, rmat.g], w=[rpg])
            k.tt('dve', rt1.t[0:32, 0:N], X[0:32], cs[0:32, 0, 0:N], ALU.mult, r=[Xg, csg], w=[rt1.g])
            k.tt('dve', rt2.t[0:32, 0:N], rp[0:32, 0:N], cs[0:32, 1, 0:N], ALU.mult, r=[rpg, csg], w=[rt2.g])
            k.tt('pool', X[0:32], rt1.t[0:32, 0:N], rt2.t[0:32, 0:N], ALU.add, r=[rt1.g, rt2.g], w=[Xg])

        def load_cs(cs, tok0, N):
            a = tok0 // 2048
            off = tok0 % 2048
            k.dma(cs.t[0:32, 0, 0:N], costab.t[32 * a:32 * a + 32, off:off + N], r=[costab.g], w=[cs.g])
            k.dma(cs.t[0:32, 1, 0:N], sintab.t[32 * a:32 * a + 32, off:off + N], r=[sintab.g], w=[cs.g])

        with ExitStack() as es1:
            sb1, _ = _mk(nc, es1)
            wst = [sb1(f"wst{j}", [128, NCOLS], F32) for j in range(2)]
            for kc in range(16):
                st = wst[kc % 2]
                k.dma(st.t[:], wc[kc * 128:(kc + 1) * 128, :], w=[st.g])
                k.ts('dve' if kc % 2 == 0 else 'pool', Wb.t[:, kc, :], st.t[:], normw.t[:, kc:kc + 1], None,
                     ALU.mult, None, r=[st.g, normw.g], w=[Wb.g])
            S.barrier()
            stop_here('weights')

        with ExitStack() as esA:
            sbA, psA = _mk(nc, esA)
            kcmpT = sbA("kcmpT", [128, S_LEN], BF16)
            vcmpT = sbA("vcmpT", [128, S_LEN], BF16)
            if NTILES < 64:
                k.ms('pool', kcmpT.t[:], 0.0, w=[kcmpT.g])
                k.ms('pool', vcmpT.t[:], 0.0, w=[vcmpT.g])
            WbA = sbA("WbA", [128, 16, 256], BF16)
            junkA = sbA("junkA", [128, 2048], BF16)
            with ExitStack() as es1b:
                sb1b, _ = _mk(nc, es1b)
                wstc = [sb1b(f"wstc{j}", [128, 256], F32) for j in range(2)]
                for kc in range(16):
                    st = wstc[kc % 2]
                    k.dma(st.t[:], wcc[kc * 128:(kc + 1) * 128, :], w=[st.g])
                    k.ts('dve' if kc % 2 == 0 else 'pool', WbA.t[:, kc, :], st.t[:], normw.t[:, kc:kc + 1], None,
                         ALU.mult, None, r=[st.g, normw.g], w=[WbA.g])
                S.barrier()
            with ExitStack() as esA1:
                sbA1, psA1 = _mk(nc, esA1)
                xts = [sbA1(f"xt{j}", [128, D], F32) for j in range(2)]
                xns = [sbA1(f"xn{j}", [128, D], BF16) for j in range(2)]
                xnTs = [sbA1(f"xnTa{j}", [128, 16, 128], BF16) for j in range(2)]
                css = [sbA1(f"csA{j}", [32, 2, 128], F32) for j in range(2)]
                tps = [psA1(f"tpA{j}", [128, 8, 128], BF16) for j in range(2)]
                tpg = [PReg(), PReg()]
                ca = psA1("caA", [128, 256], F32)
                cag = [PReg()] * 2
                rp = psA1("rpA", [128, 512], F32)
                rpg = PReg()
                for n in range(NTILES):
                    xt, xn, xnT, cs = xts[n % 2], xns[n % 2], xnTs[n % 2], css[n % 2]
                    k.dma(xt.t[:], x[128 * n:128 * (n + 1), :], w=[xt.g])
                    load_cs(cs, 128 * n, 128)
                    _norm_transpose(k, xt, xn, [tps[0], tps[1]], tpg, xnT.t, xnT.g, ident, junkA, ss, rs, rs2)
                    k.dma(xnT_d[n].rearrange("p (k t) -> p k t", k=16), xnT.t[:, :, :], r=[xnT.g], w=[])
                    for ci in range(2):
                        for kc in range(16):
                            k.mm(ca[:, ci * 128:(ci + 1) * 128], WbA.t[:, kc, ci * 128:(ci + 1) * 128],
                                 xnT.t[:, kc, :], kc == 0, kc == 15, r=[WbA.g, xnT.g], w=[cag[ci]])
                    k.cp('act', kcmpT.t[:, 128 * n:128 * (n + 1)], ca[:, 0:128], r=[cag[0]], w=[kcmpT.g])
                    rope(kcmpT.t[:, 128 * n:128 * (n + 1)], kcmpT.g, 128, cs.t, cs.g, rp, rpg)
                    k.cp('dve', vcmpT.t[:, 128 * n:128 * (n + 1)], ca[:, 128:256], r=[cag[1]], w=[vcmpT.g])
                S.barrier()
                stop_here('passA')

            with ExitStack() as esM:
                sbM, psM = _mk(nc, esM)
                w1st = [sbM(f"w1st{j}", [128, 4, 512], F32) for j in range(2)]
                w1b = [sbM(f"w1b{j}", [128, 4, 512], BF16) for j in range(2)]
                hacc = [psM(f"hacc{j}", [128, 512], F32) for j in range(4)]
                hag = [PReg() for _ in range(4)]
                pbs = [psM(f"pbias{j}", [128, 512], F32) for j in range(4)]
                pbg = [PReg() for _ in range(4)]
                posf = sbM("posf", [128, 32], F32)
                posb = sbM("posb", [128, 32], BF16)
                b1s = sbM("b1s", [128, 4], F32)
                btot = sbM("btot", [128, 4], F32)
                w2f = sbM("w2f", [128, 4, 128], F32)
                w2b = sbM("w2b", [128, 4, 128], BF16)
                hb = sbM("hb", [128, 512], F32)
                t1 = sbM("mt1", [128, 512], F32)
                t2 = sbM("mt2", [128, 512], F32)
                hT = sbM("hT", [128, 4, 512], BF16)
                k.ms('pool', hT.t[:, :, :], 0.0, w=[hT.g])
                k.ms('pool', kcT.t[:, :], 0.0, w=[kcT.g])
                k.ms('pool', vc.t[:, :, :], 0.0, w=[vc.g])
                k.ms('pool', vc.t[:, :, 128:129], 1.0, w=[vc.g])
                lgc = 0
                for kv in "kv":
                    src = kcmpT if kv == "k" else vcmpT
                    src3 = src.t[:, :].rearrange("p (c s) -> p c s", s=16)
                    W = cw[kv]
                    k.dma(posf.t[:], W["pos"], w=[posf.g])
                    k.cp('pool', posb.t[:], posf.t[:], r=[posf.g], w=[posb.g])
                    k.dma(b1s.t[:], W["b1"], w=[b1s.g])
                    k.dma(w2f.t[:, :, :], W["w2"].rearrange("(c p) d -> p c d", p=128), w=[w2f.g])
                    k.cp('pool', w2b.t[:, :, :], w2f.t[:, :, :], r=[w2f.g], w=[w2b.g])
                    w1v = W["w1"].rearrange("(l d) h -> d l h", d=128)
                    for lg in range(8):
                        st, wb = w1st[lgc % 2], w1b[lgc % 2]
                        lgc += 1
                        k.dma(st.t[:, :, :], w1v[:, 4 * lg:4 * lg + 4, :], w=[st.g])
                        k.cp('dve' if lg % 2 == 0 else 'pool', wb.t[:, :, :], st.t[:, :, :], r=[st.g], w=[wb.g])
                        for ll in range(4):
                            l = 4 * lg + ll
                            rhs = src3[:, (l // 16):(l // 16) + 511, l % 16]
                            for hc in range(4):
                                k.mm(hacc[hc][:, 0:511], wb.t[:, ll, hc * 128:(hc + 1) * 128], rhs, l == 0, l == 31,
                                     r=[wb.g, src.g], w=[hag[hc]])
                                k.mm(pbs[hc][:, 0:1], wb.t[:, ll, hc * 128:(hc + 1) * 128], posb.t[:, l:l + 1],
                                     l == 0, l == 31, r=[wb.g, posb.g], w=[pbg[hc]],
                                     inc=(l == 31 or (ll == 3 and hc == 3)))
                    for hc in range(4):
                        k.tt('dve', btot.t[:, hc:hc + 1], pbs[hc][:, 0:1], b1s.t[:, hc:hc + 1], ALU.add,
                             r=[pbg[hc], b1s.g], w=[btot.g])
                    for hc in range(4):
                        k.act(hb.t[:, 0:511], hacc[hc][:, 0:511], AF.Identity, r=[hag[hc], btot.g], w=[hb.g],
                              bias=btot.t[:, hc:hc + 1])
                        k.tt('dve', t1.t[:, 0:511], hb.t[:, 0:511], hb.t[:, 0:511], ALU.mult, r=[hb.g], w=[t1.g])
                        k.ts('dve', t1.t[:, 0:511], t1.t[:, 0:511], 0.044715, 1.0, ALU.mult, ALU.add, r=[], w=[t1.g])
                        k.tt('dve', t1.t[:, 0:511], t1.t[:, 0:511], hb.t[:, 0:511], ALU.mult, r=[hb.g], w=[t1.g])
                        k.act(t2.t[:, 0:511], t1.t[:, 0:511], AF.Sigmoid, r=[t1.g], w=[t2.g], scale=1.5957691216057308)
                        k.tt('dve', hT.t[:, hc, 0:511], hb.t[:, 0:511], t2.t[:, 0:511], ALU.mult, r=[hb.g, t2.g],
                             w=[hT.g])
                    if kv == "k":
                        for hc in range(4):
                            k.mm(hacc[0][:, 0:511], w2b.t[:, hc, :], hT.t[:, hc, 0:511], hc == 0, hc == 3,
                                 r=[w2b.g, hT.g], w=[hag[0]])
                        k.cp('act', kcT.t[:, 0:511], hacc[0][:, 0:511], r=[hag[0]], w=[kcT.g])
                    else:
                        for ct in range(4):
                            for hc in range(4):
                                k.mm(hacc[ct][:, 0:128], hT.t[:, hc, 128 * ct:128 * (ct + 1)], w2b.t[:, hc, :],
                                     hc == 0, hc == 3, r=[w2b.g, hT.g], w=[hag[ct]])
                            k.cp('act', vc.t[:, ct, 0:128], hacc[ct][:, 0:128], r=[hag[ct]], w=[vc.g])
                S.barrier()
                stop_here('mlp')
            S.barrier()

        with ExitStack() as esB:
            sbB, psB = _mk(nc, esB)
            bexp = sbB("bexp_s", [128, 8192], BF16)
            ovx = sbB("ovx_s", [128, 4, 132], BF16)
            mtw = sbB("mtw_s", [128, 2304], BF16)
            m1w = sbB("m1w_s", [128, 256], F32)
            m2w = sbB("m2w_s", [128, 256], F32)
            lb = sbB("lb_s", [128, 128], F32)
            lb1 = sbB("lb1_s", [128, 128], F32)
            oml = sbB("oml_s", [128, 128], F32)
            hnw = sbB("hnw_s", [128, 128], F32)
            for b_, d_ in ((bexp, bexp_d), (ovx, ovx_d), (mtw, mtw_d), (m1w, m1w_d), (m2w, m2w_d), (lb, lb0_d),
                           (lb1, lb1_d), (hnw, hnw_d)):
                k.dma(b_.t[:], d_, w=[b_.g])
            k.tt('dve', lb.t[:], lb.t[:], lb1.t[:], ALU.subtract, r=[lb1.g], w=[lb.g])
            k.act(lb.t[:], lb.t[:], AF.Sigmoid, r=[], w=[lb.g])
            k.ts('dve', oml.t[:], lb.t[:], -1.0, 1.0, ALU.mult, ALU.add, r=[lb.g], w=[oml.g])

            kslcT = sbB("kslcT", [128, S_LEN], BF16)
            ksg = [Reg() for _ in range(16)]
            vslc = sbB("vslc", [128, 64, 132], BF16)
            vsg = [Reg() for _ in range(64)]
            kwinT = sbB("kwinT", [128, 1024], BF16)
            kwg = [Reg(), Reg()]
            vwin = sbB("vwin", [128, 8, 132], BF16)
            vwg = [Reg() for _ in range(8)]
            k.ms('pool', vslc.t[:, :, 128:132], 0.0, w=vsg)
            k.ms('pool', vslc.t[:, :, 128:129], 1.0, w=vsg)
            k.ms('pool', vwin.t[:, :, 128:132], 0.0, w=vwg)
            k.ms('pool', vwin.t[:, :, 128:129], 1.0, w=vwg)

            stop_here('b_setup')
            xnT = sbB("xnTb", [128, 16, 512], BF16)
            csB = sbB("csB", [32, 2, 512], F32)
            qtmp = sbB("qtmp", [128, 512], BF16)
            qT = sbB("qT", [128, 2048], BF16)
            hq = sbB("hq", [128, 512], F32)
            sgb = [sbB(f"sgb{j}", [128, 128], F32) for j in range(4)]
            vhg = [sbB(f"vhg{j}", [128, 128], BF16) for j in range(4)]
            zs = [sbB(f"zs{j}", [128, 128], F32) for j in range(4)]
            zn = [sbB(f"zn{j}", [128, 256], F32) for j in range(4)]
            gs = [sbB(f"gs{j}", [128, 16], F32) for j in range(4)]
            fb = sbB("fb", [128, 128], F32)
            gl = sbB("gl", [128, 128], F32)
            omf = sbB("omf", [128, 128], F32)
            enb = sbB("enb", [128, 128], F32)
            ebt = sbB("ebt", [128, 128], F32)
            ktm = sbB("ktm", [128, 128], BF16)
            kTs = sbB("kTs", [128, 128], BF16)
            qfT = sbB("qfT", [128, 128], BF16)
            qa = sbB("qa", [128, 128], BF16)
            qb = sbB("qb", [128, 128], BF16)
            Am = sbB("Am", [128, 128], BF16)
            Sf = sbB("Sf", [128, 128], F32)
            Sf2 = sbB("Sf2", [128, 128], F32)
            S1 = sbB("S1", [128, 128], F32)
            SAbf = [sbB(f"SAbf{j}", [128, 128], BF16) for j in range(2)]
            SBbf = sbB("SBbf", [128, 128], BF16)
            hz = sbB("hz", [128, 128], F32)
            ssh = sbB("ssh", [128, 1], F32)
            rsh = sbB("rsh", [128, 1], F32)
            rsh2 = sbB("rsh2", [128, 1], F32)
            og = sbB("og", [128, 128], BF16)
            oxh = sbB("oxh", [128, 512], BF16)
            oxn = sbB("oxn", [128, 2, 512], BF16)
            k.ms('pool', qa.t[:], 0.0, w=[qa.g])
            k.ms('pool', qb.t[:], 0.0, w=[qb.g])
            k.ms('pool', Sf.t[:], 0.0, w=[Sf.g])
            k.ms('pool', SAbf[0].t[:], 0.0, w=[SAbf[0].g])
            Ec = [sbB(f"Ec{j}", [128, 512], BF16) for j in range(2)] * 2
            Ecm = [sbB(f"Ecm{j}", [128, 512], BF16) for j in range(4)]
            Es = [sbB(f"Es{j}", [128, 512], BF16) for j in range(3)]
            Esm = [sbB(f"Esm{j}", [128, 512], BF16) for j in range(3)]
            mkd = [sbB(f"mkd{j}", [128, 128], F32) for j in range(2)]
            rz4 = sbB("rz4", [128, 4], F32)
            imps = sbB("imps", [128, 128], F32)
            sc = sbB("sc", [128, 128], F32)
            sc2 = sbB("sc2", [128, 128], F32)
            m8 = sbB("m8", [128, 8], F32)
            m8b = sbB("m8b", [128, 8], F32)
            thr = sbB("thr", [128, 1], F32)
            selb = sbB("selb", [128, 128], BF16)
            selTs = [sbB(f"selT{j}", [128, 128], BF16) for j in range(2)]
            ocmp = [sbB(f"ocmp{j}", [128, 264], F32) for j in range(2)]
            z3 = sbB("z3", [128, 3], F32)
            a3 = sbB("a3", [128, 3], F32)
            acc = sbB("acc", [128, 128], F32)
            onb = sbB("onb", [128, 128], BF16)

            BG = [psB(f"BG{j}", [128, 512], F32) for j in range(2)]
            BGg = [PReg(), PReg()]
            RP = psB("RPb", [128, 512], F32)
            RPg = PReg()
            SM = [psB(f"SM{j}", [128, 4, 128], F32) for j in range(2)]
            SMg = [[PReg()] * 4 for _ in range(2)]
            TPB = psB("TPB", [128, 8, 128], BF16)
            TPg = [PReg()] * 8
            OA = [psB(f"OA{j}", [128, 512], F32) for j in range(2)]
            OAg = [[PReg()] * 3 for _ in range(2)]
            bgc = [0]

            def nbg():
                j = bgc[0] % 2
                bgc[0] += 1
                return BG[j], BGg[j]

            hl = [None]
            es_c = [0]
            ipc = [0]

            def nbg4():
                j = ipc[0] % 4
                ipc[0] += 1
                return [(BG[0], BGg[0]), (BG[1], BGg[1]), (OA[0], OAg[0][0]), (OA[1], OAg[1][0])][j]

            for T in range(NMT):
                for i in range(4):
                    k.dma(xnT.t[:, :, 128 * i:128 * (i + 1)], xnT_d[4 * T + i].rearrange("p (k t) -> p k t", k=16),
                          w=[xnT.g])
                load_cs(csB, 512 * T, 512)
                stop_here('b_load')
                for i in range(4):
                    n = 4 * T + i
                    bg, bgg = nbg4()
                    for kc in range(16):
                        k.mm(bg[:, 0:512], xnT.t[:, kc, 128 * i:128 * (i + 1)], Wb.t[:, kc, TM0:TM0 + 512], kc == 0,
                             kc == 15, r=[xnT.g, Wb.g], w=[bgg])
                    stop_here('tm_a')
                    k.act(sgb[i].t[:], bg[:, 0:128], AF.Sigmoid, r=[], w=[bgg, sgb[i].g])
                    stop_here('tm_a1')
                    k.cp('dve', vhg[i].t[:], bg[:, 128:256], r=[], w=[bgg, vhg[i].g])
                    stop_here('tm_a2')
                    k.act(zs[i].t[:], bg[:, 256:384], AF.Sigmoid, r=[], w=[bgg, zs[i].g])
                    k.tt('dve', zs[i].t[:], zs[i].t[:], bg[:, 256:384], ALU.mult, r=[], w=[bgg, zs[i].g])
                    k.cp('dve', vslc.t[:, n, 0:128], bg[:, 384:512], r=[], w=[bgg, vsg[n]])
                    stop_here('tm_b')
                    bg, bgg = nbg4()
                    for kc in range(16):
                        k.mm(bg[:, 0:400], xnT.t[:, kc, 128 * i:128 * (i + 1)], Wb.t[:, kc, TM0 + 512:TM0 + 912],
                             kc == 0, kc == 15, r=[xnT.g, Wb.g], w=[bgg])
                    stop_here('tm_c')
                    k.cp('dve', vwin.t[:, n % 8, 0:128], bg[:, 0:128], r=[], w=[bgg, vwg[n % 8]])
                    k.act(zn[i].t[:], bg[:, 128:384], AF.Sigmoid, r=[], w=[bgg, zn[i].g])
                    k.tt('dve', zn[i].t[:], zn[i].t[:], bg[:, 128:384], ALU.mult, r=[], w=[bgg, zn[i].g])
                    k.act(gs[i].t[:, 0:16], bg[:, 384:400], AF.Sigmoid, r=[], w=[bgg, gs[i].g])
                stop_here('b_tm')
                for ch in range(7):
                    bg, bgg = nbg4()
                    for kc in range(16):
                        k.mm(bg[:, 0:512], Wb.t[:, kc, ch * 128:(ch + 1) * 128], xnT.t[:, kc, :], kc == 0, kc == 15,
                             r=[xnT.g, Wb.g], w=[bgg])
                    if ch == 0:
                        k.cp('act', hq.t[:], bg[:, 0:512], r=[bgg], w=[hq.g])
                    elif ch <= 4:
                        h = ch - 1
                        k.act(qtmp.t[:], bg[:, 0:512], AF.Identity, r=[bgg], w=[qtmp.g], scale=float(128 ** -0.5))
                        rope(qtmp.t[:, :], qtmp.g, 512, csB.t, csB.g, RP, RPg)
                        k.cp('pool', qT.t[:, :].rearrange("p (i h q) -> p i h q", i=4, h=4)[:, :, h, :],
                             qtmp.t[:, :].rearrange("p (i q) -> p i q", i=4), r=[qtmp.g], w=[qT.g])
                    elif ch == 5:
                        k.cp('act', kslcT.t[:, 512 * T:512 * (T + 1)], bg[:, 0:512], r=[bgg], w=[ksg[T]])
                        rope(kslcT.t[:, 512 * T:512 * (T + 1)], ksg[T], 512, csB.t, csB.g, RP, RPg)
                    else:
                        o_ = (T % 2) * 512
                        k.cp('act', kwinT.t[:, o_:o_ + 512], bg[:, 0:512], r=[bgg], w=[kwg[T % 2]])
                        rope(kwinT.t[:, o_:o_ + 512], kwg[T % 2], 512, csB.t, csB.g, RP, RPg)

                stop_here('b_fm')
                def hgrn_tile(i):
                    n = 4 * T + i
                    p = n % 2
                    k.tt('dve', fb.t[:], sgb[i].t[:], oml.t[:], ALU.mult, r=[sgb[i].g, oml.g], w=[fb.g])
                    k.tt('dve', fb.t[:], fb.t[:], lb.t[:], ALU.add, r=[lb.g], w=[fb.g])
                    k.act(gl.t[:], fb.t[:], AF.Ln, r=[fb.g], w=[gl.g])
                    k.ts('dve', omf.t[:], fb.t[:], -1.0, 1.0, ALU.mult, ALU.add, r=[fb.g], w=[omf.g])
                    yield
                    k.mm(SM[0][:, 0, :], ublk.t[:], gl.t[:], True, True, r=[ublk.g, gl.g], w=[SMg[0][0]])
                    k.mm(SM[0][:, 1, :], gl.t[:], ublk.t[:], True, True, r=[ublk.g, gl.g], w=[SMg[0][1]])
                    yield
                    k.act(enb.t[:], SM[0][:, 0, :], AF.Exp, r=[SMg[0][0]], w=[enb.g], scale=-1.0)
                    k.tt('dve', ktm.t[:], omf.t[:], enb.t[:], ALU.mult, r=[omf.g, enb.g], w=[ktm.g])
                    yield
                    k.act(ebt.t[:], SM[0][:, 1, :], AF.Exp, r=[SMg[0][1]], w=[ebt.g])
                    k.tt('dve', qfT.t[:], hq.t[:, 128 * i:128 * (i + 1)], ebt.t[:], ALU.mult, r=[hq.g, ebt.g],
                         w=[qfT.g])
                    k.cp('pool', qa.t[:, 0:64], qfT.t[:, 0:64], r=[qfT.g], w=[qa.g])
                    k.cp('pool', qb.t[:, 64:128], qfT.t[:, 64:128], r=[qfT.g], w=[qb.g])
                    yield
                    k.tr(TPB[:, 0, :], ktm.t[:], ident.t[:], r=[ktm.g, ident.g], w=[TPg[0]])
                    k.cp('act', kTs.t[:], TPB[:, 0, :], r=[TPg[0]], w=[kTs.g])
                    yield
                    k.mm(SM[0][:, 2, :], kTs.t[:], qfT.t[:], True, True, r=[kTs.g, qfT.g], w=[SMg[0][2]])
                    k.tt('dve', Am.t[:], SM[0][:, 2, :], ublk.t[:], ALU.mult, r=[SMg[0][2], ublk.g], w=[Am.g])
                    yield
                    k.mm(SM[1][:, 0, :], ktm.t[0:64, :], vhg[i].t[0:64, :], True, True, r=[ktm.g, vhg[i].g],
                         w=[SMg[1][0]])
                    k.mm(SM[1][:, 1, :], ktm.t[64:128, :], vhg[i].t[64:128, :], True, True, r=[ktm.g, vhg[i].g],
                         w=[SMg[1][1]])
                    yield
                    k.ts('dve', S1.t[:], Sf.t[:], ebt.t[:, 63:64], None, ALU.mult, None, r=[Sf.g, ebt.g], w=[S1.g])
                    k.stt('dve', Sf2.t[:], SM[1][:, 0, :], ebt.t[:, 63:64], S1.t[:], ALU.mult, ALU.add,
                          r=[SMg[1][0], ebt.g, S1.g], w=[Sf2.g])
                    yield
                    k.cp('pool', SBbf.t[:], Sf2.t[:], r=[Sf2.g], w=[SBbf.g])
                    k.ts('dve', S1.t[:], Sf2.t[:], ebt.t[:, 127:128], None, ALU.mult, None, r=[Sf2.g, ebt.g],
                         w=[S1.g])
                    k.stt('dve', Sf.t[:], SM[1][:, 1, :], ebt.t[:, 127:128], S1.t[:], ALU.mult, ALU.add,
                          r=[SMg[1][1], ebt.g, S1.g], w=[Sf.g])
                    k.cp('pool', SAbf[1 - p].t[:], Sf.t[:], r=[Sf.g], w=[SAbf[1 - p].g])
                    yield
                    k.mm(SM[0][:, 3, :], Am.t[:], vhg[i].t[:], True, False, r=[Am.g, vhg[i].g], w=[SMg[0][3]], inc=False)
                    k.mm(SM[0][:, 3, :], qa.t[:], SAbf[p].t[:], False, False, r=[qa.g, SAbf[p].g], w=[SMg[0][3]],
                         inc=False)
                    k.mm(SM[0][:, 3, :], qb.t[:], SBbf.t[:], False, True, r=[qb.g, SBbf.g], w=[SMg[0][3]])
                    k.act(junk.t[:, :], SM[0][:, 3, :], AF.Square, r=[SMg[0][3]], w=[junk.g, ssh.g],
                          accum_out=ssh.t[:])
                    k.ts('dve', rsh.t[:], ssh.t[:], 1.0 / 128, 1e-6, ALU.mult, ALU.add, r=[ssh.g], w=[rsh.g])
                    k.act(rsh.t[:], rsh.t[:], AF.Ln, r=[], w=[rsh.g])
                    k.act(rsh2.t[:], rsh.t[:], AF.Exp, r=[rsh.g], w=[rsh2.g], scale=-0.5)
                    k.tt('pool', hz.t[:], hnw.t[:], zs[i].t[:], ALU.mult, r=[hnw.g, zs[i].g], w=[hz.g])
                    k.stt('dve', og.t[:], SM[0][:, 3, :], rsh2.t[:, 0:1], hz.t[:], ALU.mult, ALU.mult,
                          r=[SMg[0][3], rsh2.g, hz.g], w=[og.g])
                    yield
                    k.tr(TPB[:, 1, :], og.t[:], ident.t[:], r=[og.g, ident.g], w=[TPg[1]])
                    k.cp('act', oxh.t[:, 128 * i:128 * (i + 1)], TPB[:, 1, :], r=[TPg[1]], w=[oxh.g])

                for i in range(4):
                    n = 4 * T + i
                    def cmp_topk(i):
                        n = 4 * T + i
                        qall = qT.t[:, i * 512:(i + 1) * 512]
                        nct = (8 * n + 6) // 128 + 1
                        for ct in range(nct):
                            if ct > 0:
                                yield
                            bg, bgg = nbg()
                            k.mm(bg[:, 0:512], kcT.t[:, 128 * ct:128 * (ct + 1)], qall, True, True, r=[kcT.g, qT.g],
                                 w=[bgg])
                            k.act(Ec[ct].t[:], bg[:, 0:512], AF.Exp, r=[bgg], w=[Ec[ct].g])
                            st = min(128 * n - 2048 * ct + 96, 2176)
                            for h in range(4):
                                k.tt('pool', Ecm[ct].t[:, h * 128:(h + 1) * 128], Ec[ct].t[:, h * 128:(h + 1) * 128],
                                     mtw.t[:, st:st + 128], ALU.mult, r=[Ec[ct].g, mtw.g], w=[Ecm[ct].g])
                        yield
                        for hh in range(2):
                            for ct in range(nct):
                                k.mm(RP[:, hh * 132:(hh + 1) * 132], Ecm[ct].t[:, (HL + hh) * 128:(HL + hh + 1) * 128], vc.t[:, ct, :],
                                     ct == 0, ct == nct - 1, r=[Ecm[ct].g, vc.g], w=[RPg])
                        k.cp('act', ocmp[i % 2].t[:, 0:264], RP[:, 0:264], r=[RPg], w=[ocmp[i % 2].g])
                        for pair in range(2):
                            yield
                            bg, bgg = nbg()
                            for h2 in range(2):
                                h = 2 * pair + h2
                                for ct in range(nct):
                                    k.mm(bg[:, h2 * 132:(h2 + 1) * 132], Ecm[ct].t[:, h * 128:(h + 1) * 128],
                                         ovx.t[:, ct, :], ct == 0, ct == nct - 1, r=[Ecm[ct].g, ovx.g], w=[bgg])
                            for h2 in range(2):
                                h = 2 * pair + h2
                                k.ts('dve', rz4.t[:, h:h + 1], bg[:, h2 * 132 + 128:h2 * 132 + 129], 1e-30, None, ALU.max,
                                     None, r=[bgg], w=[rz4.g])
                                k.rcp(rz4.t[:, h:h + 1], rz4.t[:, h:h + 1], r=[], w=[rz4.g])
                                if h == 0:
                                    k.ts('dve', imps.t[:], bg[:, 0:128], rz4.t[:, 0:1], None, ALU.mult, None,
                                         r=[bgg, rz4.g], w=[imps.g])
                                else:
                                    k.stt('dve', imps.t[:], bg[:, h2 * 132:h2 * 132 + 128], rz4.t[:, h:h + 1], imps.t[:],
                                          ALU.mult, ALU.add, r=[bgg, rz4.g], w=[imps.g])
                        yield
                        w0 = 128 - 2 * n
                        k.tt('dve', sc.t[:], imps.t[:], m1w.t[:, w0:w0 + 128], ALU.mult, r=[imps.g, m1w.g], w=[sc.g])
                        k.tt('dve', sc.t[:], sc.t[:], m2w.t[:, w0:w0 + 128], ALU.add, r=[m2w.g], w=[sc.g])
                        k.ms('dve', sc.t[:, 0:1], 3e30, w=[sc.g])
                        yield
                        S.op('dve', lambda e: e.max(out=m8.t[:], in_=sc.t[:]), r=[sc.g], w=[m8.g])
                        S.op('dve', lambda e: e.match_replace(out=sc2.t[:], in_to_replace=m8.t[:], in_values=sc.t[:],
                                                              imm_value=-2.0), r=[sc.g, m8.g], w=[sc2.g])
                        S.op('dve', lambda e: e.max(out=m8b.t[:], in_=sc2.t[:]), r=[sc2.g], w=[m8b.g])
                        k.ts('dve', thr.t[:], m8b.t[:, 7:8], 0.0, None, ALU.max, None, r=[m8b.g], w=[thr.g])
                        k.ts('dve', selb.t[:], sc.t[:], thr.t[:, 0:1], None, ALU.is_ge, None, r=[sc.g, thr.g], w=[selb.g])
                        yield
                        k.tr(TPB[:, 2, :], selb.t[:], ident.t[:], r=[selb.g, ident.g], w=[TPg[2]])
                        k.cp('act', selTs[i % 2].t[:], TPB[:, 2, :], r=[TPg[2]], w=[selTs[i % 2].g])
                    if i == 0:
                        for _ in cmp_topk(0):
                            pass
                    gens = [hgrn_tile(i)] + ([cmp_topk(i + 1)] if i < 3 else [])
                    selT = selTs[i % 2]
                    qown = qT.t[:, i * 512 + HL * 128:i * 512 + HL * 128 + 256]
                    jobs = []
                    for br in range(2):
                        kts = list(range(0, n + 1)) if br == 0 else list(range(max(0, n - 4), n + 1))
                        for p0 in range(0, len(kts), 2):
                            jobs.append((br, kts, kts[p0:p0 + 2]))

                    def stage_a(job, pi):
                        br, kts, pk = job
                        bg, bgg = nbg()
                        e_s = Es[es_c[0] % 3]
                        e_m = Esm[es_c[0] % 3]
                        es_c[0] += 1
                        mb = pi % 2
                        for j, kt in enumerate(pk):
                            if br == 0:
                                lhs = kslcT.t[:, 128 * kt:128 * (kt + 1)]
                                lg_ = ksg[kt // 4]
                            else:
                                o_ = (kt % 8) * 128
                                lhs = kwinT.t[:, o_:o_ + 128]
                                lg_ = kwg[(kt // 4) % 2]
                            k.mm(bg[:, j * 256:(j + 1) * 256], lhs, qown, True, True, r=[lg_, qT.g], w=[bgg],
                                 inc=(j == len(pk) - 1))
                        if br == 0:
                            for j, kt in enumerate(pk):
                                k.mm(SM[mb][:, 2 + j, :], bexp.t[:, 128 * kt:128 * (kt + 1)], selT.t[:], True, True,
                                     r=[bexp.g, selT.g], w=[SMg[mb][2 + j]])
                        k.act(e_s.t[:, 0:256 * len(pk)], bg[:, 0:256 * len(pk)], AF.Exp, r=[bgg], w=[e_s.g])
                        for j, kt in enumerate(pk):
                            if br == 0:
                                if kt == n:
                                    k.tt('dve', mkd[j].t[:], SM[mb][:, 2 + j, :], tri.t[:], ALU.mult,
                                         r=[SMg[mb][2 + j], tri.g], w=[mkd[j].g])
                                    mk, mkg = mkd[j].t[:], mkd[j].g
                                else:
                                    mk, mkg = SM[mb][:, 2 + j, :], SMg[mb][2 + j]
                            elif kt == n:
                                mk, mkg = tri.t[:], tri.g
                            elif kt == n - 4:
                                mk, mkg = trilo.t[:], trilo.g
                            else:
                                mk = None
                            for hh in range(2):
                                c0 = j * 256 + hh * 128
                                if mk is None:
                                    k.cp('pool', e_m.t[:, c0:c0 + 128], e_s.t[:, c0:c0 + 128], r=[e_s.g], w=[e_m.g])
                                else:
                                    k.tt('dve', e_m.t[:, c0:c0 + 128], e_s.t[:, c0:c0 + 128], mk, ALU.mult,
                                         r=[e_s.g, mkg], w=[e_m.g])
                        return e_m

                    def stage_b(job, e_m):
                        br, kts, pk = job
                        for j, kt in enumerate(pk):
                            for hh in range(2):
                                c0 = j * 256 + hh * 128
                                if br == 0:
                                    rhs, rg = vslc.t[:, kt, :], vsg[kt]
                                else:
                                    rhs, rg = vwin.t[:, kt % 8, :], vwg[kt % 8]
                                last = (j == len(pk) - 1 and hh == 1)
                                k.mm(OA[hh][:, br * 132:(br + 1) * 132], e_m.t[:, c0:c0 + 128], rhs,
                                     kt == kts[0], kt == kts[-1], r=[e_m.g, rg], w=[OAg[hh][br]],
                                     inc=(last or kt == kts[-1]))

                    prev = None
                    for pi, job in enumerate(jobs):
                        em = stage_a(job, pi)
                        if prev is not None:
                            stage_b(*prev)
                        prev = (job, em)
                        for g_ in gens:
                            next(g_, None)
                    stage_b(*prev)
                    for g_ in gens:
                        for _ in g_:
                            pass
                    for hh in range(2):
                        k.ts('dve', z3.t[:, 0:1], ocmp[i % 2].t[:, hh * 132 + 128:hh * 132 + 129], 1e-30, None, ALU.max,
                             None, r=[ocmp[i % 2].g], w=[z3.g])
                        for bi, c_ in ((1, 128), (2, 260)):
                            k.ts('dve', z3.t[:, bi:bi + 1], OA[hh][:, c_:c_ + 1], 1e-30, None, ALU.max, None,
                                 r=[OAg[hh][bi - 1]], w=[z3.g])
                        k.rcp(z3.t[:, :], z3.t[:, :], r=[], w=[z3.g])
                        k.tt('dve', a3.t[:, :], z3.t[:, :], gs[i].t[:, 3 * hh:3 * hh + 3], ALU.mult,
                             r=[z3.g, gs[i].g], w=[a3.g])
                        k.ts('dve', acc.t[:], ocmp[i % 2].t[:, hh * 132:hh * 132 + 128], a3.t[:, 0:1], None, ALU.mult, None,
                             r=[ocmp[i % 2].g, a3.g], w=[acc.g])
                        k.stt('dve', acc.t[:], OA[hh][:, 0:128], a3.t[:, 1:2], acc.t[:], ALU.mult, ALU.add,
                              r=[OAg[hh][0], a3.g], w=[acc.g])
                        k.stt('dve', acc.t[:], OA[hh][:, 132:260], a3.t[:, 2:3], acc.t[:], ALU.mult, ALU.add,
                              r=[OAg[hh][1], a3.g], w=[acc.g])
                        k.tt('dve', onb.t[:], acc.t[:], zn[i].t[:, hh * 128:(hh + 1) * 128], ALU.mult,
                             r=[acc.g, zn[i].g], w=[onb.g])
                        k.tr(TPB[:, 3 + hh, :], onb.t[:], ident.t[:], r=[onb.g, ident.g], w=[TPg[3 + hh]])
                        k.cp('act', oxn.t[:, hh, 128 * i:128 * (i + 1)], TPB[:, 3 + hh, :], r=[TPg[3 + hh]], w=[oxn.g])
                store_ox(0, T, oxh.t[:, :], oxh.g)
                store_ox(4, T, oxn.t[:, 0, :], oxn.g)
                store_ox(8, T, oxn.t[:, 1, :], oxn.g)
            S.barrier()
        esP.close()

        if FUSED:
            ccs = es.enter_context(nc.semaphore("ccsem"))
            S.barrier()
            S.q['pool'].append(lambda e: e.collective_compute(
                "AllGather", ALU.bypass, replica_groups=[list(range(8))],
                ins=[oxd_t.ap()], outs=[gath_t.ap()]).then_inc(ccs))
            S.q['pool'].append(lambda e: e.wait_ge(ccs, 1))
            gth = Buf(None)
            dmy = sb_outer("dmy", [128, 8], F32)
            k.ms('pool', dmy.t[:], 0.0, w=[dmy.g, gth.g])

            def load_oT(oT, psx, sbx):
                selm = sbx("selm_s", [128, 8, 128], BF16)
                k.dma(selm.t[:, :, :], P2["selm"], w=[selm.g])
                gts = [sbx(f"gt{j}", [128, 512], BF16) for j in range(4)]
                sacc = [psx(f"sacc{j}", [128, 512], F32) for j in range(2)]
                saccg = [PReg(), PReg()]
                cnt = 0
                gi = 0
                for rho in range(8):
                    for cl in range(3):
                        chunk = rho if cl == 0 else 8 + 2 * rho + (cl - 1)
                        for half in range(2):
                            ac, acg = sacc[cnt % 2], saccg[cnt % 2]
                            cnt += 1
                            for a0 in range(4):
                                a = 4 * cl + a0
                                for jh in range(2):
                                    gt = gts[gi % 4]
                                    gi += 1
                                    r0 = ((rho * 12 + a) * 8 + 4 * jh) * 32
                                    k.dma(gt.t[:, :], gath[r0:r0 + 128, 512 * half:512 * (half + 1)], r=[gth.g],
                                          w=[gt.g])
                                    first = (a0 == 0 and jh == 0)
                                    last = (a0 == 3 and jh == 1)
                                    k.mm(ac[:, :], selm.t[:, 2 * a0 + jh, :], gt.t[:, :], first, last,
                                         r=[selm.g, gt.g], w=[acg], inc=True)
                            k.cp('act' if cnt % 2 == 0 else 'dve', oT.t[:, chunk, 512 * half:512 * (half + 1)], ac[:, :],
                                 r=[acg], w=[oT.g])

            _phase2(nc, S, k, P2, normw, ident, load_oT)
        S.finish()
        S.emit()


HL = 0


def build_p2():
    nc = bass.Bass("TRN2", target_bir_lowering=False)

    def din(name, shape, dt=F32):
        return nc.dram_tensor(name, list(shape), dt, kind="ExternalInput").ap()

    P2 = dict(xo=din("xo", [1024, D]), wm=din("wm", [D, 4096]), wbh=din("wbh", [1024, D]), wbn=din("wbn", [D, D]),
              wo=din("wo", [D, D]), fnw=din("fnw", [128, D]),
              y=nc.dram_tensor("y", [1024, D], F32, kind="ExternalOutput").ap())
    oo = din("oo", [3072, 1024], BF16)
    normw_d = din("normw", [128, 16])
    ident_d = din("ident", [128, 128], BF16)
    with ExitStack() as es:
        S = Sched(nc, es)
        k = K(S)
        sb, ps = _mk(nc, es)
        normw = sb("normw_s", [128, 16], F32)
        ident = sb("ident_s", [128, 128], BF16)
        k.dma(normw.t[:], normw_d, w=[normw.g])
        k.dma(ident.t[:], ident_d, w=[ident.g])

        def load_oT(oT, psx, sbx):
            for c in range(24):
                k.dma(oT.t[:, c, :], oo[128 * c:128 * (c + 1), :], w=[oT.g])

        _phase2(nc, S, k, P2, normw, ident, load_oT)
        S.finish()
        S.emit()
    return nc


def _phase2(nc, S, k, P2, normw, ident, load_oT):
    xo, wm, wbh, wbn, wo, fnw_d, y = P2["xo"], P2["wm"], P2["wbh"], P2["wbn"], P2["wo"], P2["fnw"], P2["y"]
    with ExitStack() as es:
        sb, ps = _mk(nc, es)
        mT = sb("p2mT", [128, 16, 1024], BF16)
        mTg = [Reg() for _ in range(16)]
        junk = sb("junk2", [128, 2048], BF16)
        ss = sb("ss2", [128, 1], F32)
        rs = sb("rs_2", [128, 1], F32)
        rs2 = sb("rs2_2", [128, 1], F32)

        with ExitStack() as esA:
            sbA, psA = _mk(nc, esA)
            xnT = sbA("p2xnT", [128, 16, 1024], BF16)
            oT = sbA("p2oT", [128, 24, 1024], BF16)
            with ExitStack() as esL:
                sbL, psL = _mk(nc, esL)
                load_oT(oT, psL, sbL)
                S.barrier()
            with ExitStack() as esA1:
                sbA1, psA1 = _mk(nc, esA1)
                xts = [sbA1(f"p2xt{j}", [128, D], F32) for j in range(2)]
                xns = [sbA1(f"p2xn{j}", [128, D], BF16) for j in range(2)]
                tps = [psA1(f"p2tp{j}", [128, 8, 128], BF16) for j in range(2)]
                tpg = [PReg(), PReg()]
                for n in range(8):
                    xt, xn = xts[n % 2], xns[n % 2]
                    k.dma(xt.t[:], xo[128 * n:128 * (n + 1), :], w=[xt.g])
                    _norm_transpose(k, xt, xn, tps, tpg, xnT.t[:, :, 128 * n:128 * (n + 1)], xnT.g, ident, junk, ss,
                                    rs, rs2)
                S.barrier()
            with ExitStack() as esA2:
                sbA2, psA2 = _mk(nc, esA2)
                wst = [sbA2(f"p2wst{j}", [128, 56, 128], F32) for j in range(2)]
                wbf = [sbA2(f"wbf{j}", [128, 56, 128], BF16) for j in range(2)]
                g1s = sbA2("g1s", [128, 512], F32)
                g2s = sbA2("g2s", [128, 512], F32)
                acc = [psA2(f"p2acc{j}", [128, 512], F32) for j in range(8)]
                accg = [PReg() for _ in range(8)]
                ac = [0]
                for fo in range(16):
                    st, wb = wst[fo % 2], wbf[fo % 2]
                    cs_ = slice(128 * fo, 128 * (fo + 1))
                    k.dma(st.t[:, 0:16, :], wm[:, 128 * fo:128 * (fo + 1)].rearrange("(k p) c -> p k c", p=128), w=[st.g])
                    k.dma(st.t[:, 16:32, :], wm[:, 2048 + 128 * fo:2048 + 128 * (fo + 1)].rearrange("(k p) c -> p k c", p=128),
                          w=[st.g])
                    k.dma(st.t[:, 32:40, :], wbh[:, cs_].rearrange("(k p) c -> p k c", p=128), w=[st.g])
                    k.dma(st.t[:, 40:56, :], wbn[:, cs_].rearrange("(k p) c -> p k c", p=128), w=[st.g])
                    for kc in range(16):
                        k.ts('dve' if kc % 2 == 0 else 'pool', wb.t[:, kc, :], st.t[:, kc, :], normw.t[:, kc:kc + 1],
                             None, ALU.mult, None, r=[st.g, normw.g], w=[wb.g])
                        k.ts('pool' if kc % 2 == 0 else 'dve', wb.t[:, 16 + kc, :], st.t[:, 16 + kc, :],
                             normw.t[:, kc:kc + 1], None, ALU.mult, None, r=[st.g, normw.g], w=[wb.g])
                    k.cp('pool', wb.t[:, 32:56, :], st.t[:, 32:56, :], r=[st.g], w=[wb.g])
                    for half in range(2):
                        ts_ = slice(512 * half, 512 * (half + 1))
                        a4 = []
                        for j in range(4):
                            a4.append((acc[ac[0] % 8], accg[ac[0] % 8]))
                            ac[0] += 1
                        for kc in range(16):
                            k.mm(a4[0][0][:, :], wb.t[:, kc, :], xnT.t[:, kc, ts_], kc == 0, kc == 15,
                                 r=[wb.g, xnT.g], w=[a4[0][1]])
                        for kc in range(16):
                            k.mm(a4[1][0][:, :], wb.t[:, 16 + kc, :], xnT.t[:, kc, ts_], kc == 0, kc == 15,
                                 r=[wb.g, xnT.g], w=[a4[1][1]])
                        for kc in range(8):
                            k.mm(a4[2][0][:, :], wb.t[:, 32 + kc, :], oT.t[:, kc, ts_], kc == 0, kc == 7,
                                 r=[wb.g, oT.g], w=[a4[2][1]])
                        for kc in range(16):
                            k.mm(a4[3][0][:, :], wb.t[:, 40 + kc, :], oT.t[:, 8 + kc, ts_], kc == 0, kc == 15,
                                 r=[wb.g, oT.g], w=[a4[3][1]])
                        k.act(g1s.t[:], a4[0][0][:, :], AF.Sigmoid, r=[a4[0][1]], w=[g1s.g])
                        k.act(g2s.t[:], a4[1][0][:, :], AF.Sigmoid, r=[a4[1][1]], w=[g2s.g])
                        k.tt('dve', g1s.t[:], g1s.t[:], a4[2][0][:, :], ALU.mult, r=[a4[2][1]], w=[g1s.g])
                        k.tt('dve', g2s.t[:], g2s.t[:], a4[3][0][:, :], ALU.mult, r=[a4[3][1]], w=[g2s.g])
                        k.tt('pool', mT.t[:, fo, ts_], g1s.t[:], g2s.t[:], ALU.add, r=[g1s.g, g2s.g], w=[mTg[fo]])
                S.barrier()
            S.barrier()

        with ExitStack() as esB:
            sbB, psB = _mk(nc, esB)
            wob = sbB("wob", [128, 16, D], BF16)
            wog = [Reg() for _ in range(4)]
            wost = [sbB(f"wost{j}", [128, 4, D], F32) for j in range(2)]
            fnw = sbB("fnw_s", [128, D], F32)
            xts = [sbB(f"xt2{j}", [128, D], F32) for j in range(2)]
            hs = [sbB(f"hs{j}", [128, D], F32) for j in range(2)]
            acc = [psB(f"acc2{j}", [128, 512], F32) for j in range(4)]
            accg = [PReg() for _ in range(4)]
            k.dma(fnw.t[:], fnw_d, w=[fnw.g])
            for q4 in range(4):
                st = wost[q4 % 2]
                k.dma(st.t[:, :, :], wo[512 * q4:512 * (q4 + 1), :].rearrange("(k p) c -> p k c", p=128), w=[st.g])
                k.cp('dve' if q4 % 2 == 0 else 'pool', wob.t[:, 4 * q4:4 * q4 + 4, :], st.t[:, :, :], r=[st.g],
                     w=[wog[q4]])
            for n in range(8):
                xt, h_ = xts[n % 2], hs[n % 2]
                k.dma(xt.t[:], xo[128 * n:128 * (n + 1), :], w=[xt.g])
                for cg in range(4):
                    for fo in range(16):
                        k.mm(acc[cg][:, :], mT.t[:, fo, 128 * n:128 * (n + 1)], wob.t[:, fo, 512 * cg:512 * (cg + 1)],
                             fo == 0, fo == 15, r=[mTg[fo], wog[fo // 4]], w=[accg[cg]])
                    k.tt('dve', h_.t[:, 512 * cg:512 * (cg + 1)], acc[cg][:, :], xt.t[:, 512 * cg:512 * (cg + 1)],
                         ALU.add, r=[accg[cg], xt.g], w=[h_.g])
                k.act(junk.t[:], h_.t[:], AF.Square, r=[h_.g], w=[junk.g, ss.g], accum_out=ss.t[:])
                k.ts('dve', rs.t[:], ss.t[:], 1.0 / D, 1e-6, ALU.mult, ALU.add, r=[ss.g], w=[rs.g])
                k.act(rs.t[:], rs.t[:], AF.Sqrt, r=[], w=[rs.g])
                k.rcp(rs2.t[:], rs.t[:], r=[rs.g], w=[rs2.g])
                k.stt('dve', h_.t[:], h_.t[:], rs2.t[:, 0:1], fnw.t[:], ALU.mult, ALU.mult, r=[rs2.g, fnw.g],
                      w=[h_.g])
                k.dma(y[128 * n:128 * (n + 1), :], h_.t[:], r=[h_.g])
            S.barrier()


def _consts():
    c = {}
    c["ident"] = np.eye(128, dtype=np.float32).astype(NBF)
    s_ = np.arange(128)
    c["ublk"] = ((s_[:, None] // 64 == s_[None, :] // 64) & (s_[:, None] <= s_[None, :])).astype(np.float32)
    c["tri"] = (s_[:, None] <= s_[None, :]).astype(np.float32).astype(NBF)
    c["trilo"] = (s_[:, None] > s_[None, :]).astype(np.float32).astype(NBF)
    r = np.zeros((128, 32), np.float32)
    for m in range(16):
        r[m + 16, m] = -1.0
        r[m, m + 16] = 1.0
    c["rmat"] = r.astype(NBF)
    cc = np.arange(8192)
    c["bexp"] = (cc[None, :] // 64 == s_[:, None]).astype(np.float32).astype(NBF)
    ov = np.zeros((128, 4, 132), np.float32)
    ci = (np.arange(4)[None, :] * 128 + s_[:, None]) * 16
    sj = np.arange(128) * 64
    ovl = (ci[:, :, None] < sj[None, None, :] + 64) & (ci[:, :, None] + 32 > sj[None, None, :])
    ov[:, :, 0:128] = ovl
    ov[:, :, 128] = 1.0
    ov[127, 3, :] = 0.0
    c["ovx"] = ov.astype(NBF)
    v = np.arange(2304)[None, :] - 96
    c["mtw"] = ((16 * s_[:, None] + 31) <= v).astype(np.float32).astype(NBF)
    d_ = np.arange(256)[None, :] - 128
    jt = (s_[:, None] >= 64).astype(np.int64)
    c["m1w"] = (d_ <= jt - 2).astype(np.float32)
    m2 = np.zeros((128, 256), np.float32)
    m2[np.broadcast_to(d_ == jt, (128, 256))] = 2e30
    m2[np.broadcast_to(d_ == jt - 1, (128, 256))] = 1e30
    m2[np.broadcast_to(d_ > jt, (128, 256))] = -1.0
    c["m2w"] = m2
    p_ = np.arange(128)
    c["postab"] = (2048 * (p_[:, None] // 32) + np.arange(2048)[None, :]).astype(np.float32)
    inv = (np.float32(500000.0) ** (-np.arange(0, 32, 2, dtype=np.float32) / np.float32(32))).astype(np.float32)
    c["invf"] = inv[(p_ % 32) % 16].reshape(128, 1).astype(np.float32)
    return c


_NC_CACHE = {}
_DEBUG_MAPS = None
FUSED_MODE = True


def kernel(x, norm_w, w_in, hg_lb_logits, hg_norm_w, cmp_k_pos, cmp_k_w1, cmp_k_b1, cmp_k_w2,
           cmp_v_pos, cmp_v_w1, cmp_v_b1, cmp_v_w2, w_branch_hg, w_branch_nsa, w_out, final_norm_w):
    global HL
    f32 = lambda a: np.ascontiguousarray(np.asarray(a, dtype=np.float32))
    x2 = f32(x).reshape(S_LEN, D)
    w = f32(w_in)[0]
    nw = f32(norm_w)[0]
    normw = np.ascontiguousarray(nw.reshape(16, 128).T)
    cst = _consts()
    lbl = f32(hg_lb_logits)
    hnw = f32(hg_norm_w)[0]
    cmpw = {}
    for kv, (pos, w1, b1, w2) in (("k", (cmp_k_pos, cmp_k_w1, cmp_k_b1, cmp_k_w2)),
                                  ("v", (cmp_v_pos, cmp_v_w1, cmp_v_b1, cmp_v_w2))):
        cmpw[f"c{kv}_posT"] = np.ascontiguousarray(f32(pos)[0].T)
        cmpw[f"c{kv}_w1"] = f32(w1)[0]
        cmpw[f"c{kv}_b1"] = np.ascontiguousarray(f32(b1)[0].reshape(4, 128).T)
        cmpw[f"c{kv}_w2"] = f32(w2)[0]

    HL = 0
    in_maps = []
    for c in range(8):
        g = c // 2
        hl = (c % 2) * 2
        order = [hl, hl + 1] + [h for h in range(4) if h not in (hl, hl + 1)]
        cols = []
        cols += list(range(0 + 128 * c, 128 * c + 128))
        for h in order:
            cols += list(range(4096 + 128 * (4 * g + h), 4096 + 128 * (4 * g + h) + 128))
        cols += list(range(7168 + 128 * g, 7168 + 128 * g + 128))
        cols += list(range(8192 + 128 * g, 8192 + 128 * g + 128))
        cols += list(range(1024 + 128 * c, 1024 + 128 * c + 128))
        cols += list(range(2048 + 128 * c, 2048 + 128 * c + 128))
        cols += list(range(3072 + 128 * c, 3072 + 128 * c + 128))
        cols += list(range(7680 + 128 * g, 7680 + 128 * g + 128))
        cols += list(range(8704 + 128 * g, 8704 + 128 * g + 128))
        cols += list(range(9264 + 128 * (4 * g + hl), 9264 + 128 * (4 * g + hl) + 256))
        gc = 9216 + 12 * g + 3 * hl
        cols += list(range(gc, gc + 6)) + [gc] * 10
        wcore = np.ascontiguousarray(w[:, cols])
        ccols = list(range(6144 + 128 * g, 6144 + 128 * g + 128)) + list(range(6656 + 128 * g, 6656 + 128 * g + 128))
        m = dict(x=x2, wc=wcore, wcc=np.ascontiguousarray(w[:, ccols]), normw=normw,
                 lb0=np.ascontiguousarray(np.broadcast_to(lbl[0, 128 * c:128 * (c + 1)], (128, 128))),
                 lb1=np.ascontiguousarray(np.broadcast_to(lbl[1, 128 * c:128 * (c + 1)], (128, 128))),
                 hnw=np.ascontiguousarray(np.broadcast_to(hnw, (128, 128))))
        m.update(cmpw)
        m.update(cst)
        in_maps.append(m)
    wmg = np.ascontiguousarray(w[:, 11312:11312 + 4096])
    fnw = np.ascontiguousarray(np.broadcast_to(f32(final_norm_w), (128, D)))
    wbh_, wbn_, wo_ = f32(w_branch_hg)[0], f32(w_branch_nsa)[0], f32(w_out)[0]
    if FUSED_MODE:
        for r in range(8):
            sel = np.zeros((128, 8, 128), np.float32)
            for a0 in range(4):
                for jh in range(2):
                    if r // 4 == jh:
                        j4 = r % 4
                        for f in range(32):
                            sel[j4 * 32 + f, 2 * a0 + jh, 32 * a0 + f] = 1.0
            in_maps[r].update(xo=np.ascontiguousarray(x2[1024 * r:1024 * (r + 1)]), wm=wmg, wbh=wbh_, wbn=wbn_,
                              wo=wo_, fnw=fnw, selm=sel.astype(NBF))
        if _DEBUG_MAPS is not None:
            _DEBUG_MAPS.append(in_maps)
            return None
        if "pf" not in _NC_CACHE:
            _NC_CACHE["pf"] = build_p1(FUSED=True)
        res = run_bass_kernel_spmd(_NC_CACHE["pf"], in_maps, core_ids=list(range(8)))
        out = np.concatenate([np.asarray(res.results[r]["y"]) for r in range(8)], axis=0)
        return out.reshape(1, S_LEN, D).astype(np.float32)
    if _DEBUG_MAPS is not None:
        _DEBUG_MAPS.append(in_maps)
        return None
    if "p1" not in _NC_CACHE:
        _NC_CACHE["p1"] = build_p1()
    nc1 = _NC_CACHE["p1"]
    res1 = run_bass_kernel_spmd(nc1, in_maps, core_ids=list(range(8)))
    oxs = [np.asarray(res1.results[c]["ox"]) for c in range(8)]

    full = np.concatenate([o[0:128] for o in oxs] + [o[128:384] for o in oxs], axis=0)
    if "p2" not in _NC_CACHE:
        _NC_CACHE["p2"] = build_p2()
    nc2 = _NC_CACHE["p2"]
    in2 = []
    for r in range(8):
        in2.append(dict(xo=np.ascontiguousarray(x2[1024 * r:1024 * (r + 1)]),
                        oo=np.ascontiguousarray(full[:, 1024 * r:1024 * (r + 1)]),
                        wm=wmg, wbh=wbh_, wbn=wbn_, wo=wo_,
                        normw=normw, fnw=fnw, ident=cst["ident"]))
    res2 = run_bass_kernel_spmd(nc2, in2, core_ids=list(range(8)))
    out = np.concatenate([np.asarray(res2.results[r]["y"]) for r in range(8)], axis=0)
    return out.reshape(1, S_LEN, D).astype(np.float32)
```

```python
import numpy as np
import ml_dtypes
from contextlib import ExitStack
import concourse.bass as bass
import concourse.mybir as mybir
from concourse.bass_utils import run_bass_kernel_spmd

F32 = mybir.dt.float32
BF16 = mybir.dt.bfloat16
AF = mybir.ActivationFunctionType
ALU = mybir.AluOpType
ENG = ('pe', 'act', 'dve', 'pool', 'sp')
NBF = ml_dtypes.bfloat16

S_LEN = 8192
D = 2048
NCOLS = 1808
FM0 = 0
TM0 = 896


class Reg:
    __slots__ = ('w', 'rs', 'excl')

    def __init__(s, excl=False):
        s.w = None
        s.rs = {}
        s.excl = excl


def PReg():
    return Reg(True)


class Buf:
    def __init__(s, t):
        s.t = t
        s.g = Reg()


class Sched:
    EP = 4000
    NDMA = 24

    def __init__(s, nc, es):
        s.nc = nc
        s.es = es
        s.q = {k: [] for k in ENG}
        s.cnt = {k: 0 for k in ENG}
        s.waited = {k: {} for k in ENG}
        s.csem = {k: [] for k in ENG}
        s.dsem = [es.enter_context(nc.semaphore(f"d{j}")) for j in range(s.NDMA)]
        s.dcnt = [0] * s.NDMA
        s.dn = 0

    def _csem(s, eng, ep):
        while len(s.csem[eng]) <= ep:
            s.csem[eng].append(s.es.enter_context(s.nc.semaphore(f"c_{eng}_{len(s.csem[eng])}")))
        return s.csem[eng][ep]

    def _wait(s, eng, evs):
        need = {}
        for ev in evs:
            if ev is None:
                continue
            if ev[0] == 'c':
                _, e2, idx = ev
                if e2 == eng and idx > s.cnt[eng]:
                    continue
                ep = (idx - 1) // s.EP
                val = (idx - 1) % s.EP + 1
                w = s.waited[eng].get(('c', e2), (-1, 0))
                if w[0] > ep or (w[0] == ep and w[1] >= val):
                    continue
                key = ('c', e2)
                cur = need.get(key)
                if cur is None or (ep, val) > (cur[0], cur[1]):
                    need[key] = (ep, val)
            else:
                _, j, m = ev
                if s.waited[eng].get(('d', j), (0, 0))[1] >= m:
                    continue
                key = ('d', j)
                cur = need.get(key)
                if cur is None or m > cur[1]:
                    need[key] = (0, m)
        for key, (ep, val) in need.items():
            s.waited[eng][key] = (ep, val)
            if key[0] == 'c':
                sem = s._csem(key[1], ep)
                v = val
            else:
                sem = s.dsem[key[1]]
                v = 16 * val
            s.q[eng].append(lambda e, sem=sem, v=v: e.wait_ge(sem, v))

    @staticmethod
    def _deps(r, w):
        evs = []
        for x in r:
            evs.append(x.w)
        for x in w:
            evs.append(x.w)
            evs.extend(x.rs.values())
        return evs

    @staticmethod
    def _upd(ev, key, r, w):
        for x in r:
            x.rs[key] = ev
        for x in w:
            x.w = ev
            x.rs = {}

    def op(s, eng, fn, r=(), w=(), inc=True):
        if any(x.excl for x in r):
            w = list(w) + [x for x in r if x.excl]
            r = [x for x in r if not x.excl]
        s._wait(eng, s._deps(r, w))
        idx = s.cnt[eng] + 1
        ev = ('c', eng, idx)
        if inc:
            s.cnt[eng] = idx
            sem = s._csem(eng, (idx - 1) // s.EP)
            s.q[eng].append(lambda e, fn=fn, sem=sem: fn(e).then_inc(sem, 1))
        else:
            s.q[eng].append(lambda e, fn=fn: fn(e))
        s._upd(ev, ('c', eng), r, w)
        return ev

    def dma(s, eng, out, in_, r=(), w=()):
        j = s.dn % s.NDMA
        prev = [('d', j, s.dcnt[j])] if s.dcnt[j] > 0 else []
        s._wait(eng, s._deps(r, w) + prev)
        s.dn += 1
        s.dcnt[j] += 1
        ev = ('d', j, s.dcnt[j])
        sem = s.dsem[j]
        s.q[eng].append(lambda e, out=out, in_=in_, sem=sem: e.dma_start(out=out, in_=in_).then_inc(sem, 16))
        s._upd(ev, ('d', j), r, w)
        return ev

    def barrier(s):
        evs = [('c', e2, s.cnt[e2]) for e2 in ENG if s.cnt[e2] > 0]
        evs += [('d', j, s.dcnt[j]) for j in range(s.NDMA) if s.dcnt[j] > 0]
        for e_ in ENG:
            s._wait(e_, evs)

    def finish(s):
        evs = [('d', j, s.dcnt[j]) for j in range(s.NDMA) if s.dcnt[j] > 0]
        s._wait('sp', evs)

    def emit(s):
        nc = s.nc
        with nc.Block() as block:
            @block.sync
            def _(e):
                for t in s.q['sp']:
                    t(e)

            @block.tensor
            def _(e):
                for t in s.q['pe']:
                    t(e)

            @block.scalar
            def _(e):
                for t in s.q['act']:
                    t(e)

            @block.vector
            def _(e):
                for t in s.q['dve']:
                    t(e)

            @block.gpsimd
            def _(e):
                for t in s.q['pool']:
                    t(e)


class K:
    def __init__(s, S):
        s.S = S

    def act(s, out, in_, func, r, w, **kw):
        s.S.op('act', lambda e: e.activation(out, in_, func, **kw), r=r, w=w)

    def ts(s, eng, out, in0, s1, s2, op0, op1, r, w, **kw):
        if op1 is None:
            s.S.op(eng, lambda e: e.tensor_scalar(out, in0, s1, s2, op0, **kw), r=r, w=w)
        else:
            s.S.op(eng, lambda e: e.tensor_scalar(out, in0, s1, s2, op0, op1, **kw), r=r, w=w)

    def tt(s, eng, out, in0, in1, op, r, w):
        s.S.op(eng, lambda e: e.tensor_tensor(out, in0, in1, op), r=r, w=w)

    def stt(s, eng, out, in0, sc, in1, op0, op1, r, w):
        s.S.op(eng, lambda e: e.scalar_tensor_tensor(out, in0, sc, in1, op0, op1), r=r, w=w)

    def cp(s, eng, out, in_, r, w):
        if eng == 'act':
            s.S.op('act', lambda e: e.activation(out, in_, AF.Copy), r=r, w=w)
        else:
            s.S.op(eng, lambda e: e.tensor_copy(out, in_), r=r, w=w)

    def ms(s, eng, ap, val, w):
        s.S.op(eng, lambda e: e.memset(ap, val), r=(), w=w)

    def mm(s, out, lhsT, rhs, start, stop, r, w, inc=None):
        if inc is None:
            inc = stop
        s.S.op('pe', lambda e: e.matmul(out, lhsT, rhs, start=start, stop=stop), r=r, w=w, inc=inc)

    def tr(s, out, in_, ident, r, w, inc=True):
        s.S.op('pe', lambda e: e.transpose(out, in_, ident), r=r, w=w, inc=inc)

    def rcp(s, out, in_, r, w):
        s.S.op('dve', lambda e: e.reciprocal(out, in_), r=r, w=w)

    def dma(s, out, in_, r=(), w=(), eng='sp'):
        s.S.dma(eng, out, in_, r=r, w=w)


def _mk(nc, es):
    def sb(name, shape, dt):
        return Buf(es.enter_context(nc.sbuf_tensor(name, list(shape), dt)))

    def ps(name, shape, dt):
        return es.enter_context(nc.psum_tensor(name, list(shape), dt))
    return sb, ps


def _norm_transpose(k, xt, xn, tps, tpg, xnT, xnTg, ident, junk, ss, rs, rs2):
    k.act(junk.t[:], xt.t[:], AF.Square, r=[xt.g], w=[junk.g, ss.g], accum_out=ss.t[:])
    k.ts('dve', rs.t[:], ss.t[:], 1.0 / D, 1e-6, ALU.mult, ALU.add, r=[ss.g], w=[rs.g])
    k.act(rs.t[:], rs.t[:], AF.Sqrt, r=[], w=[rs.g])
    k.rcp(rs2.t[:], rs.t[:], r=[rs.g], w=[rs2.g])
    k.ts('dve', xn.t[:], xt.t[:], rs2.t[:, 0:1], None, ALU.mult, None, r=[xt.g, rs2.g], w=[xn.g])
    for half in range(2):
        for kk in range(8):
            kc = half * 8 + kk
            k.tr(tps[half][:, kk, :], xn.t[:, kc * 128:(kc + 1) * 128], ident.t[:], r=[xn.g, ident.g],
                 w=[tpg[half]], inc=(kk == 7))
        k.cp('act' if half == 0 else 'dve', xnT[:, half * 8:(half + 1) * 8, :], tps[half][:, :, :],
             r=[tpg[half]], w=[xnTg])


class _Stop(Exception):
    pass


def build_p1(NTILES=64, STOP=None, FUSED=False):
    nc = bass.Bass("TRN2", target_bir_lowering=False)
    try:
        _build_p1(nc, NTILES, STOP, FUSED)
    except _Stop:
        pass
    return nc


def _build_p1(nc, NTILES, STOP, FUSED):
    NMT = NTILES // 4

    def din(name, shape, dt=F32):
        return nc.dram_tensor(name, list(shape), dt, kind="ExternalInput").ap()

    x = din("x", [S_LEN, D])
    wc = din("wc", [D, NCOLS])
    wcc = din("wcc", [D, 256])
    normw_d = din("normw", [128, 16])
    lb0_d = din("lb0", [128, 128])
    lb1_d = din("lb1", [128, 128])
    hnw_d = din("hnw", [128, 128])
    cw = {}
    for kv in "kv":
        cw[kv] = dict(pos=din(f"c{kv}_posT", [128, 32]), w1=din(f"c{kv}_w1", [4096, 512]),
                      b1=din(f"c{kv}_b1", [128, 4]), w2=din(f"c{kv}_w2", [512, 128]))
    ident_d = din("ident", [128, 128], BF16)
    ublk_d = din("ublk", [128, 128])
    tri_d = din("tri", [128, 128], BF16)
    trilo_d = din("trilo", [128, 128], BF16)
    rmat_d = din("rmat", [128, 32], BF16)
    bexp_d = din("bexp", [128, 8192], BF16)
    ovx_d = din("ovx", [128, 4, 132], BF16)
    mtw_d = din("mtw", [128, 2304], BF16)
    m1w_d = din("m1w", [128, 256])
    m2w_d = din("m2w", [128, 256])
    post_d = din("postab", [128, 2048])
    invf_d = din("invf", [128, 1])
    if FUSED:
        oxd_t = nc.dram_tensor("oxd", [12 * 8 * 32, 1024], BF16)
        gath_t = nc.dram_tensor("gath", [8 * 12 * 8 * 32, 1024], BF16)
        oxd = oxd_t.ap()
        gath = gath_t.ap()
        P2 = dict(xo=din("xo", [1024, D]), wm=din("wm", [D, 4096]), wbh=din("wbh", [1024, D]),
                  wbn=din("wbn", [D, D]), wo=din("wo", [D, D]), fnw=din("fnw", [128, D]),
                  selm=din("selm", [128, 8, 128], BF16),
                  y=nc.dram_tensor("y", [1024, D], F32, kind="ExternalOutput").ap())
    else:
        ox = nc.dram_tensor("ox", [384, S_LEN], BF16, kind="ExternalOutput").ap()
    xnT_d = nc.dram_tensor("xnT_d", [64, 128, 2048], BF16).ap()

    def store_ox(a_base, T, src, srcg):
        if not FUSED:
            k_[0].dma(ox[32 * a_base:32 * a_base + 128, 512 * T:512 * (T + 1)], src, r=[srcg])
            return
        j = T // 2
        c0 = (T % 2) * 512
        for a0 in range(4):
            r0 = ((a_base + a0) * 8 + j) * 32
            k_[0].dma(oxd[r0:r0 + 32, c0:c0 + 512], src[32 * a0:32 * a0 + 32], r=[srcg])

    k_ = [None]

    with ExitStack() as es:
        S = Sched(nc, es)
        k = K(S)
        k_[0] = k
        sb_outer, _ = _mk(nc, es)
        esP = ExitStack()
        sb, ps = _mk(nc, esP)

        def stop_here(tag):
            if STOP == tag:
                S.barrier()
                S.finish()
                S.emit()
                raise _Stop()

        normw = sb_outer("normw_s", [128, 16], F32)
        ident = sb_outer("ident_s", [128, 128], BF16)
        Wb = sb("Wb", [128, 16, NCOLS], BF16)
        ublk = sb("ublk_s", [128, 128], F32)
        tri = sb("tri_s", [128, 128], BF16)
        trilo = sb("trilo_s", [128, 128], BF16)
        rmat = sb("rmat_s", [128, 32], BF16)
        costab = sb("costab", [128, 2048], F32)
        sintab = sb("sintab", [128, 2048], F32)
        kcT = sb("kcT", [128, 512], BF16)
        vc = sb("vc", [128, 4, 132], BF16)
        junk = sb("junk", [128, 128], BF16)
        ss = sb("ss", [128, 1], F32)
        rs = sb("rs", [128, 1], F32)
        rs2 = sb("rs2", [128, 1], F32)
        rt1 = sb("rt1", [32, 512], F32)
        rt2 = sb("rt2", [32, 512], F32)

        for b_, d_ in ((normw, normw_d), (ident, ident_d), (ublk, ublk_d), (tri, tri_d), (trilo, trilo_d),
                       (rmat, rmat_d)):
            k.dma(b_.t[:], d_, w=[b_.g])

        with ExitStack() as es0:
            sb0, _ = _mk(nc, es0)
            post = sb0("post", [128, 2048], F32)
            invf = sb0("invf_s", [128, 1], F32)
            u = sb0("ang_u", [128, 2048], F32)
            u2 = sb0("ang_u2", [128, 2048], F32)
            k.dma(post.t[:], post_d, w=[post.g])
            k.dma(invf.t[:], invf_d, w=[invf.g])
            k.ts('dve', u.t[:], post.t[:], invf.t[:, 0:1], 1.0 / (2 * np.pi), ALU.mult, ALU.mult,
                 r=[post.g, invf.g], w=[u.g])
            SC = 2 * np.pi * (1 - 1e-6)
            BI = -np.pi * (1 - 1e-6)
            ui = sb0("ang_i", [128, 2048], mybir.dt.int32)
            m1 = sb0("ang_m1", [128, 2048], F32)

            def table(dst, shift):
                if shift:
                    k.ts('dve', u2.t[:], u.t[:], shift, None, ALU.add, None, r=[u.g], w=[u2.g])
                    src = u2
                else:
                    src = u
                k.cp('dve', ui.t[:], src.t[:], r=[src.g], w=[ui.g])
                k.cp('dve', m1.t[:], ui.t[:], r=[ui.g], w=[m1.g])
                k.tt('dve', u2.t[:], src.t[:], m1.t[:], ALU.subtract, r=[src.g, m1.g], w=[u2.g])
                k.ts('dve', m1.t[:], u2.t[:], 0.5, None, ALU.is_gt, None, r=[u2.g], w=[m1.g])
                k.tt('dve', u2.t[:], u2.t[:], m1.t[:], ALU.subtract, r=[m1.g], w=[u2.g])
                k.ts('dve', m1.t[:], u2.t[:], -0.5, None, ALU.is_lt, None, r=[u2.g], w=[m1.g])
                k.tt('dve', u2.t[:], u2.t[:], m1.t[:], ALU.add, r=[m1.g], w=[u2.g])
                k.act(dst.t[:], u2.t[:], AF.Sin, r=[u2.g], w=[dst.g], scale=SC)

            table(sintab, 0.0)
            table(costab, 0.25)
            S.barrier()
            stop_here('tables')

        def rope(X, Xg, N, cs, csg, rp, rpg):
            k.mm(rp[0:32, 0:N], rmat.t[:, :], X, True, True, r=[Xg, rmat.g], w=[rpg])
            k.tt('dve', rt1.t[0:32, 0:N], X[0:32], cs[0:32, 0, 0:N], ALU.mult, r=[Xg, csg], w=[rt1.g])
            k.tt('dve', rt2.t[0:32, 0:N], rp[0:32, 0:N], cs[0:32, 1, 0:N], ALU.mult, r=[rpg, csg], w=[rt2.g])
            k.tt('pool', X[0:32], rt1.t[0:32, 0:N], rt2.t[0:32, 0:N], ALU.add, r=[rt1.g, rt2.g], w=[Xg])

        def load_cs(cs, tok0, N):
            a = tok0 // 2048
            off = tok0 % 2048
            k.dma(cs.t[0:32, 0, 0:N], costab.t[32 * a:32 * a + 32, off:off + N], r=[costab.g], w=[cs.g])
            k.dma(cs.t[0:32, 1, 0:N], sintab.t[32 * a:32 * a + 32, off:off + N], r=[sintab.g], w=[cs.g])

        with ExitStack() as es1:
            sb1, _ = _mk(nc, es1)
            wst = [sb1(f"wst{j}", [128, NCOLS], F32) for j in range(2)]
            for kc in range(16):
                st = wst[kc % 2]
                k.dma(st.t[:], wc[kc * 128:(kc + 1) * 128, :], w=[st.g])
                k.ts('dve' if kc % 2 == 0 else 'pool', Wb.t[:, kc, :], st.t[:], normw.t[:, kc:kc + 1], None,
                     ALU.mult, None, r=[st.g, normw.g], w=[Wb.g])
            S.barrier()
            stop_here('weights')

        with ExitStack() as esA:
            sbA, psA = _mk(nc, esA)
            kcmpT = sbA("kcmpT", [128, S_LEN], BF16)
            vcmpT = sbA("vcmpT", [128, S_LEN], BF16)
            if NTILES < 64:
                k.ms('pool', kcmpT.t[:], 0.0, w=[kcmpT.g])
                k.ms('pool', vcmpT.t[:], 0.0, w=[vcmpT.g])
            WbA = sbA("WbA", [128, 16, 256], BF16)
            junkA = sbA("junkA", [128, 2048], BF16)
            with ExitStack() as es1b:
                sb1b, _ = _mk(nc, es1b)
                wstc = [sb1b(f"wstc{j}", [128, 256], F32) for j in range(2)]
                for kc in range(16):
                    st = wstc[kc % 2]
                    k.dma(st.t[:], wcc[kc * 128:(kc + 1) * 128, :], w=[st.g])
                    k.ts('dve' if kc % 2 == 0 else 'pool', WbA.t[:, kc, :], st.t[:], normw.t[:, kc:kc + 1], None,
                         ALU.mult, None, r=[st.g, normw.g], w=[WbA.g])
                S.barrier()
            with ExitStack() as esA1:
                sbA1, psA1 = _mk(nc, esA1)
                xts = [sbA1(f"xt{j}", [128, D], F32) for j in range(2)]
                xns = [sbA1(f"xn{j}", [128, D], BF16) for j in range(2)]
                xnTs = [sbA1(f"xnTa{j}", [128, 16, 512], BF16) for j in range(2)]
                xnTg = [[Reg() for _ in range(4)] for _ in range(2)]
                css = [sbA1(f"csA{j}", [32, 2, 512], F32) for j in range(2)]
                tps = [psA1(f"tpA{j}", [128, 8, 128], BF16) for j in range(2)]
                tpg = [PReg(), PReg()]
                cas = [psA1(f"caA{j}", [128, 512], F32) for j in range(2)]
                cag = [PReg(), PReg()]
                rp = psA1("rpA", [128, 512], F32)
                rpg = PReg()
                for T in range(NMT):
                    xnT, cs = xnTs[T % 2], css[T % 2]
                    load_cs(cs, 512 * T, 512)
                    for i in range(4):
                        n = 4 * T + i
                        xt, xn = xts[n % 2], xns[n % 2]
                        k.dma(xt.t[:], x[128 * n:128 * (n + 1), :], w=[xt.g])
                        _norm_transpose(k, xt, xn, [tps[0], tps[1]], tpg, xnT.t[:, :, 128 * i:128 * (i + 1)],
                                        xnTg[T % 2][i], ident, junkA, ss, rs, rs2)
                        k.dma(xnT_d[n].rearrange("p (k t) -> p k t", k=16), xnT.t[:, :, 128 * i:128 * (i + 1)],
                              r=[xnTg[T % 2][i]], w=[])
                    for ci in range(2):
                        for kc in range(16):
                            k.mm(cas[ci][:, :], WbA.t[:, kc, ci * 128:(ci + 1) * 128], xnT.t[:, kc, :], kc == 0,
                                 kc == 15, r=[WbA.g] + xnTg[T % 2], w=[cag[ci]])
                    k.cp('act', kcmpT.t[:, 512 * T:512 * (T + 1)], cas[0][:, :], r=[cag[0]], w=[kcmpT.g])
                    rope(kcmpT.t[:, 512 * T:512 * (T + 1)], kcmpT.g, 512, cs.t, cs.g, rp, rpg)
                    k.cp('dve', vcmpT.t[:, 512 * T:512 * (T + 1)], cas[1][:, :], r=[cag[1]], w=[vcmpT.g])
                S.barrier()
                stop_here('passA')

            with ExitStack() as esM:
                sbM, psM = _mk(nc, esM)
                w1st = [sbM(f"w1st{j}", [128, 4, 512], F32) for j in range(2)]
                w1b = [sbM(f"w1b{j}", [128, 4, 512], BF16) for j in range(2)]
                hacc = [psM(f"hacc{j}", [128, 512], F32) for j in range(4)]
                hag = [PReg() for _ in range(4)]
                pbs = [psM(f"pbias{j}", [128, 512], F32) for j in range(4)]
                pbg = [PReg() for _ in range(4)]
                posf = sbM("posf", [128, 32], F32)
                posb = sbM("posb", [128, 32], BF16)
                b1s = sbM("b1s", [128, 4], F32)
                btot = sbM("btot", [128, 4], F32)
                w2f = sbM("w2f", [128, 4, 128], F32)
                w2b = sbM("w2b", [128, 4, 128], BF16)
                hb = sbM("hb", [128, 512], F32)
                t1 = sbM("mt1", [128, 512], F32)
                t2 = sbM("mt2", [128, 512], F32)
                hT = sbM("hT", [128, 4, 512], BF16)
                k.ms('pool', hT.t[:, :, :], 0.0, w=[hT.g])
                k.ms('pool', kcT.t[:, :], 0.0, w=[kcT.g])
                k.ms('pool', vc.t[:, :, :], 0.0, w=[vc.g])
                k.ms('pool', vc.t[:, :, 128:129], 1.0, w=[vc.g])
                lgc = 0
                for kv in "kv":
                    src = kcmpT if kv == "k" else vcmpT
                    src3 = src.t[:, :].rearrange("p (c s) -> p c s", s=16)
                    W = cw[kv]
                    k.dma(posf.t[:], W["pos"], w=[posf.g])
                    k.cp('pool', posb.t[:], posf.t[:], r=[posf.g], w=[posb.g])
                    k.dma(b1s.t[:], W["b1"], w=[b1s.g])
                    k.dma(w2f.t[:, :, :], W["w2"].rearrange("(c p) d -> p c d", p=128), w=[w2f.g])
                    k.cp('pool', w2b.t[:, :, :], w2f.t[:, :, :], r=[w2f.g], w=[w2b.g])
                    w1v = W["w1"].rearrange("(l d) h -> d l h", d=128)
                    for lg in range(8):
                        st, wb = w1st[lgc % 2], w1b[lgc % 2]
                        lgc += 1
                        k.dma(st.t[:, :, :], w1v[:, 4 * lg:4 * lg + 4, :], w=[st.g])
                        k.cp('dve' if lg % 2 == 0 else 'pool', wb.t[:, :, :], st.t[:, :, :], r=[st.g], w=[wb.g])
                        for ll in range(4):
                            l = 4 * lg + ll
                            rhs = src3[:, (l // 16):(l // 16) + 511, l % 16]
                            for hc in range(4):
                                k.mm(hacc[hc][:, 0:511], wb.t[:, ll, hc * 128:(hc + 1) * 128], rhs, l == 0, l == 31,
                                     r=[wb.g, src.g], w=[hag[hc]])
                                k.mm(pbs[hc][:, 0:1], wb.t[:, ll, hc * 128:(hc + 1) * 128], posb.t[:, l:l + 1],
                                     l == 0, l == 31, r=[wb.g, posb.g], w=[pbg[hc]],
                                     inc=(l == 31 or (ll == 3 and hc == 3)))
                    for hc in range(4):
                        k.tt('dve', btot.t[:, hc:hc + 1], pbs[hc][:, 0:1], b1s.t[:, hc:hc + 1], ALU.add,
                             r=[pbg[hc], b1s.g], w=[btot.g])
                    for hc in range(4):
                        k.act(hb.t[:, 0:511], hacc[hc][:, 0:511], AF.Identity, r=[hag[hc], btot.g], w=[hb.g],
                              bias=btot.t[:, hc:hc + 1])
                        k.tt('dve', t1.t[:, 0:511], hb.t[:, 0:511], hb.t[:, 0:511], ALU.mult, r=[hb.g], w=[t1.g])
                        k.ts('dve', t1.t[:, 0:511], t1.t[:, 0:511], 0.044715, 1.0, ALU.mult, ALU.add, r=[], w=[t1.g])
                        k.tt('dve', t1.t[:, 0:511], t1.t[:, 0:511], hb.t[:, 0:511], ALU.mult, r=[hb.g], w=[t1.g])
                        k.act(t2.t[:, 0:511], t1.t[:, 0:511], AF.Sigmoid, r=[t1.g], w=[t2.g], scale=1.5957691216057308)
                        k.tt('dve', hT.t[:, hc, 0:511], hb.t[:, 0:511], t2.t[:, 0:511], ALU.mult, r=[hb.g, t2.g],
                             w=[hT.g])
                    if kv == "k":
                        for hc in range(4):
                            k.mm(hacc[0][:, 0:511], w2b.t[:, hc, :], hT.t[:, hc, 0:511], hc == 0, hc == 3,
                                 r=[w2b.g, hT.g], w=[hag[0]])
                        k.cp('act', kcT.t[:, 0:511], hacc[0][:, 0:511], r=[hag[0]], w=[kcT.g])
                    else:
                        for ct in range(4):
                            for hc in range(4):
                                k.mm(hacc[ct][:, 0:128], hT.t[:, hc, 128 * ct:128 * (ct + 1)], w2b.t[:, hc, :],
                                     hc == 0, hc == 3, r=[w2b.g, hT.g], w=[hag[ct]])
                            k.cp('act', vc.t[:, ct, 0:128], hacc[ct][:, 0:128], r=[hag[ct]], w=[vc.g])
                S.barrier()
                stop_here('mlp')
            S.barrier()

        with ExitStack() as esB:
            sbB, psB = _mk(nc, esB)
            bexp = sbB("bexp_s", [128, 8192], BF16)
            ovx = sbB("ovx_s", [128, 4, 132], BF16)
            mtw = sbB("mtw_s", [128, 2304], BF16)
            m1w = sbB("m1w_s", [128, 256], F32)
            m2w = sbB("m2w_s", [128, 256], F32)
            lb = sbB("lb_s", [128, 128], F32)
            lb1 = sbB("lb1_s", [128, 128], F32)
            oml = sbB("oml_s", [128, 128], F32)
            hnw = sbB("hnw_s", [128, 128], F32)
            for b_, d_ in ((bexp, bexp_d), (ovx, ovx_d), (mtw, mtw_d), (m1w, m1w_d), (m2w, m2w_d), (lb, lb0_d),
                           (lb1, lb1_d), (hnw, hnw_d)):
                k.dma(b_.t[:], d_, w=[b_.g])
            k.tt('dve', lb.t[:], lb.t[:], lb1.t[:], ALU.subtract, r=[lb1.g], w=[lb.g])
            k.act(lb.t[:], lb.t[:], AF.Sigmoid, r=[], w=[lb.g])
            k.ts('dve', oml.t[:], lb.t[:], -1.0, 1.0, ALU.mult, ALU.add, r=[lb.g], w=[oml.g])

            kslcT = sbB("kslcT", [128, S_LEN], BF16)
            ksg = [Reg() for _ in range(16)]
            vslc = sbB("vslc", [128, 64, 132], BF16)
            vsg = [Reg() for _ in range(64)]
            kwinT = sbB("kwinT", [128, 1024], BF16)
            kwg = [Reg(), Reg()]
            vwin = sbB("vwin", [128, 8, 132], BF16)
            vwg = [Reg() for _ in range(8)]
            k.ms('pool', vslc.t[:, :, 128:132], 0.0, w=vsg)
            k.ms('pool', vslc.t[:, :, 128:129], 1.0, w=vsg)
            k.ms('pool', vwin.t[:, :, 128:132], 0.0, w=vwg)
            k.ms('pool', vwin.t[:, :, 128:129], 1.0, w=vwg)

            stop_here('b_setup')
            xnT = sbB("xnTb", [128, 16, 512], BF16)
            csB = sbB("csB", [32, 2, 512], F32)
            qtmp = sbB("qtmp", [128, 512], BF16)
            qT = sbB("qT", [128, 2048], BF16)
            hq = sbB("hq", [128, 512], F32)
            sgb = [sbB(f"sgb{j}", [128, 128], F32) for j in range(4)]
            vhg = [sbB(f"vhg{j}", [128, 128], BF16) for j in range(4)]
            zs = [sbB(f"zs{j}", [128, 128], F32) for j in range(4)]
            zn = [sbB(f"zn{j}", [128, 256], F32) for j in range(4)]
            gs = [sbB(f"gs{j}", [128, 16], F32) for j in range(4)]
            fb = sbB("fb", [128, 128], F32)
            gl = sbB("gl", [128, 128], F32)
            omf = sbB("omf", [128, 128], F32)
            enb = sbB("enb", [128, 128], F32)
            ebt = sbB("ebt", [128, 128], F32)
            ktm = sbB("ktm", [128, 128], BF16)
            kTs = sbB("kTs", [128, 128], BF16)
            qfT = sbB("qfT", [128, 128], BF16)
            qa = sbB("qa", [128, 128], BF16)
            qb = sbB("qb", [128, 128], BF16)
            Am = sbB("Am", [128, 128], BF16)
            Sf = sbB("Sf", [128, 128], F32)
            Sf2 = sbB("Sf2", [128, 128], F32)
            S1 = sbB("S1", [128, 128], F32)
            SAbf = [sbB(f"SAbf{j}", [128, 128], BF16) for j in range(2)]
            SBbf = sbB("SBbf", [128, 128], BF16)
            hz = sbB("hz", [128, 128], F32)
            ssh = sbB("ssh", [128, 1], F32)
            rsh = sbB("rsh", [128, 1], F32)
            rsh2 = sbB("rsh2", [128, 1], F32)
            og = sbB("og", [128, 128], BF16)
            oxh = sbB("oxh", [128, 512], BF16)
            oxn = sbB("oxn", [128, 2, 512], BF16)
            k.ms('pool', qa.t[:], 0.0, w=[qa.g])
            k.ms('pool', qb.t[:], 0.0, w=[qb.g])
            k.ms('pool', Sf.t[:], 0.0, w=[Sf.g])
            k.ms('pool', SAbf[0].t[:], 0.0, w=[SAbf[0].g])
            Ec = [sbB(f"Ec{j}", [128, 512], BF16) for j in range(2)] * 2
            Ecm = [sbB(f"Ecm{j}", [128, 512], BF16) for j in range(4)]
            Es = [sbB(f"Es{j}", [128, 512], BF16) for j in range(3)]
            Esm = [sbB(f"Esm{j}", [128, 512], BF16) for j in range(3)]
            mkd = [sbB(f"mkd{j}", [128, 128], F32) for j in range(2)]
            rz4 = sbB("rz4", [128, 4], F32)
            imps = sbB("imps", [128, 128], F32)
            sc = sbB("sc", [128, 128], F32)
            sc2 = sbB("sc2", [128, 128], F32)
            m8 = sbB("m8", [128, 8], F32)
            m8b = sbB("m8b", [128, 8], F32)
            thr = sbB("thr", [128, 1], F32)
            selb = sbB("selb", [128, 128], BF16)
            selTs = [sbB(f"selT{j}", [128, 128], BF16) for j in range(2)]
            ocmp = [sbB(f"ocmp{j}", [128, 264], F32) for j in range(2)]
            z3 = sbB("z3", [128, 3], F32)
            a3 = sbB("a3", [128, 3], F32)
            acc = sbB("acc", [128, 128], F32)
            onb = sbB("onb", [128, 128], BF16)

            BG = [psB(f"BG{j}", [128, 512], F32) for j in range(2)]
            BGg = [PReg(), PReg()]
            RP = psB("RPb", [128, 512], F32)
            RPg = PReg()
            SM = [psB(f"SM{j}", [128, 4, 128], F32) for j in range(2)]
            SMg = [[PReg()] * 4 for _ in range(2)]
            TPB = psB("TPB", [128, 8, 128], BF16)
            TPg = [PReg()] * 8
            OA = [psB(f"OA{j}", [128, 512], F32) for j in range(2)]
            OAg = [[PReg()] * 3 for _ in range(2)]
            bgc = [0]

            def nbg():
                j = bgc[0] % 2
                bgc[0] += 1
                return BG[j], BGg[j]

            hl = [None]
            es_c = [0]
            ipc = [0]

            def nbg4():
                j = ipc[0] % 4
                ipc[0] += 1
                return [(BG[0], BGg[0]), (BG[1], BGg[1]), (OA[0], OAg[0][0]), (OA[1], OAg[1][0])][j]

            for T in range(NMT):
                for i in range(4):
                    k.dma(xnT.t[:, :, 128 * i:128 * (i + 1)], xnT_d[4 * T + i].rearrange("p (k t) -> p k t", k=16),
                          w=[xnT.g])
                load_cs(csB, 512 * T, 512)
                stop_here('b_load')
                for i in range(4):
                    n = 4 * T + i
                    bg, bgg = nbg4()
                    for kc in range(16):
                        k.mm(bg[:, 0:512], xnT.t[:, kc, 128 * i:128 * (i + 1)], Wb.t[:, kc, TM0:TM0 + 512], kc == 0,
                             kc == 15, r=[xnT.g, Wb.g], w=[bgg])
                    stop_here('tm_a')
                    k.act(sgb[i].t[:], bg[:, 0:128], AF.Sigmoid, r=[], w=[bgg, sgb[i].g])
                    stop_here('tm_a1')
                    k.cp('dve', vhg[i].t[:], bg[:, 128:256], r=[], w=[bgg, vhg[i].g])
                    stop_here('tm_a2')
                    k.act(zs[i].t[:], bg[:, 256:384], AF.Sigmoid, r=[], w=[bgg, zs[i].g])
                    k.tt('dve', zs[i].t[:], zs[i].t[:], bg[:, 256:384], ALU.mult, r=[], w=[bgg, zs[i].g])
                    k.cp('dve', vslc.t[:, n, 0:128], bg[:, 384:512], r=[], w=[bgg, vsg[n]])
                    stop_here('tm_b')
                    bg, bgg = nbg4()
                    for kc in range(16):
                        k.mm(bg[:, 0:400], xnT.t[:, kc, 128 * i:128 * (i + 1)], Wb.t[:, kc, TM0 + 512:TM0 + 912],
                             kc == 0, kc == 15, r=[xnT.g, Wb.g], w=[bgg])
                    stop_here('tm_c')
                    k.cp('dve', vwin.t[:, n % 8, 0:128], bg[:, 0:128], r=[], w=[bgg, vwg[n % 8]])
                    k.act(zn[i].t[:], bg[:, 128:384], AF.Sigmoid, r=[], w=[bgg, zn[i].g])
                    k.tt('dve', zn[i].t[:], zn[i].t[:], bg[:, 128:384], ALU.mult, r=[], w=[bgg, zn[i].g])
                    k.act(gs[i].t[:, 0:16], bg[:, 384:400], AF.Sigmoid, r=[], w=[bgg, gs[i].g])
                stop_here('b_tm')
                for ch in range(7):
                    bg, bgg = nbg4()
                    for kc in range(16):
                        k.mm(bg[:, 0:512], Wb.t[:, kc, ch * 128:(ch + 1) * 128], xnT.t[:, kc, :], kc == 0, kc == 15,
                             r=[xnT.g, Wb.g], w=[bgg])
                    if ch == 0:
                        k.cp('act', hq.t[:], bg[:, 0:512], r=[bgg], w=[hq.g])
                    elif ch <= 4:
                        h = ch - 1
                        k.act(qtmp.t[:], bg[:, 0:512], AF.Identity, r=[bgg], w=[qtmp.g], scale=float(128 ** -0.5))
                        rope(qtmp.t[:, :], qtmp.g, 512, csB.t, csB.g, RP, RPg)
                        k.cp('pool', qT.t[:, :].rearrange("p (i h q) -> p i h q", i=4, h=4)[:, :, h, :],
                             qtmp.t[:, :].rearrange("p (i q) -> p i q", i=4), r=[qtmp.g], w=[qT.g])
                    elif ch == 5:
                        k.cp('act', kslcT.t[:, 512 * T:512 * (T + 1)], bg[:, 0:512], r=[bgg], w=[ksg[T]])
                        rope(kslcT.t[:, 512 * T:512 * (T + 1)], ksg[T], 512, csB.t, csB.g, RP, RPg)
                    else:
                        o_ = (T % 2) * 512
                        k.cp('act', kwinT.t[:, o_:o_ + 512], bg[:, 0:512], r=[bgg], w=[kwg[T % 2]])
                        rope(kwinT.t[:, o_:o_ + 512], kwg[T % 2], 512, csB.t, csB.g, RP, RPg)

                stop_here('b_fm')
                def hgrn_tile(i):
                    n = 4 * T + i
                    p = n % 2
                    k.tt('dve', fb.t[:], sgb[i].t[:], oml.t[:], ALU.mult, r=[sgb[i].g, oml.g], w=[fb.g])
                    k.tt('dve', fb.t[:], fb.t[:], lb.t[:], ALU.add, r=[lb.g], w=[fb.g])
                    k.act(gl.t[:], fb.t[:], AF.Ln, r=[fb.g], w=[gl.g])
                    k.ts('dve', omf.t[:], fb.t[:], -1.0, 1.0, ALU.mult, ALU.add, r=[fb.g], w=[omf.g])
                    yield
                    k.mm(SM[0][:, 0, :], ublk.t[:], gl.t[:], True, True, r=[ublk.g, gl.g], w=[SMg[0][0]])
                    k.mm(SM[0][:, 1, :], gl.t[:], ublk.t[:], True, True, r=[ublk.g, gl.g], w=[SMg[0][1]])
                    yield
                    k.act(enb.t[:], SM[0][:, 0, :], AF.Exp, r=[SMg[0][0]], w=[enb.g], scale=-1.0)
                    k.tt('dve', ktm.t[:], omf.t[:], enb.t[:], ALU.mult, r=[omf.g, enb.g], w=[ktm.g])
                    yield
                    k.act(ebt.t[:], SM[0][:, 1, :], AF.Exp, r=[SMg[0][1]], w=[ebt.g])
                    k.tt('dve', qfT.t[:], hq.t[:, 128 * i:128 * (i + 1)], ebt.t[:], ALU.mult, r=[hq.g, ebt.g],
                         w=[qfT.g])
                    k.cp('pool', qa.t[:, 0:64], qfT.t[:, 0:64], r=[qfT.g], w=[qa.g])
                    k.cp('pool', qb.t[:, 64:128], qfT.t[:, 64:128], r=[qfT.g], w=[qb.g])
                    yield
                    k.tr(TPB[:, 0, :], ktm.t[:], ident.t[:], r=[ktm.g, ident.g], w=[TPg[0]])
                    k.cp('act', kTs.t[:], TPB[:, 0, :], r=[TPg[0]], w=[kTs.g])
                    yield
                    k.mm(SM[0][:, 2, :], kTs.t[:], qfT.t[:], True, True, r=[kTs.g, qfT.g], w=[SMg[0][2]])
                    k.tt('dve', Am.t[:], SM[0][:, 2, :], ublk.t[:], ALU.mult, r=[SMg[0][2], ublk.g], w=[Am.g])
                    yield
                    k.mm(SM[1][:, 0, :], ktm.t[0:64, :], vhg[i].t[0:64, :], True, True, r=[ktm.g, vhg[i].g],
                         w=[SMg[1][0]])
                    k.mm(SM[1][:, 1, :], ktm.t[64:128, :], vhg[i].t[64:128, :], True, True, r=[ktm.g, vhg[i].g],
                         w=[SMg[1][1]])
                    yield
                    k.ts('dve', S1.t[:], Sf.t[:], ebt.t[:, 63:64], None, ALU.mult, None, r=[Sf.g, ebt.g], w=[S1.g])
                    k.stt('dve', Sf2.t[:], SM[1][:, 0, :], ebt.t[:, 63:64], S1.t[:], ALU.mult, ALU.add,
                          r=[SMg[1][0], ebt.g, S1.g], w=[Sf2.g])
                    yield
                    k.cp('pool', SBbf.t[:], Sf2.t[:], r=[Sf2.g], w=[SBbf.g])
                    k.ts('dve', S1.t[:], Sf2.t[:], ebt.t[:, 127:128], None, ALU.mult, None, r=[Sf2.g, ebt.g],
                         w=[S1.g])
                    k.stt('dve', Sf.t[:], SM[1][:, 1, :], ebt.t[:, 127:128], S1.t[:], ALU.mult, ALU.add,
                          r=[SMg[1][1], ebt.g, S1.g], w=[Sf.g])
                    k.cp('pool', SAbf[1 - p].t[:], Sf.t[:], r=[Sf.g], w=[SAbf[1 - p].g])
                    yield
                    k.mm(SM[0][:, 3, :], Am.t[:], vhg[i].t[:], True, False, r=[Am.g, vhg[i].g], w=[SMg[0][3]], inc=False)
                    k.mm(SM[0][:, 3, :], qa.t[:], SAbf[p].t[:], False, False, r=[qa.g, SAbf[p].g], w=[SMg[0][3]],
                         inc=False)
                    k.mm(SM[0][:, 3, :], qb.t[:], SBbf.t[:], False, True, r=[qb.g, SBbf.g], w=[SMg[0][3]])
                    k.act(junk.t[:, :], SM[0][:, 3, :], AF.Square, r=[SMg[0][3]], w=[junk.g, ssh.g],
                          accum_out=ssh.t[:])
                    k.ts('dve', rsh.t[:], ssh.t[:], 1.0 / 128, 1e-6, ALU.mult, ALU.add, r=[ssh.g], w=[rsh.g])
                    k.act(rsh.t[:], rsh.t[:], AF.Ln, r=[], w=[rsh.g])
                    k.act(rsh2.t[:], rsh.t[:], AF.Exp, r=[rsh.g], w=[rsh2.g], scale=-0.5)
                    k.tt('pool', hz.t[:], hnw.t[:], zs[i].t[:], ALU.mult, r=[hnw.g, zs[i].g], w=[hz.g])
                    k.stt('dve', og.t[:], SM[0][:, 3, :], rsh2.t[:, 0:1], hz.t[:], ALU.mult, ALU.mult,
                          r=[SMg[0][3], rsh2.g, hz.g], w=[og.g])
                    yield
                    k.tr(TPB[:, 1, :], og.t[:], ident.t[:], r=[og.g, ident.g], w=[TPg[1]])
                    k.cp('act', oxh.t[:, 128 * i:128 * (i + 1)], TPB[:, 1, :], r=[TPg[1]], w=[oxh.g])

                for i in range(4):
                    n = 4 * T + i
                    def cmp_topk(i):
                        n = 4 * T + i
                        qall = qT.t[:, i * 512:(i + 1) * 512]
                        nct = (8 * n + 6) // 128 + 1
                        for ct in range(nct):
                            if ct > 0:
                                yield
                            bg, bgg = nbg()
                            k.mm(bg[:, 0:512], kcT.t[:, 128 * ct:128 * (ct + 1)], qall, True, True, r=[kcT.g, qT.g],
                                 w=[bgg])
                            k.act(Ec[ct].t[:], bg[:, 0:512], AF.Exp, r=[bgg], w=[Ec[ct].g])
                            st = min(128 * n - 2048 * ct + 96, 2176)
                            for h in range(4):
                                k.tt('pool', Ecm[ct].t[:, h * 128:(h + 1) * 128], Ec[ct].t[:, h * 128:(h + 1) * 128],
                                     mtw.t[:, st:st + 128], ALU.mult, r=[Ec[ct].g, mtw.g], w=[Ecm[ct].g])
                        yield
                        for hh in range(2):
                            for ct in range(nct):
                                k.mm(RP[:, hh * 132:(hh + 1) * 132], Ecm[ct].t[:, (HL + hh) * 128:(HL + hh + 1) * 128], vc.t[:, ct, :],
                                     ct == 0, ct == nct - 1, r=[Ecm[ct].g, vc.g], w=[RPg])
                        k.cp('act', ocmp[i % 2].t[:, 0:264], RP[:, 0:264], r=[RPg], w=[ocmp[i % 2].g])
                        for pair in range(2):
                            yield
                            bg, bgg = nbg()
                            for h2 in range(2):
                                h = 2 * pair + h2
                                for ct in range(nct):
                                    k.mm(bg[:, h2 * 132:(h2 + 1) * 132], Ecm[ct].t[:, h * 128:(h + 1) * 128],
                                         ovx.t[:, ct, :], ct == 0, ct == nct - 1, r=[Ecm[ct].g, ovx.g], w=[bgg])
                            for h2 in range(2):
                                h = 2 * pair + h2
                                k.ts('dve', rz4.t[:, h:h + 1], bg[:, h2 * 132 + 128:h2 * 132 + 129], 1e-30, None, ALU.max,
                                     None, r=[bgg], w=[rz4.g])
                                k.rcp(rz4.t[:, h:h + 1], rz4.t[:, h:h + 1], r=[], w=[rz4.g])
                                if h == 0:
                                    k.ts('dve', imps.t[:], bg[:, 0:128], rz4.t[:, 0:1], None, ALU.mult, None,
                                         r=[bgg, rz4.g], w=[imps.g])
                                else:
                                    k.stt('dve', imps.t[:], bg[:, h2 * 132:h2 * 132 + 128], rz4.t[:, h:h + 1], imps.t[:],
                                          ALU.mult, ALU.add, r=[bgg, rz4.g], w=[imps.g])
                        yield
                        w0 = 128 - 2 * n
                        k.tt('dve', sc.t[:], imps.t[:], m1w.t[:, w0:w0 + 128], ALU.mult, r=[imps.g, m1w.g], w=[sc.g])
                        k.tt('dve', sc.t[:], sc.t[:], m2w.t[:, w0:w0 + 128], ALU.add, r=[m2w.g], w=[sc.g])
                        k.ms('dve', sc.t[:, 0:1], 3e30, w=[sc.g])
                        yield
                        S.op('dve', lambda e: e.max(out=m8.t[:], in_=sc.t[:]), r=[sc.g], w=[m8.g])
                        S.op('dve', lambda e: e.match_replace(out=sc2.t[:], in_to_replace=m8.t[:], in_values=sc.t[:],
                                                              imm_value=-2.0), r=[sc.g, m8.g], w=[sc2.g])
                        S.op('dve', lambda e: e.max(out=m8b.t[:], in_=sc2.t[:]), r=[sc2.g], w=[m8b.g])
                        k.ts('dve', thr.t[:], m8b.t[:, 7:8], 0.0, None, ALU.max, None, r=[m8b.g], w=[thr.g])
                        k.ts('dve', selb.t[:], sc.t[:], thr.t[:, 0:1], None, ALU.is_ge, None, r=[sc.g, thr.g], w=[selb.g])
                        yield
                        k.tr(TPB[:, 2, :], selb.t[:], ident.t[:], r=[selb.g, ident.g], w=[TPg[2]])
                        k.cp('act', selTs[i % 2].t[:], TPB[:, 2, :], r=[TPg[2]], w=[selTs[i % 2].g])
                    if i == 0:
                        for _ in cmp_topk(0):
                            pass
                    gens = [hgrn_tile(i)] + ([cmp_topk(i + 1)] if i < 3 else [])
                    selT = selTs[i % 2]
                    qown = qT.t[:, i * 512 + HL * 128:i * 512 + HL * 128 + 256]
                    jobs = []
                    for br in range(2):
                        kts = list(range(0, n + 1)) if br == 0 else list(range(max(0, n - 4), n + 1))
                        for p0 in range(0, len(kts), 2):
                            jobs.append((br, kts, kts[p0:p0 + 2]))

                    def stage_a(job, pi):
                        br, kts, pk = job
                        bg, bgg = nbg()
                        e_s = Es[es_c[0] % 3]
                        e_m = Esm[es_c[0] % 3]
                        es_c[0] += 1
                        mb = pi % 2
                        for j, kt in enumerate(pk):
                            if br == 0:
                                lhs = kslcT.t[:, 128 * kt:128 * (kt + 1)]
                                lg_ = ksg[kt // 4]
                            else:
                                o_ = (kt % 8) * 128
                                lhs = kwinT.t[:, o_:o_ + 128]
                                lg_ = kwg[(kt // 4) % 2]
                            k.mm(bg[:, j * 256:(j + 1) * 256], lhs, qown, True, True, r=[lg_, qT.g], w=[bgg],
                                 inc=(j == len(pk) - 1))
                        if br == 0:
                            for j, kt in enumerate(pk):
                                k.mm(SM[mb][:, 2 + j, :], bexp.t[:, 128 * kt:128 * (kt + 1)], selT.t[:], True, True,
                                     r=[bexp.g, selT.g], w=[SMg[mb][2 + j]])
                        k.act(e_s.t[:, 0:256 * len(pk)], bg[:, 0:256 * len(pk)], AF.Exp, r=[bgg], w=[e_s.g])
                        for j, kt in enumerate(pk):
                            if br == 0:
                                if kt == n:
                                    k.tt('dve', mkd[j].t[:], SM[mb][:, 2 + j, :], tri.t[:], ALU.mult,
                                         r=[SMg[mb][2 + j], tri.g], w=[mkd[j].g])
                                    mk, mkg = mkd[j].t[:], mkd[j].g
                                else:
                                    mk, mkg = SM[mb][:, 2 + j, :], SMg[mb][2 + j]
                            elif kt == n:
                                mk, mkg = tri.t[:], tri.g
                            elif kt == n - 4:
                                mk, mkg = trilo.t[:], trilo.g
                            else:
                                mk = None
                            for hh in range(2):
                                c0 = j * 256 + hh * 128
                                if mk is None:
                                    k.cp('pool', e_m.t[:, c0:c0 + 128], e_s.t[:, c0:c0 + 128], r=[e_s.g], w=[e_m.g])
                                else:
                                    k.tt('dve', e_m.t[:, c0:c0 + 128], e_s.t[:, c0:c0 + 128], mk, ALU.mult,
                                         r=[e_s.g, mkg], w=[e_m.g])
                        return e_m

                    def stage_b(job, e_m):
                        br, kts, pk = job
                        for j, kt in enumerate(pk):
                            for hh in range(2):
                                c0 = j * 256 + hh * 128
                                if br == 0:
                                    rhs, rg = vslc.t[:, kt, :], vsg[kt]
                                else:
                                    rhs, rg = vwin.t[:, kt % 8, :], vwg[kt % 8]
                                last = (j == len(pk) - 1 and hh == 1)
                                k.mm(OA[hh][:, br * 132:(br + 1) * 132], e_m.t[:, c0:c0 + 128], rhs,
                                     kt == kts[0], kt == kts[-1], r=[e_m.g, rg], w=[OAg[hh][br]],
                                     inc=(last or kt == kts[-1]))

                    prev = None
                    for pi, job in enumerate(jobs):
                        em = stage_a(job, pi)
                        if prev is not None:
                            stage_b(*prev)
                        prev = (job, em)
                        for g_ in gens:
                            next(g_, None)
                    stage_b(*prev)
                    for g_ in gens:
                        for _ in g_:
                            pass
                    for hh in range(2):
                        k.ts('dve', z3.t[:, 0:1], ocmp[i % 2].t[:, hh * 132 + 128:hh * 132 + 129], 1e-30, None, ALU.max,
                             None, r=[ocmp[i % 2].g], w=[z3.g])
                        for bi, c_ in ((1, 128), (2, 260)):
                            k.ts('dve', z3.t[:, bi:bi + 1], OA[hh][:, c_:c_ + 1], 1e-30, None, ALU.max, None,
                                 r=[OAg[hh][bi - 1]], w=[z3.g])
                        k.rcp(z3.t[:, :], z3.t[:, :], r=[], w=[z3.g])
                        k.tt('dve', a3.t[:, :], z3.t[:, :], gs[i].t[:, 3 * hh:3 * hh + 3], ALU.mult,
                             r=[z3.g, gs[i].g], w=[a3.g])
                        k.ts('dve', acc.t[:], ocmp[i % 2].t[:, hh * 132:hh * 132 + 128], a3.t[:, 0:1], None, ALU.mult, None,
                             r=[ocmp[i % 2].g, a3.g], w=[acc.g])
                        k.stt('dve', acc.t[:], OA[hh][:, 0:128], a3.t[:, 1:2], acc.t[:], ALU.mult, ALU.add,
                              r=[OAg[hh][0], a3.g], w=[acc.g])
                        k.stt('dve', acc.t[:], OA[hh][:, 132:260], a3.t[:, 2:3], acc.t[:], ALU.mult, ALU.add,
                              r=[OAg[hh][1], a3.g], w=[acc.g])
                        k.tt('dve', onb.t[:], acc.t[:], zn[i].t[:, hh * 128:(hh + 1) * 128], ALU.mult,
                             r=[acc.g, zn[i].g], w=[onb.g])
                        k.tr(TPB[:, 3 + hh, :], onb.t[:], ident.t[:], r=[onb.g, ident.g], w=[TPg[3 + hh]])
                        k.cp('act', oxn.t[:, hh, 128 * i:128 * (i + 1)], TPB[:, 3 + hh, :], r=[TPg[3 + hh]], w=[oxn.g])
                store_ox(0, T, oxh.t[:, :], oxh.g)
                store_ox(4, T, oxn.t[:, 0, :], oxn.g)
                store_ox(8, T, oxn.t[:, 1, :], oxn.g)
            S.barrier()
        esP.close()

        if FUSED:
            ccs = es.enter_context(nc.semaphore("ccsem"))
            S.barrier()
            S.q['pool'].append(lambda e: e.collective_compute(
                "AllGather", ALU.bypass, replica_groups=[list(range(8))],
                ins=[oxd_t.ap()], outs=[gath_t.ap()]).then_inc(ccs))
            S.q['pool'].append(lambda e: e.wait_ge(ccs, 1))
            gth = Buf(None)
            dmy = sb_outer("dmy", [128, 8], F32)
            k.ms('pool', dmy.t[:], 0.0, w=[dmy.g, gth.g])

            def load_oT(oT, psx, sbx):
                selm = sbx("selm_s", [128, 8, 128], BF16)
                k.dma(selm.t[:, :, :], P2["selm"], w=[selm.g])
                gts = [sbx(f"gt{j}", [128, 512], BF16) for j in range(4)]
                sacc = [psx(f"sacc{j}", [128, 512], F32) for j in range(2)]
                saccg = [PReg(), PReg()]
                cnt = 0
                gi = 0
                for rho in range(8):
                    for cl in range(3):
                        chunk = rho if cl == 0 else 8 + 2 * rho + (cl - 1)
                        for half in range(2):
                            ac, acg = sacc[cnt % 2], saccg[cnt % 2]
                            cnt += 1
                            for a0 in range(4):
                                a = 4 * cl + a0
                                for jh in range(2):
                                    gt = gts[gi % 4]
                                    gi += 1
                                    r0 = ((rho * 12 + a) * 8 + 4 * jh) * 32
                                    k.dma(gt.t[:, :], gath[r0:r0 + 128, 512 * half:512 * (half + 1)], r=[gth.g],
                                          w=[gt.g])
                                    first = (a0 == 0 and jh == 0)
                                    last = (a0 == 3 and jh == 1)
                                    k.mm(ac[:, :], selm.t[:, 2 * a0 + jh, :], gt.t[:, :], first, last,
                                         r=[selm.g, gt.g], w=[acg], inc=True)
                            k.cp('act' if cnt % 2 == 0 else 'dve', oT.t[:, chunk, 512 * half:512 * (half + 1)], ac[:, :],
                                 r=[acg], w=[oT.g])

            _phase2(nc, S, k, P2, normw, ident, load_oT)
        S.finish()
        S.emit()


HL = 0


def build_p2():
    nc = bass.Bass("TRN2", target_bir_lowering=False)

    def din(name, shape, dt=F32):
        return nc.dram_tensor(name, list(shape), dt, kind="ExternalInput").ap()

    P2 = dict(xo=din("xo", [1024, D]), wm=din("wm", [D, 4096]), wbh=din("wbh", [1024, D]), wbn=din("wbn", [D, D]),
              wo=din("wo", [D, D]), fnw=din("fnw", [128, D]),
              y=nc.dram_tensor("y", [1024, D], F32, kind="ExternalOutput").ap())
    oo = din("oo", [3072, 1024], BF16)
    normw_d = din("normw", [128, 16])
    ident_d = din("ident", [128, 128], BF16)
    with ExitStack() as es:
        S = Sched(nc, es)
        k = K(S)
        sb, ps = _mk(nc, es)
        normw = sb("normw_s", [128, 16], F32)
        ident = sb("ident_s", [128, 128], BF16)
        k.dma(normw.t[:], normw_d, w=[normw.g])
        k.dma(ident.t[:], ident_d, w=[ident.g])

        def load_oT(oT, psx, sbx):
            for c in range(24):
                k.dma(oT.t[:, c, :], oo[128 * c:128 * (c + 1), :], w=[oT.g])

        _phase2(nc, S, k, P2, normw, ident, load_oT)
        S.finish()
        S.emit()
    return nc


def _phase2(nc, S, k, P2, normw, ident, load_oT):
    xo, wm, wbh, wbn, wo, fnw_d, y = P2["xo"], P2["wm"], P2["wbh"], P2["wbn"], P2["wo"], P2["fnw"], P2["y"]
    with ExitStack() as es:
        sb, ps = _mk(nc, es)
        mT = sb("p2mT", [128, 16, 1024], BF16)
        mTg = [Reg() for _ in range(16)]
        junk = sb("junk2", [128, 2048], BF16)
        ss = sb("ss2", [128, 1], F32)
        rs = sb("rs_2", [128, 1], F32)
        rs2 = sb("rs2_2", [128, 1], F32)

        with ExitStack() as esA:
            sbA, psA = _mk(nc, esA)
            xnT = sbA("p2xnT", [128, 16, 1024], BF16)
            oT = sbA("p2oT", [128, 24, 1024], BF16)
            with ExitStack() as esL:
                sbL, psL = _mk(nc, esL)
                load_oT(oT, psL, sbL)
                S.barrier()
            with ExitStack() as esA1:
                sbA1, psA1 = _mk(nc, esA1)
                xts = [sbA1(f"p2xt{j}", [128, D], F32) for j in range(2)]
                xns = [sbA1(f"p2xn{j}", [128, D], BF16) for j in range(2)]
                tps = [psA1(f"p2tp{j}", [128, 8, 128], BF16) for j in range(2)]
                tpg = [PReg(), PReg()]
                for n in range(8):
                    xt, xn = xts[n % 2], xns[n % 2]
                    k.dma(xt.t[:], xo[128 * n:128 * (n + 1), :], w=[xt.g])
                    _norm_transpose(k, xt, xn, tps, tpg, xnT.t[:, :, 128 * n:128 * (n + 1)], xnT.g, ident, junk, ss,
                                    rs, rs2)
                S.barrier()
            with ExitStack() as esA2:
                sbA2, psA2 = _mk(nc, esA2)
                wst = [sbA2(f"p2wst{j}", [128, 56, 128], F32) for j in range(2)]
                wbf = [sbA2(f"wbf{j}", [128, 56, 128], BF16) for j in range(2)]
                g1s = sbA2("g1s", [128, 512], F32)
                g2s = sbA2("g2s", [128, 512], F32)
                acc = [psA2(f"p2acc{j}", [128, 512], F32) for j in range(8)]
                accg = [PReg() for _ in range(8)]
                ac = [0]
                for fo in range(16):
                    st, wb = wst[fo % 2], wbf[fo % 2]
                    cs_ = slice(128 * fo, 128 * (fo + 1))
                    k.dma(st.t[:, 0:16, :], wm[:, 128 * fo:128 * (fo + 1)].rearrange("(k p) c -> p k c", p=128), w=[st.g])
                    k.dma(st.t[:, 16:32, :], wm[:, 2048 + 128 * fo:2048 + 128 * (fo + 1)].rearrange("(k p) c -> p k c", p=128),
                          w=[st.g])
                    k.dma(st.t[:, 32:40, :], wbh[:, cs_].rearrange("(k p) c -> p k c", p=128), w=[st.g])
                    k.dma(st.t[:, 40:56, :], wbn[:, cs_].rearrange("(k p) c -> p k c", p=128), w=[st.g])
                    for kc in range(16):
                        k.ts('dve' if kc % 2 == 0 else 'pool', wb.t[:, kc, :], st.t[:, kc, :], normw.t[:, kc:kc + 1],
                             None, ALU.mult, None, r=[st.g, normw.g], w=[wb.g])
                        k.ts('pool' if kc % 2 == 0 else 'dve', wb.t[:, 16 + kc, :], st.t[:, 16 + kc, :],
                             normw.t[:, kc:kc + 1], None, ALU.mult, None, r=[st.g, normw.g], w=[wb.g])
                    k.cp('pool', wb.t[:, 32:56, :], st.t[:, 32:56, :], r=[st.g], w=[wb.g])
                    for half in range(2):
                        ts_ = slice(512 * half, 512 * (half + 1))
                        a4 = []
                        for j in range(4):
                            a4.append((acc[ac[0] % 8], accg[ac[0] % 8]))
                            ac[0] += 1
                        for kc in range(16):
                            k.mm(a4[0][0][:, :], wb.t[:, kc, :], xnT.t[:, kc, ts_], kc == 0, kc == 15,
                                 r=[wb.g, xnT.g], w=[a4[0][1]])
                        for kc in range(16):
                            k.mm(a4[1][0][:, :], wb.t[:, 16 + kc, :], xnT.t[:, kc, ts_], kc == 0, kc == 15,
                                 r=[wb.g, xnT.g], w=[a4[1][1]])
                        for kc in range(8):
                            k.mm(a4[2][0][:, :], wb.t[:, 32 + kc, :], oT.t[:, kc, ts_], kc == 0, kc == 7,
                                 r=[wb.g, oT.g], w=[a4[2][1]])
                        for kc in range(16):
                            k.mm(a4[3][0][:, :], wb.t[:, 40 + kc, :], oT.t[:, 8 + kc, ts_], kc == 0, kc == 15,
                                 r=[wb.g, oT.g], w=[a4[3][1]])
                        k.act(g1s.t[:], a4[0][0][:, :], AF.Sigmoid, r=[a4[0][1]], w=[g1s.g])
                        k.act(g2s.t[:], a4[1][0][:, :], AF.Sigmoid, r=[a4[1][1]], w=[g2s.g])
                        k.tt('dve', g1s.t[:], g1s.t[:], a4[2][0][:, :], ALU.mult, r=[a4[2][1]], w=[g1s.g])
                        k.tt('dve', g2s.t[:], g2s.t[:], a4[3][0][:, :], ALU.mult, r=[a4[3][1]], w=[g2s.g])
                        k.tt('pool', mT.t[:, fo, ts_], g1s.t[:], g2s.t[:], ALU.add, r=[g1s.g, g2s.g], w=[mTg[fo]])
                S.barrier()
            S.barrier()

        with ExitStack() as esB:
            sbB, psB = _mk(nc, esB)
            wob = sbB("wob", [128, 16, D], BF16)
            wog = [Reg() for _ in range(4)]
            wost = [sbB(f"wost{j}", [128, 4, D], F32) for j in range(2)]
            fnw = sbB("fnw_s", [128, D], F32)
            xts = [sbB(f"xt2{j}", [128, D], F32) for j in range(2)]
            hs = [sbB(f"hs{j}", [128, D], F32) for j in range(2)]
            acc = [psB(f"acc2{j}", [128, 512], F32) for j in range(4)]
            accg = [PReg() for _ in range(4)]
            k.dma(fnw.t[:], fnw_d, w=[fnw.g])
            for q4 in range(4):
                st = wost[q4 % 2]
                k.dma(st.t[:, :, :], wo[512 * q4:512 * (q4 + 1), :].rearrange("(k p) c -> p k c", p=128), w=[st.g])
                k.cp('dve' if q4 % 2 == 0 else 'pool', wob.t[:, 4 * q4:4 * q4 + 4, :], st.t[:, :, :], r=[st.g],
                     w=[wog[q4]])
            for n in range(8):
                xt, h_ = xts[n % 2], hs[n % 2]
                k.dma(xt.t[:], xo[128 * n:128 * (n + 1), :], w=[xt.g])
                for cg in range(4):
                    for fo in range(16):
                        k.mm(acc[cg][:, :], mT.t[:, fo, 128 * n:128 * (n + 1)], wob.t[:, fo, 512 * cg:512 * (cg + 1)],
                             fo == 0, fo == 15, r=[mTg[fo], wog[fo // 4]], w=[accg[cg]])
                    k.tt('dve', h_.t[:, 512 * cg:512 * (cg + 1)], acc[cg][:, :], xt.t[:, 512 * cg:512 * (cg + 1)],
                         ALU.add, r=[accg[cg], xt.g], w=[h_.g])
                k.act(junk.t[:], h_.t[:], AF.Square, r=[h_.g], w=[junk.g, ss.g], accum_out=ss.t[:])
                k.ts('dve', rs.t[:], ss.t[:], 1.0 / D, 1e-6, ALU.mult, ALU.add, r=[ss.g], w=[rs.g])
                k.act(rs.t[:], rs.t[:], AF.Sqrt, r=[], w=[rs.g])
                k.rcp(rs2.t[:], rs.t[:], r=[rs.g], w=[rs2.g])
                k.stt('dve', h_.t[:], h_.t[:], rs2.t[:, 0:1], fnw.t[:], ALU.mult, ALU.mult, r=[rs2.g, fnw.g],
                      w=[h_.g])
                k.dma(y[128 * n:128 * (n + 1), :], h_.t[:], r=[h_.g])
            S.barrier()


def _consts():
    c = {}
    c["ident"] = np.eye(128, dtype=np.float32).astype(NBF)
    s_ = np.arange(128)
    c["ublk"] = ((s_[:, None] // 64 == s_[None, :] // 64) & (s_[:, None] <= s_[None, :])).astype(np.float32)
    c["tri"] = (s_[:, None] <= s_[None, :]).astype(np.float32).astype(NBF)
    c["trilo"] = (s_[:, None] > s_[None, :]).astype(np.float32).astype(NBF)
    r = np.zeros((128, 32), np.float32)
    for m in range(16):
        r[m + 16, m] = -1.0
        r[m, m + 16] = 1.0
    c["rmat"] = r.astype(NBF)
    cc = np.arange(8192)
    c["bexp"] = (cc[None, :] // 64 == s_[:, None]).astype(np.float32).astype(NBF)
    ov = np.zeros((128, 4, 132), np.float32)
    ci = (np.arange(4)[None, :] * 128 + s_[:, None]) * 16
    sj = np.arange(128) * 64
    ovl = (ci[:, :, None] < sj[None, None, :] + 64) & (ci[:, :, None] + 32 > sj[None, None, :])
    ov[:, :, 0:128] = ovl
    ov[:, :, 128] = 1.0
    ov[127, 3, :] = 0.0
    c["ovx"] = ov.astype(NBF)
    v = np.arange(2304)[None, :] - 96
    c["mtw"] = ((16 * s_[:, None] + 31) <= v).astype(np.float32).astype(NBF)
    d_ = np.arange(256)[None, :] - 128
    jt = (s_[:, None] >= 64).astype(np.int64)
    c["m1w"] = (d_ <= jt - 2).astype(np.float32)
    m2 = np.zeros((128, 256), np.float32)
    m2[np.broadcast_to(d_ == jt, (128, 256))] = 2e30
    m2[np.broadcast_to(d_ == jt - 1, (128, 256))] = 1e30
    m2[np.broadcast_to(d_ > jt, (128, 256))] = -1.0
    c["m2w"] = m2
    p_ = np.arange(128)
    c["postab"] = (2048 * (p_[:, None] // 32) + np.arange(2048)[None, :]).astype(np.float32)
    inv = (np.float32(500000.0) ** (-np.arange(0, 32, 2, dtype=np.float32) / np.float32(32))).astype(np.float32)
    c["invf"] = inv[(p_ % 32) % 16].reshape(128, 1).astype(np.float32)
    return c


_NC_CACHE = {}
_DEBUG_MAPS = None
FUSED_MODE = False


def kernel(x, norm_w, w_in, hg_lb_logits, hg_norm_w, cmp_k_pos, cmp_k_w1, cmp_k_b1, cmp_k_w2,
           cmp_v_pos, cmp_v_w1, cmp_v_b1, cmp_v_w2, w_branch_hg, w_branch_nsa, w_out, final_norm_w):
    global HL
    f32 = lambda a: np.ascontiguousarray(np.asarray(a, dtype=np.float32))
    x2 = f32(x).reshape(S_LEN, D)
    w = f32(w_in)[0]
    nw = f32(norm_w)[0]
    normw = np.ascontiguousarray(nw.reshape(16, 128).T)
    cst = _consts()
    lbl = f32(hg_lb_logits)
    hnw = f32(hg_norm_w)[0]
    cmpw = {}
    for kv, (pos, w1, b1, w2) in (("k", (cmp_k_pos, cmp_k_w1, cmp_k_b1, cmp_k_w2)),
                                  ("v", (cmp_v_pos, cmp_v_w1, cmp_v_b1, cmp_v_w2))):
        cmpw[f"c{kv}_posT"] = np.ascontiguousarray(f32(pos)[0].T)
        cmpw[f"c{kv}_w1"] = f32(w1)[0]
        cmpw[f"c{kv}_b1"] = np.ascontiguousarray(f32(b1)[0].reshape(4, 128).T)
        cmpw[f"c{kv}_w2"] = f32(w2)[0]

    HL = 0
    in_maps = []
    for c in range(8):
        g = c // 2
        hl = (c % 2) * 2
        order = [hl, hl + 1] + [h for h in range(4) if h not in (hl, hl + 1)]
        cols = []
        cols += list(range(0 + 128 * c, 128 * c + 128))
        for h in order:
            cols += list(range(4096 + 128 * (4 * g + h), 4096 + 128 * (4 * g + h) + 128))
        cols += list(range(7168 + 128 * g, 7168 + 128 * g + 128))
        cols += list(range(8192 + 128 * g, 8192 + 128 * g + 128))
        cols += list(range(1024 + 128 * c, 1024 + 128 * c + 128))
        cols += list(range(2048 + 128 * c, 2048 + 128 * c + 128))
        cols += list(range(3072 + 128 * c, 3072 + 128 * c + 128))
        cols += list(range(7680 + 128 * g, 7680 + 128 * g + 128))
        cols += list(range(8704 + 128 * g, 8704 + 128 * g + 128))
        cols += list(range(9264 + 128 * (4 * g + hl), 9264 + 128 * (4 * g + hl) + 256))
        gc = 9216 + 12 * g + 3 * hl
        cols += list(range(gc, gc + 6)) + [gc] * 10
        wcore = np.ascontiguousarray(w[:, cols])
        ccols = list(range(6144 + 128 * g, 6144 + 128 * g + 128)) + list(range(6656 + 128 * g, 6656 + 128 * g + 128))
        m = dict(x=x2, wc=wcore, wcc=np.ascontiguousarray(w[:, ccols]), normw=normw,
                 lb0=np.ascontiguousarray(np.broadcast_to(lbl[0, 128 * c:128 * (c + 1)], (128, 128))),
                 lb1=np.ascontiguousarray(np.broadcast_to(lbl[1, 128 * c:128 * (c + 1)], (128, 128))),
                 hnw=np.ascontiguousarray(np.broadcast_to(hnw, (128, 128))))
        m.update(cmpw)
        m.update(cst)
        in_maps.append(m)
    wmg = np.ascontiguousarray(w[:, 11312:11312 + 4096])
    fnw = np.ascontiguousarray(np.broadcast_to(f32(final_norm_w), (128, D)))
    wbh_, wbn_, wo_ = f32(w_branch_hg)[0], f32(w_branch_nsa)[0], f32(w_out)[0]
    if FUSED_MODE:
        for r in range(8):
            sel = np.zeros((128, 8, 128), np.float32)
            for a0 in range(4):
                for jh in range(2):
                    if r // 4 == jh:
                        j4 = r % 4
                        for f in range(32):
                            sel[j4 * 32 + f, 2 * a0 + jh, 32 * a0 + f] = 1.0
            in_maps[r].update(xo=np.ascontiguousarray(x2[1024 * r:1024 * (r + 1)]), wm=wmg, wbh=wbh_, wbn=wbn_,
                              wo=wo_, fnw=fnw, selm=sel.astype(NBF))
        if _DEBUG_MAPS is not None:
            _DEBUG_MAPS.append(in_maps)
            return None
        if "pf" not in _NC_CACHE:
            _NC_CACHE["pf"] = build_p1(FUSED=True)
        res = run_bass_kernel_spmd(_NC_CACHE["pf"], in_maps, core_ids=list(range(8)))
        out = np.concatenate([np.asarray(res.results[r]["y"]) for r in range(8)], axis=0)
        return out.reshape(1, S_LEN, D).astype(np.float32)
    if _DEBUG_MAPS is not None:
        _DEBUG_MAPS.append(in_maps)
        return None
    if "p1" not in _NC_CACHE:
        _NC_CACHE["p1"] = build_p1()
    nc1 = _NC_CACHE["p1"]
    res1 = run_bass_kernel_spmd(nc1, in_maps, core_ids=list(range(8)))
    oxs = [np.asarray(res1.results[c]["ox"]) for c in range(8)]

    full = np.concatenate([o[0:128] for o in oxs] + [o[128:384] for o in oxs], axis=0)
    if "p2" not in _NC_CACHE:
        _NC_CACHE["p2"] = build_p2()
    nc2 = _NC_CACHE["p2"]
    in2 = []
    for r in range(8):
        in2.append(dict(xo=np.ascontiguousarray(x2[1024 * r:1024 * (r + 1)]),
                        oo=np.ascontiguousarray(full[:, 1024 * r:1024 * (r + 1)]),
                        wm=wmg, wbh=wbh_, wbn=wbn_, wo=wo_,
                        normw=normw, fnw=fnw, ident=cst["ident"]))
    res2 = run_bass_kernel_spmd(nc2, in2, core_ids=list(range(8)))
    out = np.concatenate([np.asarray(res2.results[r]["y"]) for r in range(8)], axis=0)
    return out.reshape(1, S_LEN, D).astype(np.float32)
```

```python
import numpy as np
import ml_dtypes
from contextlib import ExitStack
import concourse.bass as bass
import concourse.mybir as mybir
from concourse.bass_utils import run_bass_kernel_spmd

F32 = mybir.dt.float32
BF16 = mybir.dt.bfloat16
AF = mybir.ActivationFunctionType
ALU = mybir.AluOpType
ENG = ('pe', 'act', 'dve', 'pool', 'sp')
NBF = ml_dtypes.bfloat16

S_LEN = 8192
D = 2048
NCOLS = 1808
FM0 = 0
TM0 = 896


class Reg:
    __slots__ = ('w', 'rs', 'excl')

    def __init__(s, excl=False):
        s.w = None
        s.rs = {}
        s.excl = excl


def PReg():
    return Reg(True)


class Buf:
    def __init__(s, t):
        s.t = t
        s.g = Reg()


class Sched:
    EP = 4000
    NDMA = 24

    def __init__(s, nc, es):
        s.nc = nc
        s.es = es
        s.q = {k: [] for k in ENG}
        s.cnt = {k: 0 for k in ENG}
        s.waited = {k: {} for k in ENG}
        s.csem = {k: [] for k in ENG}
        s.dsem = [es.enter_context(nc.semaphore(f"d{j}")) for j in range(s.NDMA)]
        s.dcnt = [0] * s.NDMA
        s.dn = 0

    def _csem(s, eng, ep):
        while len(s.csem[eng]) <= ep:
            s.csem[eng].append(s.es.enter_context(s.nc.semaphore(f"c_{eng}_{len(s.csem[eng])}")))
        return s.csem[eng][ep]

    def _wait(s, eng, evs):
        need = {}
        for ev in evs:
            if ev is None:
                continue
            if ev[0] == 'c':
                _, e2, idx = ev
                if e2 == eng and idx > s.cnt[eng]:
                    continue
                ep = (idx - 1) // s.EP
                val = (idx - 1) % s.EP + 1
                w = s.waited[eng].get(('c', e2), (-1, 0))
                if w[0] > ep or (w[0] == ep and w[1] >= val):
                    continue
                key = ('c', e2)
                cur = need.get(key)
                if cur is None or (ep, val) > (cur[0], cur[1]):
                    need[key] = (ep, val)
            else:
                _, j, m = ev
                if s.waited[eng].get(('d', j), (0, 0))[1] >= m:
                    continue
                key = ('d', j)
                cur = need.get(key)
                if cur is None or m > cur[1]:
                    need[key] = (0, m)
        for key, (ep, val) in need.items():
            s.waited[eng][key] = (ep, val)
            if key[0] == 'c':
                sem = s._csem(key[1], ep)
                v = val
            else:
                sem = s.dsem[key[1]]
                v = 16 * val
            s.q[eng].append(lambda e, sem=sem, v=v: e.wait_ge(sem, v))

    @staticmethod
    def _deps(r, w):
        evs = []
        for x in r:
            evs.append(x.w)
        for x in w:
            evs.append(x.w)
            evs.extend(x.rs.values())
        return evs

    @staticmethod
    def _upd(ev, key, r, w):
        for x in r:
            x.rs[key] = ev
        for x in w:
            x.w = ev
            x.rs = {}

    def op(s, eng, fn, r=(), w=(), inc=True):
        if any(x.excl for x in r):
            w = list(w) + [x for x in r if x.excl]
            r = [x for x in r if not x.excl]
        s._wait(eng, s._deps(r, w))
        idx = s.cnt[eng] + 1
        ev = ('c', eng, idx)
        if inc:
            s.cnt[eng] = idx
            sem = s._csem(eng, (idx - 1) // s.EP)
            s.q[eng].append(lambda e, fn=fn, sem=sem: fn(e).then_inc(sem, 1))
        else:
            s.q[eng].append(lambda e, fn=fn: fn(e))
        s._upd(ev, ('c', eng), r, w)
        return ev

    def dma(s, eng, out, in_, r=(), w=()):
        j = s.dn % s.NDMA
        prev = [('d', j, s.dcnt[j])] if s.dcnt[j] > 0 else []
        s._wait(eng, s._deps(r, w) + prev)
        s.dn += 1
        s.dcnt[j] += 1
        ev = ('d', j, s.dcnt[j])
        sem = s.dsem[j]
        s.q[eng].append(lambda e, out=out, in_=in_, sem=sem: e.dma_start(out=out, in_=in_).then_inc(sem, 16))
        s._upd(ev, ('d', j), r, w)
        return ev

    def barrier(s):
        evs = [('c', e2, s.cnt[e2]) for e2 in ENG if s.cnt[e2] > 0]
        evs += [('d', j, s.dcnt[j]) for j in range(s.NDMA) if s.dcnt[j] > 0]
        for e_ in ENG:
            s._wait(e_, evs)

    def finish(s):
        evs = [('d', j, s.dcnt[j]) for j in range(s.NDMA) if s.dcnt[j] > 0]
        s._wait('sp', evs)

    def emit(s):
        nc = s.nc
        with nc.Block() as block:
            @block.sync
            def _(e):
                for t in s.q['sp']:
                    t(e)

            @block.tensor
            def _(e):
                for t in s.q['pe']:
                    t(e)

            @block.scalar
            def _(e):
                for t in s.q['act']:
                    t(e)

            @block.vector
            def _(e):
                for t in s.q['dve']:
                    t(e)

            @block.gpsimd
            def _(e):
                for t in s.q['pool']:
                    t(e)


class K:
    def __init__(s, S):
        s.S = S

    def act(s, out, in_, func, r, w, **kw):
        s.S.op('act', lambda e: e.activation(out, in_, func, **kw), r=r, w=w)

    def ts(s, eng, out, in0, s1, s2, op0, op1, r, w, **kw):
        if op1 is None:
            s.S.op(eng, lambda e: e.tensor_scalar(out, in0, s1, s2, op0, **kw), r=r, w=w)
        else:
            s.S.op(eng, lambda e: e.tensor_scalar(out, in0, s1, s2, op0, op1, **kw), r=r, w=w)

    def tt(s, eng, out, in0, in1, op, r, w):
        s.S.op(eng, lambda e: e.tensor_tensor(out, in0, in1, op), r=r, w=w)

    def stt(s, eng, out, in0, sc, in1, op0, op1, r, w):
        s.S.op(eng, lambda e: e.scalar_tensor_tensor(out, in0, sc, in1, op0, op1), r=r, w=w)

    def cp(s, eng, out, in_, r, w):
        if eng == 'act':
            s.S.op('act', lambda e: e.activation(out, in_, AF.Copy), r=r, w=w)
        else:
            s.S.op(eng, lambda e: e.tensor_copy(out, in_), r=r, w=w)

    def ms(s, eng, ap, val, w):
        s.S.op(eng, lambda e: e.memset(ap, val), r=(), w=w)

    def mm(s, out, lhsT, rhs, start, stop, r, w, inc=None):
        if inc is None:
            inc = stop
        s.S.op('pe', lambda e: e.matmul(out, lhsT, rhs, start=start, stop=stop), r=r, w=w, inc=inc)

    def tr(s, out, in_, ident, r, w, inc=True):
        s.S.op('pe', lambda e: e.transpose(out, in_, ident), r=r, w=w, inc=inc)

    def rcp(s, out, in_, r, w):
        s.S.op('dve', lambda e: e.reciprocal(out, in_), r=r, w=w)

    def dma(s, out, in_, r=(), w=(), eng='sp'):
        s.S.dma(eng, out, in_, r=r, w=w)


def _mk(nc, es):
    def sb(name, shape, dt):
        return Buf(es.enter_context(nc.sbuf_tensor(name, list(shape), dt)))

    def ps(name, shape, dt):
        return es.enter_context(nc.psum_tensor(name, list(shape), dt))
    return sb, ps


def _norm_transpose(k, xt, xn, tps, tpg, xnT, xnTg, ident, junk, ss, rs, rs2):
    k.act(junk.t[:], xt.t[:], AF.Square, r=[xt.g], w=[junk.g, ss.g], accum_out=ss.t[:])
    k.ts('dve', rs.t[:], ss.t[:], 1.0 / D, 1e-6, ALU.mult, ALU.add, r=[ss.g], w=[rs.g])
    k.act(rs.t[:], rs.t[:], AF.Sqrt, r=[], w=[rs.g])
    k.rcp(rs2.t[:], rs.t[:], r=[rs.g], w=[rs2.g])
    k.ts('dve', xn.t[:], xt.t[:], rs2.t[:, 0:1], None, ALU.mult, None, r=[xt.g, rs2.g], w=[xn.g])
    for half in range(2):
        for kk in range(8):
            kc = half * 8 + kk
            k.tr(tps[half][:, kk, :], xn.t[:, kc * 128:(kc + 1) * 128], ident.t[:], r=[xn.g, ident.g],
                 w=[tpg[half]], inc=(kk == 7))
        k.cp('act' if half == 0 else 'dve', xnT[:, half * 8:(half + 1) * 8, :], tps[half][:, :, :],
             r=[tpg[half]], w=[xnTg])


class _Stop(Exception):
    pass


def build_p1(NTILES=64, STOP=None, FUSED=False):
    nc = bass.Bass("TRN2", target_bir_lowering=False)
    try:
        _build_p1(nc, NTILES, STOP, FUSED)
    except _Stop:
        pass
    return nc


def _build_p1(nc, NTILES, STOP, FUSED):
    NMT = NTILES // 4

    def din(name, shape, dt=F32):
        return nc.dram_tensor(name, list(shape), dt, kind="ExternalInput").ap()

    x = din("x", [S_LEN, D])
    wc = din("wc", [D, NCOLS])
    wcc = din("wcc", [D, 256])
    normw_d = din("normw", [128, 16])
    lb0_d = din("lb0", [128, 128])
    lb1_d = din("lb1", [128, 128])
    hnw_d = din("hnw", [128, 128])
    cw = {}
    for kv in "kv":
        cw[kv] = dict(pos=din(f"c{kv}_posT", [128, 32]), w1=din(f"c{kv}_w1", [4096, 512]),
                      b1=din(f"c{kv}_b1", [128, 4]), w2=din(f"c{kv}_w2", [512, 128]))
    ident_d = din("ident", [128, 128], BF16)
    ublk_d = din("ublk", [128, 128])
    tri_d = din("tri", [128, 128], BF16)
    trilo_d = din("trilo", [128, 128], BF16)
    rmat_d = din("rmat", [128, 32], BF16)
    bexp_d = din("bexp", [128, 8192], BF16)
    ovx_d = din("ovx", [128, 4, 132], BF16)
    mtw_d = din("mtw", [128, 2304], BF16)
    m1w_d = din("m1w", [128, 256])
    m2w_d = din("m2w", [128, 256])
    post_d = din("postab", [128, 2048])
    invf_d = din("invf", [128, 1])
    if FUSED:
        oxd_t = nc.dram_tensor("oxd", [12 * 8 * 32, 1024], BF16)
        gath_t = nc.dram_tensor("gath", [8 * 12 * 8 * 32, 1024], BF16)
        oxd = oxd_t.ap()
        gath = gath_t.ap()
        P2 = dict(xo=din("xo", [1024, D]), wp2=din("wp2", [16, 128, 56, 128]),
                  wo=din("wo", [D, D]), fnw=din("fnw", [128, D]),
                  selm=din("selm", [128, 8, 128], BF16),
                  y=nc.dram_tensor("y", [1024, D], F32, kind="ExternalOutput").ap())
    else:
        ox = nc.dram_tensor("ox", [384, S_LEN], BF16, kind="ExternalOutput").ap()
    xnT_d = nc.dram_tensor("xnT_d", [64, 128, 2048], BF16).ap()

    def store_ox(a_base, T, src, srcg):
        if not FUSED:
            k_[0].dma(ox[32 * a_base:32 * a_base + 128, 512 * T:512 * (T + 1)], src, r=[srcg])
            return
        j = T // 2
        c0 = (T % 2) * 512
        for a0 in range(4):
            r0 = ((a_base + a0) * 8 + j) * 32
            k_[0].dma(oxd[r0:r0 + 32, c0:c0 + 512], src[32 * a0:32 * a0 + 32], r=[srcg])

    k_ = [None]

    with ExitStack() as es:
        S = Sched(nc, es)
        k = K(S)
        k_[0] = k
        sb_outer, _ = _mk(nc, es)
        esP = ExitStack()
        sb, ps = _mk(nc, esP)

        def stop_here(tag):
            if STOP == tag:
                S.barrier()
                S.finish()
                S.emit()
                raise _Stop()

        normw = sb_outer("normw_s", [128, 16], F32)
        ident = sb_outer("ident_s", [128, 128], BF16)
        Wb = sb("Wb", [128, 16, NCOLS], BF16)
        ublk = sb("ublk_s", [128, 128], F32)
        tri = sb("tri_s", [128, 128], BF16)
        trilo = sb("trilo_s", [128, 128], BF16)
        rmat = sb("rmat_s", [128, 32], BF16)
        costab = sb("costab", [128, 2048], F32)
        sintab = sb("sintab", [128, 2048], F32)
        kcT = sb("kcT", [128, 512], BF16)
        vc = sb("vc", [128, 4, 132], BF16)
        junk = sb("junk", [128, 128], BF16)
        ss = sb("ss", [128, 1], F32)
        rs = sb("rs", [128, 1], F32)
        rs2 = sb("rs2", [128, 1], F32)
        rt1 = sb("rt1", [32, 512], F32)
        rt2 = sb("rt2", [32, 512], F32)

        for b_, d_ in ((normw, normw_d), (ident, ident_d), (ublk, ublk_d), (tri, tri_d), (trilo, trilo_d),
                       (rmat, rmat_d)):
            k.dma(b_.t[:], d_, w=[b_.g])

        with ExitStack() as es0:
            sb0, _ = _mk(nc, es0)
            post = sb0("post", [128, 2048], F32)
            invf = sb0("invf_s", [128, 1], F32)
            u = sb0("ang_u", [128, 2048], F32)
            u2 = sb0("ang_u2", [128, 2048], F32)
            k.dma(post.t[:], post_d, w=[post.g])
            k.dma(invf.t[:], invf_d, w=[invf.g])
            k.ts('dve', u.t[:], post.t[:], invf.t[:, 0:1], 1.0 / (2 * np.pi), ALU.mult, ALU.mult,
                 r=[post.g, invf.g], w=[u.g])
            SC = 2 * np.pi * (1 - 1e-6)
            BI = -np.pi * (1 - 1e-6)
            ui = sb0("ang_i", [128, 2048], mybir.dt.int32)
            m1 = sb0("ang_m1", [128, 2048], F32)

            def table(dst, shift):
                if shift:
                    k.ts('dve', u2.t[:], u.t[:], shift, None, ALU.add, None, r=[u.g], w=[u2.g])
                    src = u2
                else:
                    src = u
                k.cp('dve', ui.t[:], src.t[:], r=[src.g], w=[ui.g])
                k.cp('dve', m1.t[:], ui.t[:], r=[ui.g], w=[m1.g])
                k.tt('dve', u2.t[:], src.t[:], m1.t[:], ALU.subtract, r=[src.g, m1.g], w=[u2.g])
                k.ts('dve', m1.t[:], u2.t[:], 0.5, None, ALU.is_gt, None, r=[u2.g], w=[m1.g])
                k.tt('dve', u2.t[:], u2.t[:], m1.t[:], ALU.subtract, r=[m1.g], w=[u2.g])
                k.ts('dve', m1.t[:], u2.t[:], -0.5, None, ALU.is_lt, None, r=[u2.g], w=[m1.g])
                k.tt('dve', u2.t[:], u2.t[:], m1.t[:], ALU.add, r=[m1.g], w=[u2.g])
                k.act(dst.t[:], u2.t[:], AF.Sin, r=[u2.g], w=[dst.g], scale=SC)

            table(sintab, 0.0)
            table(costab, 0.25)
            S.barrier()
            stop_here('tables')

        def rope(X, Xg, N, cs, csg, rp, rpg):
            k.mm(rp[0:32, 0:N], rmat.t[:, :], X, True, True, r=[Xg, rmat.g], w=[rpg])
            k.tt('dve', rt1.t[0:32, 0:N], X[0:32], cs[0:32, 0, 0:N], ALU.mult, r=[Xg, csg], w=[rt1.g])
            k.tt('dve', rt2.t[0:32, 0:N], rp[0:32, 0:N], cs[0:32, 1, 0:N], ALU.mult, r=[rpg, csg], w=[rt2.g])
            k.tt('pool', X[0:32], rt1.t[0:32, 0:N], rt2.t[0:32, 0:N], ALU.add, r=[rt1.g, rt2.g], w=[Xg])

        def load_cs(cs, tok0, N):
            a = tok0 // 2048
            off = tok0 % 2048
            k.dma(cs.t[0:32, 0, 0:N], costab.t[32 * a:32 * a + 32, off:off + N], r=[costab.g], w=[cs.g])
            k.dma(cs.t[0:32, 1, 0:N], sintab.t[32 * a:32 * a + 32, off:off + N], r=[sintab.g], w=[cs.g])

        with ExitStack() as es1:
            sb1, _ = _mk(nc, es1)
            wst = [sb1(f"wst{j}", [128, NCOLS], F32) for j in range(2)]
            for kc in range(16):
                st = wst[kc % 2]
                k.dma(st.t[:], wc[kc * 128:(kc + 1) * 128, :], w=[st.g])
                k.ts('dve' if kc % 2 == 0 else 'pool', Wb.t[:, kc, :], st.t[:], normw.t[:, kc:kc + 1], None,
                     ALU.mult, None, r=[st.g, normw.g], w=[Wb.g])
            S.barrier()
            stop_here('weights')

        with ExitStack() as esA:
            sbA, psA = _mk(nc, esA)
            kcmpT = sbA("kcmpT", [128, S_LEN], BF16)
            vcmpT = sbA("vcmpT", [128, S_LEN], BF16)
            if NTILES < 64:
                k.ms('pool', kcmpT.t[:], 0.0, w=[kcmpT.g])
                k.ms('pool', vcmpT.t[:], 0.0, w=[vcmpT.g])
            WbA = sbA("WbA", [128, 16, 256], BF16)
            junkA = sbA("junkA", [128, 2048], BF16)
            with ExitStack() as es1b:
                sb1b, _ = _mk(nc, es1b)
                wstc = [sb1b(f"wstc{j}", [128, 256], F32) for j in range(2)]
                for kc in range(16):
                    st = wstc[kc % 2]
                    k.dma(st.t[:], wcc[kc * 128:(kc + 1) * 128, :], w=[st.g])
                    k.ts('dve' if kc % 2 == 0 else 'pool', WbA.t[:, kc, :], st.t[:], normw.t[:, kc:kc + 1], None,
                         ALU.mult, None, r=[st.g, normw.g], w=[WbA.g])
                S.barrier()
            with ExitStack() as esA1:
                sbA1, psA1 = _mk(nc, esA1)
                xts = [sbA1(f"xt{j}", [128, D], F32) for j in range(2)]
                xns = [sbA1(f"xn{j}", [128, D], BF16) for j in range(2)]
                xnTs = [sbA1(f"xnTa{j}", [128, 16, 128], BF16) for j in range(2)]
                css = [sbA1(f"csA{j}", [32, 2, 128], F32) for j in range(2)]
                tps = [psA1(f"tpA{j}", [128, 8, 128], BF16) for j in range(2)]
                tpg = [PReg(), PReg()]
                ca = psA1("caA", [128, 256], F32)
                cag = [PReg()] * 2
                rp = psA1("rpA", [128, 512], F32)
                rpg = PReg()
                for n in range(NTILES):
                    xt, xn, xnT, cs = xts[n % 2], xns[n % 2], xnTs[n % 2], css[n % 2]
                    k.dma(xt.t[:], x[128 * n:128 * (n + 1), :], w=[xt.g])
                    load_cs(cs, 128 * n, 128)
                    _norm_transpose(k, xt, xn, [tps[0], tps[1]], tpg, xnT.t, xnT.g, ident, junkA, ss, rs, rs2)
                    k.dma(xnT_d[n].rearrange("p (k t) -> p k t", k=16), xnT.t[:, :, :], r=[xnT.g], w=[])
                    for ci in range(2):
                        for kc in range(16):
                            k.mm(ca[:, ci * 128:(ci + 1) * 128], WbA.t[:, kc, ci * 128:(ci + 1) * 128],
                                 xnT.t[:, kc, :], kc == 0, kc == 15, r=[WbA.g, xnT.g], w=[cag[ci]])
                    k.cp('act', kcmpT.t[:, 128 * n:128 * (n + 1)], ca[:, 0:128], r=[cag[0]], w=[kcmpT.g])
                    rope(kcmpT.t[:, 128 * n:128 * (n + 1)], kcmpT.g, 128, cs.t, cs.g, rp, rpg)
                    k.cp('dve', vcmpT.t[:, 128 * n:128 * (n + 1)], ca[:, 128:256], r=[cag[1]], w=[vcmpT.g])
                S.barrier()
                stop_here('passA')

            with ExitStack() as esM:
                sbM, psM = _mk(nc, esM)
                w1st = [sbM(f"w1st{j}", [128, 4, 512], F32) for j in range(2)]
                w1b = [sbM(f"w1b{j}", [128, 4, 512], BF16) for j in range(2)]
                hacc = [psM(f"hacc{j}", [128, 512], F32) for j in range(4)]
                hag = [PReg() for _ in range(4)]
                pbs = [psM(f"pbias{j}", [128, 512], F32) for j in range(4)]
                pbg = [PReg() for _ in range(4)]
                posf = sbM("posf", [128, 32], F32)
                posb = sbM("posb", [128, 32], BF16)
                b1s = sbM("b1s", [128, 4], F32)
                btot = sbM("btot", [128, 4], F32)
                w2f = sbM("w2f", [128, 4, 128], F32)
                w2b = sbM("w2b", [128, 4, 128], BF16)
                hb = sbM("hb", [128, 512], F32)
                t1 = sbM("mt1", [128, 512], F32)
                t2 = sbM("mt2", [128, 512], F32)
                hT = sbM("hT", [128, 4, 512], BF16)
                k.ms('pool', hT.t[:, :, :], 0.0, w=[hT.g])
                k.ms('pool', kcT.t[:, :], 0.0, w=[kcT.g])
                k.ms('pool', vc.t[:, :, :], 0.0, w=[vc.g])
                k.ms('pool', vc.t[:, :, 128:129], 1.0, w=[vc.g])
                lgc = 0
                for kv in "kv":
                    src = kcmpT if kv == "k" else vcmpT
                    src3 = src.t[:, :].rearrange("p (c s) -> p c s", s=16)
                    W = cw[kv]
                    k.dma(posf.t[:], W["pos"], w=[posf.g])
                    k.cp('pool', posb.t[:], posf.t[:], r=[posf.g], w=[posb.g])
                    k.dma(b1s.t[:], W["b1"], w=[b1s.g])
                    k.dma(w2f.t[:, :, :], W["w2"].rearrange("(c p) d -> p c d", p=128), w=[w2f.g])
                    k.cp('pool', w2b.t[:, :, :], w2f.t[:, :, :], r=[w2f.g], w=[w2b.g])
                    w1v = W["w1"].rearrange("(l d) h -> d l h", d=128)
                    for lg in range(8):
                        st, wb = w1st[lgc % 2], w1b[lgc % 2]
                        lgc += 1
                        k.dma(st.t[:, :, :], w1v[:, 4 * lg:4 * lg + 4, :], w=[st.g])
                        k.cp('dve' if lg % 2 == 0 else 'pool', wb.t[:, :, :], st.t[:, :, :], r=[st.g], w=[wb.g])
                        for ll in range(4):
                            l = 4 * lg + ll
                            rhs = src3[:, (l // 16):(l // 16) + 511, l % 16]
                            for hc in range(4):
                                k.mm(hacc[hc][:, 0:511], wb.t[:, ll, hc * 128:(hc + 1) * 128], rhs, l == 0, l == 31,
                                     r=[wb.g, src.g], w=[hag[hc]])
                                k.mm(pbs[hc][:, 0:1], wb.t[:, ll, hc * 128:(hc + 1) * 128], posb.t[:, l:l + 1],
                                     l == 0, l == 31, r=[wb.g, posb.g], w=[pbg[hc]],
                                     inc=(l == 31 or (ll == 3 and hc == 3)))
                    for hc in range(4):
                        k.tt('dve', btot.t[:, hc:hc + 1], pbs[hc][:, 0:1], b1s.t[:, hc:hc + 1], ALU.add,
                             r=[pbg[hc], b1s.g], w=[btot.g])
                    for hc in range(4):
                        k.act(hb.t[:, 0:511], hacc[hc][:, 0:511], AF.Identity, r=[hag[hc], btot.g], w=[hb.g],
                              bias=btot.t[:, hc:hc + 1])
                        k.tt('dve', t1.t[:, 0:511], hb.t[:, 0:511], hb.t[:, 0:511], ALU.mult, r=[hb.g], w=[t1.g])
                        k.ts('dve', t1.t[:, 0:511], t1.t[:, 0:511], 0.044715, 1.0, ALU.mult, ALU.add, r=[], w=[t1.g])
                        k.tt('dve', t1.t[:, 0:511], t1.t[:, 0:511], hb.t[:, 0:511], ALU.mult, r=[hb.g], w=[t1.g])
                        k.act(t2.t[:, 0:511], t1.t[:, 0:511], AF.Sigmoid, r=[t1.g], w=[t2.g], scale=1.5957691216057308)
                        k.tt('dve', hT.t[:, hc, 0:511], hb.t[:, 0:511], t2.t[:, 0:511], ALU.mult, r=[hb.g, t2.g],
                             w=[hT.g])
                    if kv == "k":
                        for hc in range(4):
                            k.mm(hacc[0][:, 0:511], w2b.t[:, hc, :], hT.t[:, hc, 0:511], hc == 0, hc == 3,
                                 r=[w2b.g, hT.g], w=[hag[0]])
                        k.cp('act', kcT.t[:, 0:511], hacc[0][:, 0:511], r=[hag[0]], w=[kcT.g])
                    else:
                        for ct in range(4):
                            for hc in range(4):
                                k.mm(hacc[ct][:, 0:128], hT.t[:, hc, 128 * ct:128 * (ct + 1)], w2b.t[:, hc, :],
                                     hc == 0, hc == 3, r=[w2b.g, hT.g], w=[hag[ct]])
                            k.cp('act', vc.t[:, ct, 0:128], hacc[ct][:, 0:128], r=[hag[ct]], w=[vc.g])
                S.barrier()
                stop_here('mlp')
            S.barrier()

        with ExitStack() as esB:
            sbB, psB = _mk(nc, esB)
            bexp = sbB("bexp_s", [128, 8192], BF16)
            ovx = sbB("ovx_s", [128, 4, 132], BF16)
            mtw = sbB("mtw_s", [128, 2304], BF16)
            m1w = sbB("m1w_s", [128, 256], F32)
            m2w = sbB("m2w_s", [128, 256], F32)
            lb = sbB("lb_s", [128, 128], F32)
            lb1 = sbB("lb1_s", [128, 128], F32)
            oml = sbB("oml_s", [128, 128], F32)
            hnw = sbB("hnw_s", [128, 128], F32)
            for b_, d_ in ((bexp, bexp_d), (ovx, ovx_d), (mtw, mtw_d), (m1w, m1w_d), (m2w, m2w_d), (lb, lb0_d),
                           (lb1, lb1_d), (hnw, hnw_d)):
                k.dma(b_.t[:], d_, w=[b_.g])
            k.tt('dve', lb.t[:], lb.t[:], lb1.t[:], ALU.subtract, r=[lb1.g], w=[lb.g])
            k.act(lb.t[:], lb.t[:], AF.Sigmoid, r=[], w=[lb.g])
            k.ts('dve', oml.t[:], lb.t[:], -1.0, 1.0, ALU.mult, ALU.add, r=[lb.g], w=[oml.g])

            kslcT = sbB("kslcT", [128, S_LEN], BF16)
            ksg = [Reg() for _ in range(16)]
            vslc = sbB("vslc", [128, 64, 132], BF16)
            vsg = [Reg() for _ in range(64)]
            kwinT = sbB("kwinT", [128, 1024], BF16)
            kwg = [Reg(), Reg()]
            vwin = sbB("vwin", [128, 8, 132], BF16)
            vwg = [Reg() for _ in range(8)]
            k.ms('pool', vslc.t[:, :, 128:132], 0.0, w=vsg)
            k.ms('pool', vslc.t[:, :, 128:129], 1.0, w=vsg)
            k.ms('pool', vwin.t[:, :, 128:132], 0.0, w=vwg)
            k.ms('pool', vwin.t[:, :, 128:129], 1.0, w=vwg)

            stop_here('b_setup')
            xnT = sbB("xnTb", [128, 16, 512], BF16)
            csB = sbB("csB", [32, 2, 512], F32)
            qtmp = sbB("qtmp", [128, 512], BF16)
            qT = sbB("qT", [128, 2048], BF16)
            hq = sbB("hq", [128, 512], F32)
            sgb = [sbB(f"sgb{j}", [128, 128], F32) for j in range(4)]
            vhg = [sbB(f"vhg{j}", [128, 128], BF16) for j in range(4)]
            zs = [sbB(f"zs{j}", [128, 128], F32) for j in range(4)]
            zn = [sbB(f"zn{j}", [128, 256], F32) for j in range(4)]
            gs = [sbB(f"gs{j}", [128, 16], F32) for j in range(4)]
            fb = sbB("fb", [128, 128], F32)
            gl = sbB("gl", [128, 128], F32)
            omf = sbB("omf", [128, 128], F32)
            enb = sbB("enb", [128, 128], F32)
            ebt = sbB("ebt", [128, 128], F32)
            ktm = sbB("ktm", [128, 128], BF16)
            kTs = sbB("kTs", [128, 128], BF16)
            qfT = sbB("qfT", [128, 128], BF16)
            qa = sbB("qa", [128, 128], BF16)
            qb = sbB("qb", [128, 128], BF16)
            Am = sbB("Am", [128, 128], BF16)
            Sf = sbB("Sf", [128, 128], F32)
            Sf2 = sbB("Sf2", [128, 128], F32)
            S1 = sbB("S1", [128, 128], F32)
            SAbf = [sbB(f"SAbf{j}", [128, 128], BF16) for j in range(2)]
            SBbf = sbB("SBbf", [128, 128], BF16)
            hz = sbB("hz", [128, 128], F32)
            ssh = sbB("ssh", [128, 1], F32)
            rsh = sbB("rsh", [128, 1], F32)
            rsh2 = sbB("rsh2", [128, 1], F32)
            og = sbB("og", [128, 128], BF16)
            oxh = sbB("oxh", [128, 512], BF16)
            oxn = sbB("oxn", [128, 2, 512], BF16)
            k.ms('pool', qa.t[:], 0.0, w=[qa.g])
            k.ms('pool', qb.t[:], 0.0, w=[qb.g])
            k.ms('pool', Sf.t[:], 0.0, w=[Sf.g])
            k.ms('pool', SAbf[0].t[:], 0.0, w=[SAbf[0].g])
            Ec = [sbB(f"Ec{j}", [128, 512], BF16) for j in range(2)] * 2
            Ecm = [sbB(f"Ecm{j}", [128, 512], BF16) for j in range(4)]
            Es = [sbB(f"Es{j}", [128, 512], BF16) for j in range(3)]
            Esm = [sbB(f"Esm{j}", [128, 512], BF16) for j in range(3)]
            mkd = [sbB(f"mkd{j}", [128, 128], F32) for j in range(2)]
            rz4 = sbB("rz4", [128, 4], F32)
            imps = sbB("imps", [128, 128], F32)
            sc = sbB("sc", [128, 128], F32)
            sc2 = sbB("sc2", [128, 128], F32)
            m8 = sbB("m8", [128, 8], F32)
            m8b = sbB("m8b", [128, 8], F32)
            thr = sbB("thr", [128, 1], F32)
            selb = sbB("selb", [128, 128], BF16)
            selTs = [sbB(f"selT{j}", [128, 128], BF16) for j in range(2)]
            ocmp = [sbB(f"ocmp{j}", [128, 264], F32) for j in range(2)]
            z3 = sbB("z3", [128, 3], F32)
            a3 = sbB("a3", [128, 3], F32)
            acc = sbB("acc", [128, 128], F32)
            onb = sbB("onb", [128, 128], BF16)

            BG = [psB(f"BG{j}", [128, 512], F32) for j in range(2)]
            BGg = [PReg(), PReg()]
            RP = psB("RPb", [128, 512], F32)
            RPg = PReg()
            SM = [psB(f"SM{j}", [128, 4, 128], F32) for j in range(2)]
            SMg = [[PReg()] * 4 for _ in range(2)]
            TPB = psB("TPB", [128, 8, 128], BF16)
            TPg = [PReg()] * 8
            OA = [psB(f"OA{j}", [128, 512], F32) for j in range(2)]
            OAg = [[PReg()] * 3 for _ in range(2)]
            bgc = [0]

            def nbg():
                j = bgc[0] % 2
                bgc[0] += 1
                return BG[j], BGg[j]

            hl = [None]
            es_c = [0]
            ipc = [0]

            def nbg4():
                j = ipc[0] % 4
                ipc[0] += 1
                return [(BG[0], BGg[0]), (BG[1], BGg[1]), (OA[0], OAg[0][0]), (OA[1], OAg[1][0])][j]

            for T in range(NMT):
                for i in range(4):
                    k.dma(xnT.t[:, :, 128 * i:128 * (i + 1)], xnT_d[4 * T + i].rearrange("p (k t) -> p k t", k=16),
                          w=[xnT.g])
                load_cs(csB, 512 * T, 512)
                stop_here('b_load')
                for i in range(4):
                    n = 4 * T + i
                    bg, bgg = nbg4()
                    for kc in range(16):
                        k.mm(bg[:, 0:512], xnT.t[:, kc, 128 * i:128 * (i + 1)], Wb.t[:, kc, TM0:TM0 + 512], kc == 0,
                             kc == 15, r=[xnT.g, Wb.g], w=[bgg])
                    stop_here('tm_a')
                    k.act(sgb[i].t[:], bg[:, 0:128], AF.Sigmoid, r=[], w=[bgg, sgb[i].g])
                    stop_here('tm_a1')
                    k.cp('dve', vhg[i].t[:], bg[:, 128:256], r=[], w=[bgg, vhg[i].g])
                    stop_here('tm_a2')
                    k.act(zs[i].t[:], bg[:, 256:384], AF.Sigmoid, r=[], w=[bgg, zs[i].g])
                    k.tt('dve', zs[i].t[:], zs[i].t[:], bg[:, 256:384], ALU.mult, r=[], w=[bgg, zs[i].g])
                    k.cp('dve', vslc.t[:, n, 0:128], bg[:, 384:512], r=[], w=[bgg, vsg[n]])
                    stop_here('tm_b')
                    bg, bgg = nbg4()
                    for kc in range(16):
                        k.mm(bg[:, 0:400], xnT.t[:, kc, 128 * i:128 * (i + 1)], Wb.t[:, kc, TM0 + 512:TM0 + 912],
                             kc == 0, kc == 15, r=[xnT.g, Wb.g], w=[bgg])
                    stop_here('tm_c')
                    k.cp('dve', vwin.t[:, n % 8, 0:128], bg[:, 0:128], r=[], w=[bgg, vwg[n % 8]])
                    k.act(zn[i].t[:], bg[:, 128:384], AF.Sigmoid, r=[], w=[bgg, zn[i].g])
                    k.tt('dve', zn[i].t[:], zn[i].t[:], bg[:, 128:384], ALU.mult, r=[], w=[bgg, zn[i].g])
                    k.act(gs[i].t[:, 0:16], bg[:, 384:400], AF.Sigmoid, r=[], w=[bgg, gs[i].g])
                stop_here('b_tm')
                for ch in range(7):
                    bg, bgg = nbg4()
                    for kc in range(16):
                        k.mm(bg[:, 0:512], Wb.t[:, kc, ch * 128:(ch + 1) * 128], xnT.t[:, kc, :], kc == 0, kc == 15,
                             r=[xnT.g, Wb.g], w=[bgg])
                    if ch == 0:
                        k.cp('act', hq.t[:], bg[:, 0:512], r=[bgg], w=[hq.g])
                    elif ch <= 4:
                        h = ch - 1
                        k.act(qtmp.t[:], bg[:, 0:512], AF.Identity, r=[bgg], w=[qtmp.g], scale=float(128 ** -0.5))
                        rope(qtmp.t[:, :], qtmp.g, 512, csB.t, csB.g, RP, RPg)
                        k.cp('pool', qT.t[:, :].rearrange("p (i h q) -> p i h q", i=4, h=4)[:, :, h, :],
                             qtmp.t[:, :].rearrange("p (i q) -> p i q", i=4), r=[qtmp.g], w=[qT.g])
                    elif ch == 5:
                        k.cp('act', kslcT.t[:, 512 * T:512 * (T + 1)], bg[:, 0:512], r=[bgg], w=[ksg[T]])
                        rope(kslcT.t[:, 512 * T:512 * (T + 1)], ksg[T], 512, csB.t, csB.g, RP, RPg)
                    else:
                        o_ = (T % 2) * 512
                        k.cp('act', kwinT.t[:, o_:o_ + 512], bg[:, 0:512], r=[bgg], w=[kwg[T % 2]])
                        rope(kwinT.t[:, o_:o_ + 512], kwg[T % 2], 512, csB.t, csB.g, RP, RPg)

                stop_here('b_fm')
                def hgrn_tile(i):
                    n = 4 * T + i
                    p = n % 2
                    k.tt('dve', fb.t[:], sgb[i].t[:], oml.t[:], ALU.mult, r=[sgb[i].g, oml.g], w=[fb.g])
                    k.tt('dve', fb.t[:], fb.t[:], lb.t[:], ALU.add, r=[lb.g], w=[fb.g])
                    k.act(gl.t[:], fb.t[:], AF.Ln, r=[fb.g], w=[gl.g])
                    k.ts('dve', omf.t[:], fb.t[:], -1.0, 1.0, ALU.mult, ALU.add, r=[fb.g], w=[omf.g])
                    yield
                    k.mm(SM[0][:, 0, :], ublk.t[:], gl.t[:], True, True, r=[ublk.g, gl.g], w=[SMg[0][0]])
                    k.mm(SM[0][:, 1, :], gl.t[:], ublk.t[:], True, True, r=[ublk.g, gl.g], w=[SMg[0][1]])
                    yield
                    k.act(enb.t[:], SM[0][:, 0, :], AF.Exp, r=[SMg[0][0]], w=[enb.g], scale=-1.0)
                    k.tt('dve', ktm.t[:], omf.t[:], enb.t[:], ALU.mult, r=[omf.g, enb.g], w=[ktm.g])
                    yield
                    k.act(ebt.t[:], SM[0][:, 1, :], AF.Exp, r=[SMg[0][1]], w=[ebt.g])
                    k.tt('dve', qfT.t[:], hq.t[:, 128 * i:128 * (i + 1)], ebt.t[:], ALU.mult, r=[hq.g, ebt.g],
                         w=[qfT.g])
                    k.cp('pool', qa.t[:, 0:64], qfT.t[:, 0:64], r=[qfT.g], w=[qa.g])
                    k.cp('pool', qb.t[:, 64:128], qfT.t[:, 64:128], r=[qfT.g], w=[qb.g])
                    yield
                    k.tr(TPB[:, 0, :], ktm.t[:], ident.t[:], r=[ktm.g, ident.g], w=[TPg[0]])
                    k.cp('act', kTs.t[:], TPB[:, 0, :], r=[TPg[0]], w=[kTs.g])
                    yield
                    k.mm(SM[0][:, 2, :], kTs.t[:], qfT.t[:], True, True, r=[kTs.g, qfT.g], w=[SMg[0][2]])
                    k.tt('dve', Am.t[:], SM[0][:, 2, :], ublk.t[:], ALU.mult, r=[SMg[0][2], ublk.g], w=[Am.g])
                    yield
                    k.mm(SM[1][:, 0, :], ktm.t[0:64, :], vhg[i].t[0:64, :], True, True, r=[ktm.g, vhg[i].g],
                         w=[SMg[1][0]])
                    k.mm(SM[1][:, 1, :], ktm.t[64:128, :], vhg[i].t[64:128, :], True, True, r=[ktm.g, vhg[i].g],
                         w=[SMg[1][1]])
                    yield
                    k.ts('dve', S1.t[:], Sf.t[:], ebt.t[:, 63:64], None, ALU.mult, None, r=[Sf.g, ebt.g], w=[S1.g])
                    k.stt('dve', Sf2.t[:], SM[1][:, 0, :], ebt.t[:, 63:64], S1.t[:], ALU.mult, ALU.add,
                          r=[SMg[1][0], ebt.g, S1.g], w=[Sf2.g])
                    yield
                    k.cp('pool', SBbf.t[:], Sf2.t[:], r=[Sf2.g], w=[SBbf.g])
                    k.ts('dve', S1.t[:], Sf2.t[:], ebt.t[:, 127:128], None, ALU.mult, None, r=[Sf2.g, ebt.g],
                         w=[S1.g])
                    k.stt('dve', Sf.t[:], SM[1][:, 1, :], ebt.t[:, 127:128], S1.t[:], ALU.mult, ALU.add,
                          r=[SMg[1][1], ebt.g, S1.g], w=[Sf.g])
                    k.cp('pool', SAbf[1 - p].t[:], Sf.t[:], r=[Sf.g], w=[SAbf[1 - p].g])
                    yield
                    k.mm(SM[0][:, 3, :], Am.t[:], vhg[i].t[:], True, False, r=[Am.g, vhg[i].g], w=[SMg[0][3]], inc=False)
                    k.mm(SM[0][:, 3, :], qa.t[:], SAbf[p].t[:], False, False, r=[qa.g, SAbf[p].g], w=[SMg[0][3]],
                         inc=False)
                    k.mm(SM[0][:, 3, :], qb.t[:], SBbf.t[:], False, True, r=[qb.g, SBbf.g], w=[SMg[0][3]])
                    k.act(junk.t[:, :], SM[0][:, 3, :], AF.Square, r=[SMg[0][3]], w=[junk.g, ssh.g],
                          accum_out=ssh.t[:])
                    k.ts('dve', rsh.t[:], ssh.t[:], 1.0 / 128, 1e-6, ALU.mult, ALU.add, r=[ssh.g], w=[rsh.g])
                    k.act(rsh.t[:], rsh.t[:], AF.Ln, r=[], w=[rsh.g])
                    k.act(rsh2.t[:], rsh.t[:], AF.Exp, r=[rsh.g], w=[rsh2.g], scale=-0.5)
                    k.tt('pool', hz.t[:], hnw.t[:], zs[i].t[:], ALU.mult, r=[hnw.g, zs[i].g], w=[hz.g])
                    k.stt('dve', og.t[:], SM[0][:, 3, :], rsh2.t[:, 0:1], hz.t[:], ALU.mult, ALU.mult,
                          r=[SMg[0][3], rsh2.g, hz.g], w=[og.g])
                    yield
                    k.tr(TPB[:, 1, :], og.t[:], ident.t[:], r=[og.g, ident.g], w=[TPg[1]])
                    k.cp('act', oxh.t[:, 128 * i:128 * (i + 1)], TPB[:, 1, :], r=[TPg[1]], w=[oxh.g])

                for i in range(4):
                    n = 4 * T + i
                    def cmp_topk(i):
                        n = 4 * T + i
                        qall = qT.t[:, i * 512:(i + 1) * 512]
                        nct = (8 * n + 6) // 128 + 1
                        for ct in range(nct):
                            if ct > 0:
                                yield
                            bg, bgg = nbg()
                            k.mm(bg[:, 0:512], kcT.t[:, 128 * ct:128 * (ct + 1)], qall, True, True, r=[kcT.g, qT.g],
                                 w=[bgg])
                            k.act(Ec[ct].t[:], bg[:, 0:512], AF.Exp, r=[bgg], w=[Ec[ct].g])
                            st = min(128 * n - 2048 * ct + 96, 2176)
                            for h in range(4):
                                k.tt('pool', Ecm[ct].t[:, h * 128:(h + 1) * 128], Ec[ct].t[:, h * 128:(h + 1) * 128],
                                     mtw.t[:, st:st + 128], ALU.mult, r=[Ec[ct].g, mtw.g], w=[Ecm[ct].g])
                        yield
                        for hh in range(2):
                            for ct in range(nct):
                                k.mm(RP[:, hh * 132:(hh + 1) * 132], Ecm[ct].t[:, (HL + hh) * 128:(HL + hh + 1) * 128], vc.t[:, ct, :],
                                     ct == 0, ct == nct - 1, r=[Ecm[ct].g, vc.g], w=[RPg])
                        k.cp('act', ocmp[i % 2].t[:, 0:264], RP[:, 0:264], r=[RPg], w=[ocmp[i % 2].g])
                        for pair in range(2):
                            yield
                            bg, bgg = nbg()
                            for h2 in range(2):
                                h = 2 * pair + h2
                                for ct in range(nct):
                                    k.mm(bg[:, h2 * 132:(h2 + 1) * 132], Ecm[ct].t[:, h * 128:(h + 1) * 128],
                                         ovx.t[:, ct, :], ct == 0, ct == nct - 1, r=[Ecm[ct].g, ovx.g], w=[bgg])
                            for h2 in range(2):
                                h = 2 * pair + h2
                                k.ts('dve', rz4.t[:, h:h + 1], bg[:, h2 * 132 + 128:h2 * 132 + 129], 1e-30, None, ALU.max,
                                     None, r=[bgg], w=[rz4.g])
                                k.rcp(rz4.t[:, h:h + 1], rz4.t[:, h:h + 1], r=[], w=[rz4.g])
                                if h == 0:
                                    k.ts('dve', imps.t[:], bg[:, 0:128], rz4.t[:, 0:1], None, ALU.mult, None,
                                         r=[bgg, rz4.g], w=[imps.g])
                                else:
                                    k.stt('dve', imps.t[:], bg[:, h2 * 132:h2 * 132 + 128], rz4.t[:, h:h + 1], imps.t[:],
                                          ALU.mult, ALU.add, r=[bgg, rz4.g], w=[imps.g])
                        yield
                        w0 = 128 - 2 * n
                        k.tt('dve', sc.t[:], imps.t[:], m1w.t[:, w0:w0 + 128], ALU.mult, r=[imps.g, m1w.g], w=[sc.g])
                        k.tt('dve', sc.t[:], sc.t[:], m2w.t[:, w0:w0 + 128], ALU.add, r=[m2w.g], w=[sc.g])
                        k.ms('dve', sc.t[:, 0:1], 3e30, w=[sc.g])
                        yield
                        S.op('dve', lambda e: e.max(out=m8.t[:], in_=sc.t[:]), r=[sc.g], w=[m8.g])
                        S.op('dve', lambda e: e.match_replace(out=sc2.t[:], in_to_replace=m8.t[:], in_values=sc.t[:],
                                                              imm_value=-2.0), r=[sc.g, m8.g], w=[sc2.g])
                        S.op('dve', lambda e: e.max(out=m8b.t[:], in_=sc2.t[:]), r=[sc2.g], w=[m8b.g])
                        k.ts('dve', thr.t[:], m8b.t[:, 7:8], 0.0, None, ALU.max, None, r=[m8b.g], w=[thr.g])
                        k.ts('dve', selb.t[:], sc.t[:], thr.t[:, 0:1], None, ALU.is_ge, None, r=[sc.g, thr.g], w=[selb.g])
                        yield
                        k.tr(TPB[:, 2, :], selb.t[:], ident.t[:], r=[selb.g, ident.g], w=[TPg[2]])
                        k.cp('act', selTs[i % 2].t[:], TPB[:, 2, :], r=[TPg[2]], w=[selTs[i % 2].g])
                    if i == 0:
                        for _ in cmp_topk(0):
                            pass
                    gens = [hgrn_tile(i)] + ([cmp_topk(i + 1)] if i < 3 else [])
                    selT = selTs[i % 2]
                    qown = qT.t[:, i * 512 + HL * 128:i * 512 + HL * 128 + 256]
                    jobs = []
                    for br in range(2):
                        kts = list(range(0, n + 1)) if br == 0 else list(range(max(0, n - 4), n + 1))
                        for p0 in range(0, len(kts), 2):
                            jobs.append((br, kts, kts[p0:p0 + 2]))

                    def stage_a(job, pi):
                        br, kts, pk = job
                        bg, bgg = nbg()
                        e_s = Es[es_c[0] % 3]
                        e_m = Esm[es_c[0] % 3]
                        es_c[0] += 1
                        mb = pi % 2
                        for j, kt in enumerate(pk):
                            if br == 0:
                                lhs = kslcT.t[:, 128 * kt:128 * (kt + 1)]
                                lg_ = ksg[kt // 4]
                            else:
                                o_ = (kt % 8) * 128
                                lhs = kwinT.t[:, o_:o_ + 128]
                                lg_ = kwg[(kt // 4) % 2]
                            k.mm(bg[:, j * 256:(j + 1) * 256], lhs, qown, True, True, r=[lg_, qT.g], w=[bgg],
                                 inc=(j == len(pk) - 1))
                        if br == 0:
                            for j, kt in enumerate(pk):
                                k.mm(SM[mb][:, 2 + j, :], bexp.t[:, 128 * kt:128 * (kt + 1)], selT.t[:], True, True,
                                     r=[bexp.g, selT.g], w=[SMg[mb][2 + j]])
                        k.act(e_s.t[:, 0:256 * len(pk)], bg[:, 0:256 * len(pk)], AF.Exp, r=[bgg], w=[e_s.g])
                        for j, kt in enumerate(pk):
                            if br == 0:
                                if kt == n:
                                    k.tt('dve', mkd[j].t[:], SM[mb][:, 2 + j, :], tri.t[:], ALU.mult,
                                         r=[SMg[mb][2 + j], tri.g], w=[mkd[j].g])
                                    mk, mkg = mkd[j].t[:], mkd[j].g
                                else:
                                    mk, mkg = SM[mb][:, 2 + j, :], SMg[mb][2 + j]
                            elif kt == n:
                                mk, mkg = tri.t[:], tri.g
                            elif kt == n - 4:
                                mk, mkg = trilo.t[:], trilo.g
                            else:
                                mk = None
                            for hh in range(2):
                                c0 = j * 256 + hh * 128
                                if mk is None:
                                    k.cp('pool', e_m.t[:, c0:c0 + 128], e_s.t[:, c0:c0 + 128], r=[e_s.g], w=[e_m.g])
                                else:
                                    k.tt('dve', e_m.t[:, c0:c0 + 128], e_s.t[:, c0:c0 + 128], mk, ALU.mult,
                                         r=[e_s.g, mkg], w=[e_m.g])
                        return e_m

                    def stage_b(job, e_m):
                        br, kts, pk = job
                        for j, kt in enumerate(pk):
                            for hh in range(2):
                                c0 = j * 256 + hh * 128
                                if br == 0:
                                    rhs, rg = vslc.t[:, kt, :], vsg[kt]
                                else:
                                    rhs, rg = vwin.t[:, kt % 8, :], vwg[kt % 8]
                                last = (j == len(pk) - 1 and hh == 1)
                                k.mm(OA[hh][:, br * 132:(br + 1) * 132], e_m.t[:, c0:c0 + 128], rhs,
                                     kt == kts[0], kt == kts[-1], r=[e_m.g, rg], w=[OAg[hh][br]],
                                     inc=(last or kt == kts[-1]))

                    prev = None
                    for pi, job in enumerate(jobs):
                        em = stage_a(job, pi)
                        if prev is not None:
                            stage_b(*prev)
                        prev = (job, em)
                        for g_ in gens:
                            next(g_, None)
                    stage_b(*prev)
                    for g_ in gens:
                        for _ in g_:
                            pass
                    for hh in range(2):
                        k.ts('dve', z3.t[:, 0:1], ocmp[i % 2].t[:, hh * 132 + 128:hh * 132 + 129], 1e-30, None, ALU.max,
                             None, r=[ocmp[i % 2].g], w=[z3.g])
                        for bi, c_ in ((1, 128), (2, 260)):
                            k.ts('dve', z3.t[:, bi:bi + 1], OA[hh][:, c_:c_ + 1], 1e-30, None, ALU.max, None,
                                 r=[OAg[hh][bi - 1]], w=[z3.g])
                        k.rcp(z3.t[:, :], z3.t[:, :], r=[], w=[z3.g])
                        k.tt('dve', a3.t[:, :], z3.t[:, :], gs[i].t[:, 3 * hh:3 * hh + 3], ALU.mult,
                             r=[z3.g, gs[i].g], w=[a3.g])
                        k.ts('dve', acc.t[:], ocmp[i % 2].t[:, hh * 132:hh * 132 + 128], a3.t[:, 0:1], None, ALU.mult, None,
                             r=[ocmp[i % 2].g, a3.g], w=[acc.g])
                        k.stt('dve', acc.t[:], OA[hh][:, 0:128], a3.t[:, 1:2], acc.t[:], ALU.mult, ALU.add,
                              r=[OAg[hh][0], a3.g], w=[acc.g])
                        k.stt('dve', acc.t[:], OA[hh][:, 132:260], a3.t[:, 2:3], acc.t[:], ALU.mult, ALU.add,
                              r=[OAg[hh][1], a3.g], w=[acc.g])
                        k.tt('dve', onb.t[:], acc.t[:], zn[i].t[:, hh * 128:(hh + 1) * 128], ALU.mult,
                             r=[acc.g, zn[i].g], w=[onb.g])
                        k.tr(TPB[:, 3 + hh, :], onb.t[:], ident.t[:], r=[onb.g, ident.g], w=[TPg[3 + hh]])
                        k.cp('act', oxn.t[:, hh, 128 * i:128 * (i + 1)], TPB[:, 3 + hh, :], r=[TPg[3 + hh]], w=[oxn.g])
                store_ox(0, T, oxh.t[:, :], oxh.g)
                store_ox(4, T, oxn.t[:, 0, :], oxn.g)
                store_ox(8, T, oxn.t[:, 1, :], oxn.g)
            S.barrier()
        esP.close()

        if FUSED:
            ccs = es.enter_context(nc.semaphore("ccsem"))
            S.barrier()
            S.q['pool'].append(lambda e: e.collective_compute(
                "AllGather", ALU.bypass, replica_groups=[list(range(8))],
                ins=[oxd_t.ap()], outs=[gath_t.ap()]).then_inc(ccs))
            S.q['pool'].append(lambda e: e.wait_ge(ccs, 1))
            gth = Buf(None)
            dmy = sb_outer("dmy", [128, 8], F32)
            k.ms('pool', dmy.t[:], 0.0, w=[dmy.g, gth.g])

            def load_oT(oT, psx, sbx):
                selm = sbx("selm_s", [128, 8, 128], BF16)
                k.dma(selm.t[:, :, :], P2["selm"], w=[selm.g])
                gts = [sbx(f"gt{j}", [128, 512], BF16) for j in range(4)]
                sacc = [psx(f"sacc{j}", [128, 512], F32) for j in range(2)]
                saccg = [PReg(), PReg()]
                cnt = 0
                gi = 0
                for rho in range(8):
                    for cl in range(3):
                        chunk = rho if cl == 0 else 8 + 2 * rho + (cl - 1)
                        for half in range(2):
                            ac, acg = sacc[cnt % 2], saccg[cnt % 2]
                            cnt += 1
                            for a0 in range(4):
                                a = 4 * cl + a0
                                for jh in range(2):
                                    gt = gts[gi % 4]
                                    gi += 1
                                    r0 = ((rho * 12 + a) * 8 + 4 * jh) * 32
                                    k.dma(gt.t[:, :], gath[r0:r0 + 128, 512 * half:512 * (half + 1)], r=[gth.g],
                                          w=[gt.g])
                                    first = (a0 == 0 and jh == 0)
                                    last = (a0 == 3 and jh == 1)
                                    k.mm(ac[:, :], selm.t[:, 2 * a0 + jh, :], gt.t[:, :], first, last,
                                         r=[selm.g, gt.g], w=[acg], inc=True)
                            k.cp('act' if cnt % 2 == 0 else 'dve', oT.t[:, chunk, 512 * half:512 * (half + 1)], ac[:, :],
                                 r=[acg], w=[oT.g])

            _phase2(nc, S, k, P2, normw, ident, load_oT)
        S.finish()
        S.emit()


HL = 0


def build_p2():
    nc = bass.Bass("TRN2", target_bir_lowering=False)

    def din(name, shape, dt=F32):
        return nc.dram_tensor(name, list(shape), dt, kind="ExternalInput").ap()

    P2 = dict(xo=din("xo", [1024, D]), wp2=din("wp2", [16, 128, 56, 128]),
              wo=din("wo", [D, D]), fnw=din("fnw", [128, D]),
              y=nc.dram_tensor("y", [1024, D], F32, kind="ExternalOutput").ap())
    oo = din("oo", [3072, 1024], BF16)
    normw_d = din("normw", [128, 16])
    ident_d = din("ident", [128, 128], BF16)
    with ExitStack() as es:
        S = Sched(nc, es)
        k = K(S)
        sb, ps = _mk(nc, es)
        normw = sb("normw_s", [128, 16], F32)
        ident = sb("ident_s", [128, 128], BF16)
        k.dma(normw.t[:], normw_d, w=[normw.g])
        k.dma(ident.t[:], ident_d, w=[ident.g])

        def load_oT(oT, psx, sbx):
            for c in range(24):
                k.dma(oT.t[:, c, :], oo[128 * c:128 * (c + 1), :], w=[oT.g])

        _phase2(nc, S, k, P2, normw, ident, load_oT)
        S.finish()
        S.emit()
    return nc


def _phase2(nc, S, k, P2, normw, ident, load_oT):
    xo, wp2, wo, fnw_d, y = P2["xo"], P2["wp2"], P2["wo"], P2["fnw"], P2["y"]
    with ExitStack() as es:
        sb, ps = _mk(nc, es)
        mT = sb("p2mT", [128, 16, 1024], BF16)
        mTg = [Reg() for _ in range(16)]
        junk = sb("junk2", [128, 2048], BF16)
        ss = sb("ss2", [128, 1], F32)
        rs = sb("rs_2", [128, 1], F32)
        rs2 = sb("rs2_2", [128, 1], F32)

        with ExitStack() as esA:
            sbA, psA = _mk(nc, esA)
            xnT = sbA("p2xnT", [128, 16, 1024], BF16)
            oT = sbA("p2oT", [128, 24, 1024], BF16)
            with ExitStack() as esL:
                sbL, psL = _mk(nc, esL)
                load_oT(oT, psL, sbL)
                S.barrier()
            with ExitStack() as esA1:
                sbA1, psA1 = _mk(nc, esA1)
                xts = [sbA1(f"p2xt{j}", [128, D], F32) for j in range(2)]
                xns = [sbA1(f"p2xn{j}", [128, D], BF16) for j in range(2)]
                tps = [psA1(f"p2tp{j}", [128, 8, 128], BF16) for j in range(2)]
                tpg = [PReg(), PReg()]
                for n in range(8):
                    xt, xn = xts[n % 2], xns[n % 2]
                    k.dma(xt.t[:], xo[128 * n:128 * (n + 1), :], w=[xt.g])
                    _norm_transpose(k, xt, xn, tps, tpg, xnT.t[:, :, 128 * n:128 * (n + 1)], xnT.g, ident, junk, ss,
                                    rs, rs2)
                S.barrier()
            with ExitStack() as esA2:
                sbA2, psA2 = _mk(nc, esA2)
                wst = [sbA2(f"p2wst{j}", [128, 56, 128], F32) for j in range(2)]
                wbf = [sbA2(f"wbf{j}", [128, 56, 128], BF16) for j in range(2)]
                g1s = sbA2("g1s", [128, 512], F32)
                g2s = sbA2("g2s", [128, 512], F32)
                acc = [psA2(f"p2acc{j}", [128, 512], F32) for j in range(8)]
                accg = [PReg() for _ in range(8)]
                ac = [0]
                for fo in range(16):
                    st, wb = wst[fo % 2], wbf[fo % 2]
                    cs_ = slice(128 * fo, 128 * (fo + 1))
                    k.dma(st.t[:, :, :], wp2[fo], w=[st.g])
                    for kc in range(16):
                        k.ts('dve' if kc % 2 == 0 else 'pool', wb.t[:, kc, :], st.t[:, kc, :], normw.t[:, kc:kc + 1],
                             None, ALU.mult, None, r=[st.g, normw.g], w=[wb.g])
                        k.ts('pool' if kc % 2 == 0 else 'dve', wb.t[:, 16 + kc, :], st.t[:, 16 + kc, :],
                             normw.t[:, kc:kc + 1], None, ALU.mult, None, r=[st.g, normw.g], w=[wb.g])
                    k.cp('pool', wb.t[:, 32:56, :], st.t[:, 32:56, :], r=[st.g], w=[wb.g])
                    for half in range(2):
                        ts_ = slice(512 * half, 512 * (half + 1))
                        a4 = []
                        for j in range(4):
                            a4.append((acc[ac[0] % 8], accg[ac[0] % 8]))
                            ac[0] += 1
                        for kc in range(16):
                            k.mm(a4[0][0][:, :], wb.t[:, kc, :], xnT.t[:, kc, ts_], kc == 0, kc == 15,
                                 r=[wb.g, xnT.g], w=[a4[0][1]])
                        for kc in range(16):
                            k.mm(a4[1][0][:, :], wb.t[:, 16 + kc, :], xnT.t[:, kc, ts_], kc == 0, kc == 15,
                                 r=[wb.g, xnT.g], w=[a4[1][1]])
                        for kc in range(8):
                            k.mm(a4[2][0][:, :], wb.t[:, 32 + kc, :], oT.t[:, kc, ts_], kc == 0, kc == 7,
                                 r=[wb.g, oT.g], w=[a4[2][1]])
                        for kc in range(16):
                            k.mm(a4[3][0][:, :], wb.t[:, 40 + kc, :], oT.t[:, 8 + kc, ts_], kc == 0, kc == 15,
                                 r=[wb.g, oT.g], w=[a4[3][1]])
                        k.act(g1s.t[:], a4[0][0][:, :], AF.Sigmoid, r=[a4[0][1]], w=[g1s.g])
                        k.act(g2s.t[:], a4[1][0][:, :], AF.Sigmoid, r=[a4[1][1]], w=[g2s.g])
                        k.tt('dve', g1s.t[:], g1s.t[:], a4[2][0][:, :], ALU.mult, r=[a4[2][1]], w=[g1s.g])
                        k.tt('dve', g2s.t[:], g2s.t[:], a4[3][0][:, :], ALU.mult, r=[a4[3][1]], w=[g2s.g])
                        k.tt('pool', mT.t[:, fo, ts_], g1s.t[:], g2s.t[:], ALU.add, r=[g1s.g, g2s.g], w=[mTg[fo]])
                S.barrier()
            S.barrier()

        with ExitStack() as esB:
            sbB, psB = _mk(nc, esB)
            wob = sbB("wob", [128, 16, D], BF16)
            wog = [Reg() for _ in range(4)]
            wost = [sbB(f"wost{j}", [128, 4, D], F32) for j in range(2)]
            fnw = sbB("fnw_s", [128, D], F32)
            xts = [sbB(f"xt2{j}", [128, D], F32) for j in range(2)]
            hs = [sbB(f"hs{j}", [128, D], F32) for j in range(2)]
            acc = [psB(f"acc2{j}", [128, 512], F32) for j in range(4)]
            accg = [PReg() for _ in range(4)]
            k.dma(fnw.t[:], fnw_d, w=[fnw.g])
            for q4 in range(4):
                st = wost[q4 % 2]
                k.dma(st.t[:, :, :], wo[512 * q4:512 * (q4 + 1), :].rearrange("(k p) c -> p k c", p=128), w=[st.g])
                k.cp('dve' if q4 % 2 == 0 else 'pool', wob.t[:, 4 * q4:4 * q4 + 4, :], st.t[:, :, :], r=[st.g],
                     w=[wog[q4]])
            for n in range(8):
                xt, h_ = xts[n % 2], hs[n % 2]
                k.dma(xt.t[:], xo[128 * n:128 * (n + 1), :], w=[xt.g])
                for cg in range(4):
                    for fo in range(16):
                        k.mm(acc[cg][:, :], mT.t[:, fo, 128 * n:128 * (n + 1)], wob.t[:, fo, 512 * cg:512 * (cg + 1)],
                             fo == 0, fo == 15, r=[mTg[fo], wog[fo // 4]], w=[accg[cg]])
                    k.tt('dve', h_.t[:, 512 * cg:512 * (cg + 1)], acc[cg][:, :], xt.t[:, 512 * cg:512 * (cg + 1)],
                         ALU.add, r=[accg[cg], xt.g], w=[h_.g])
                k.act(junk.t[:], h_.t[:], AF.Square, r=[h_.g], w=[junk.g, ss.g], accum_out=ss.t[:])
                k.ts('dve', rs.t[:], ss.t[:], 1.0 / D, 1e-6, ALU.mult, ALU.add, r=[ss.g], w=[rs.g])
                k.act(rs.t[:], rs.t[:], AF.Sqrt, r=[], w=[rs.g])
                k.rcp(rs2.t[:], rs.t[:], r=[rs.g], w=[rs2.g])
                k.stt('dve', h_.t[:], h_.t[:], rs2.t[:, 0:1], fnw.t[:], ALU.mult, ALU.mult, r=[rs2.g, fnw.g],
                      w=[h_.g])
                k.dma(y[128 * n:128 * (n + 1), :], h_.t[:], r=[h_.g])
            S.barrier()


def _consts():
    c = {}
    c["ident"] = np.eye(128, dtype=np.float32).astype(NBF)
    s_ = np.arange(128)
    c["ublk"] = ((s_[:, None] // 64 == s_[None, :] // 64) & (s_[:, None] <= s_[None, :])).astype(np.float32)
    c["tri"] = (s_[:, None] <= s_[None, :]).astype(np.float32).astype(NBF)
    c["trilo"] = (s_[:, None] > s_[None, :]).astype(np.float32).astype(NBF)
    r = np.zeros((128, 32), np.float32)
    for m in range(16):
        r[m + 16, m] = -1.0
        r[m, m + 16] = 1.0
    c["rmat"] = r.astype(NBF)
    cc = np.arange(8192)
    c["bexp"] = (cc[None, :] // 64 == s_[:, None]).astype(np.float32).astype(NBF)
    ov = np.zeros((128, 4, 132), np.float32)
    ci = (np.arange(4)[None, :] * 128 + s_[:, None]) * 16
    sj = np.arange(128) * 64
    ovl = (ci[:, :, None] < sj[None, None, :] + 64) & (ci[:, :, None] + 32 > sj[None, None, :])
    ov[:, :, 0:128] = ovl
    ov[:, :, 128] = 1.0
    ov[127, 3, :] = 0.0
    c["ovx"] = ov.astype(NBF)
    v = np.arange(2304)[None, :] - 96
    c["mtw"] = ((16 * s_[:, None] + 31) <= v).astype(np.float32).astype(NBF)
    d_ = np.arange(256)[None, :] - 128
    jt = (s_[:, None] >= 64).astype(np.int64)
    c["m1w"] = (d_ <= jt - 2).astype(np.float32)
    m2 = np.zeros((128, 256), np.float32)
    m2[np.broadcast_to(d_ == jt, (128, 256))] = 2e30
    m2[np.broadcast_to(d_ == jt - 1, (128, 256))] = 1e30
    m2[np.broadcast_to(d_ > jt, (128, 256))] = -1.0
    c["m2w"] = m2
    p_ = np.arange(128)
    c["postab"] = (2048 * (p_[:, None] // 32) + np.arange(2048)[None, :]).astype(np.float32)
    inv = (np.float32(500000.0) ** (-np.arange(0, 32, 2, dtype=np.float32) / np.float32(32))).astype(np.float32)
    c["invf"] = inv[(p_ % 32) % 16].reshape(128, 1).astype(np.float32)
    return c


_NC_CACHE = {}
_DEBUG_MAPS = None
FUSED_MODE = False


def kernel(x, norm_w, w_in, hg_lb_logits, hg_norm_w, cmp_k_pos, cmp_k_w1, cmp_k_b1, cmp_k_w2,
           cmp_v_pos, cmp_v_w1, cmp_v_b1, cmp_v_w2, w_branch_hg, w_branch_nsa, w_out, final_norm_w):
    global HL
    f32 = lambda a: np.ascontiguousarray(np.asarray(a, dtype=np.float32))
    x2 = f32(x).reshape(S_LEN, D)
    w = f32(w_in)[0]
    nw = f32(norm_w)[0]
    normw = np.ascontiguousarray(nw.reshape(16, 128).T)
    cst = _consts()
    lbl = f32(hg_lb_logits)
    hnw = f32(hg_norm_w)[0]
    cmpw = {}
    for kv, (pos, w1, b1, w2) in (("k", (cmp_k_pos, cmp_k_w1, cmp_k_b1, cmp_k_w2)),
                                  ("v", (cmp_v_pos, cmp_v_w1, cmp_v_b1, cmp_v_w2))):
        cmpw[f"c{kv}_posT"] = np.ascontiguousarray(f32(pos)[0].T)
        cmpw[f"c{kv}_w1"] = f32(w1)[0]
        cmpw[f"c{kv}_b1"] = np.ascontiguousarray(f32(b1)[0].reshape(4, 128).T)
        cmpw[f"c{kv}_w2"] = f32(w2)[0]

    HL = 0
    in_maps = []
    for c in range(8):
        g = c // 2
        hl = (c % 2) * 2
        order = [hl, hl + 1] + [h for h in range(4) if h not in (hl, hl + 1)]
        cols = []
        cols += list(range(0 + 128 * c, 128 * c + 128))
        for h in order:
            cols += list(range(4096 + 128 * (4 * g + h), 4096 + 128 * (4 * g + h) + 128))
        cols += list(range(7168 + 128 * g, 7168 + 128 * g + 128))
        cols += list(range(8192 + 128 * g, 8192 + 128 * g + 128))
        cols += list(range(1024 + 128 * c, 1024 + 128 * c + 128))
        cols += list(range(2048 + 128 * c, 2048 + 128 * c + 128))
        cols += list(range(3072 + 128 * c, 3072 + 128 * c + 128))
        cols += list(range(7680 + 128 * g, 7680 + 128 * g + 128))
        cols += list(range(8704 + 128 * g, 8704 + 128 * g + 128))
        cols += list(range(9264 + 128 * (4 * g + hl), 9264 + 128 * (4 * g + hl) + 256))
        gc = 9216 + 12 * g + 3 * hl
        cols += list(range(gc, gc + 6)) + [gc] * 10
        wcore = np.ascontiguousarray(w[:, cols])
        ccols = list(range(6144 + 128 * g, 6144 + 128 * g + 128)) + list(range(6656 + 128 * g, 6656 + 128 * g + 128))
        m = dict(x=x2, wc=wcore, wcc=np.ascontiguousarray(w[:, ccols]), normw=normw,
                 lb0=np.ascontiguousarray(np.broadcast_to(lbl[0, 128 * c:128 * (c + 1)], (128, 128))),
                 lb1=np.ascontiguousarray(np.broadcast_to(lbl[1, 128 * c:128 * (c + 1)], (128, 128))),
                 hnw=np.ascontiguousarray(np.broadcast_to(hnw, (128, 128))))
        m.update(cmpw)
        m.update(cst)
        in_maps.append(m)
    wmg = np.ascontiguousarray(w[:, 11312:11312 + 4096])
    fnw = np.ascontiguousarray(np.broadcast_to(f32(final_norm_w), (128, D)))
    wbh_, wbn_, wo_ = f32(w_branch_hg)[0], f32(w_branch_nsa)[0], f32(w_out)[0]
    wm3 = wmg.reshape(16, 128, 4096)
    blk = lambda a, nk: a.reshape(nk, 128, 16, 128).transpose(2, 1, 0, 3)
    wp2_ = np.ascontiguousarray(np.concatenate([blk(wm3[:, :, :2048], 16), blk(wm3[:, :, 2048:], 16),
                                                blk(wbh_, 8), blk(wbn_, 16)], axis=2))
    if FUSED_MODE:
        for r in range(8):
            sel = np.zeros((128, 8, 128), np.float32)
            for a0 in range(4):
                for jh in range(2):
                    if r // 4 == jh:
                        j4 = r % 4
                        for f in range(32):
                            sel[j4 * 32 + f, 2 * a0 + jh, 32 * a0 + f] = 1.0
            in_maps[r].update(xo=np.ascontiguousarray(x2[1024 * r:1024 * (r + 1)]), wp2=wp2_,
                              wo=wo_, fnw=fnw, selm=sel.astype(NBF))
        if _DEBUG_MAPS is not None:
            _DEBUG_MAPS.append(in_maps)
            return None
        if "pf" not in _NC_CACHE:
            _NC_CACHE["pf"] = build_p1(FUSED=True)
        res = run_bass_kernel_spmd(_NC_CACHE["pf"], in_maps, core_ids=list(range(8)))
        out = np.concatenate([np.asarray(res.results[r]["y"]) for r in range(8)], axis=0)
        return out.reshape(1, S_LEN, D).astype(np.float32)
    if _DEBUG_MAPS is not None:
        _DEBUG_MAPS.append(in_maps)
        return None
    if "p1" not in _NC_CACHE:
        _NC_CACHE["p1"] = build_p1()
    nc1 = _NC_CACHE["p1"]
    res1 = run_bass_kernel_spmd(nc1, in_maps, core_ids=list(range(8)))
    oxs = [np.asarray(res1.results[c]["ox"]) for c in range(8)]

    full = np.concatenate([o[0:128] for o in oxs] + [o[128:384] for o in oxs], axis=0)
    if "p2" not in _NC_CACHE:
        _NC_CACHE["p2"] = build_p2()
    nc2 = _NC_CACHE["p2"]
    in2 = []
    for r in range(8):
        in2.append(dict(xo=np.ascontiguousarray(x2[1024 * r:1024 * (r + 1)]),
                        oo=np.ascontiguousarray(full[:, 1024 * r:1024 * (r + 1)]),
                        wp2=wp2_, wo=wo_,
                        normw=normw, fnw=fnw, ident=cst["ident"]))
    res2 = run_bass_kernel_spmd(nc2, in2, core_ids=list(range(8)))
    out = np.concatenate([np.asarray(res2.results[r]["y"]) for r in range(8)], axis=0)
    return out.reshape(1, S_LEN, D).astype(np.float32)
```
